# Optimizing a Trainium2 kernel written in Bass

```python
import jax, jax.numpy as jnp
from jax import lax
import numpy as np

D_MODEL = 1024
BATCH = 8
SEQ = 4096
DEPTH = 2

MLA_HEADS = 8
MLA_NOPE = 64
MLA_ROPE = 32
MLA_V = 64
MLA_Q_RANK = 256
MLA_KV_RANK = 128
NSA_HEADS = 8
NSA_KV_HEADS = 2
NSA_HPG = NSA_HEADS // NSA_KV_HEADS
NSA_HD = 64
NSA_ROT = NSA_HD // 4
L_CMP = 32
S_CMP = 16
CMP_HIDDEN = 128
L_SLC = 64
N_SEL = 16
WINDOW = 512
ROPE_THETA = 500000.0
Q_BLOCK = 128
D_FF = 2816
ALPHA = (2 * DEPTH) ** 0.25
BETA = (8 * DEPTH) ** -0.25
LN_EPS = 1e-5
RMS_EPS = 1e-6
NEG = -1e30
BIG = 1e30
SPLITS = (MLA_Q_RANK, MLA_KV_RANK, MLA_ROPE, NSA_HEADS * NSA_HD, 6 * NSA_KV_HEADS * NSA_HD, 3 * NSA_HEADS, D_MODEL, D_MODEL)
D_IN = sum(SPLITS)

kernel_name = 'hybrid_mla_nsa_macaron_deepnorm'


def layer_norm(x, g, b):
    xf = x.astype(jnp.float32)
    mu = jnp.mean(xf, -1, keepdims=True)
    var = jnp.mean(jnp.square(xf - mu), -1, keepdims=True)
    return ((xf - mu) * lax.rsqrt(var + LN_EPS) * g + b).astype(x.dtype)


def rms_norm(x, g):
    xf = x.astype(jnp.float32)
    return (xf * lax.rsqrt(jnp.mean(xf * xf, -1, keepdims=True) + RMS_EPS) * g).astype(x.dtype)


def swiglu(x, wg, wu, wd):
    return (jax.nn.silu(x @ wg) * (x @ wu)) @ wd


def rope_tables(rot_dim, T):
    inv = 1.0 / (ROPE_THETA ** (jnp.arange(0, rot_dim, 2, dtype=jnp.float32) / rot_dim))
    ang = jnp.arange(T, dtype=jnp.float32)[:, None] * inv[None, :]
    return jnp.cos(ang), jnp.sin(ang)


def apply_rope(x, cos, sin):
    x1, x2 = jnp.split(x, 2, axis=-1)
    c, s = cos[:, None, :], sin[:, None, :]
    return jnp.concatenate([x1 * c - x2 * s, x2 * c + x1 * s], -1).astype(x.dtype)


def partial_rope(x, cos, sin):
    return jnp.concatenate([apply_rope(x[..., :NSA_ROT], cos, sin), x[..., NSA_ROT:]], -1)


def to_blocks(a):
    B, T = a.shape[:2]
    return a.reshape((B, T // Q_BLOCK, Q_BLOCK) + a.shape[2:]).swapaxes(0, 1)


def from_blocks(a):
    nblk, B, Q = a.shape[:3]
    return a.swapaxes(0, 1).reshape((B, nblk * Q) + a.shape[3:])


def mla_attention(q_nope, q_pe, k_nope, k_pe, v):
    T = q_nope.shape[1]
    scale = (MLA_NOPE + MLA_ROPE) ** -0.5
    kpos = jnp.arange(T)

    def one_block(args):
        qs, qn, qp = args
        s = (jnp.einsum('bqhd,bkhd->bhqk', qn, k_nope, preferred_element_type=jnp.float32)
             + jnp.einsum('bqhd,bkd->bhqk', qp, k_pe, preferred_element_type=jnp.float32)) * scale
        qpos = qs + jnp.arange(Q_BLOCK)
        s = jnp.where(kpos[None, :] <= qpos[:, None], s, NEG)
        p = jax.nn.softmax(s, axis=-1).astype(v.dtype)
        return jnp.einsum('bhqk,bkhd->bqhd', p, v)

    starts = jnp.arange(T // Q_BLOCK) * Q_BLOCK
    return from_blocks(lax.map(one_block, (starts, to_blocks(q_nope), to_blocks(q_pe))))


def compress(kv, pe, w1, b1, w2):
    B, T, G, D = kv.shape
    n_cmp = (T - L_CMP) // S_CMP + 1
    idx = np.arange(n_cmp)[:, None] * S_CMP + np.arange(L_CMP)[None, :]
    blk = kv[:, idx] + pe[:, None, :]
    blk = blk.transpose(0, 1, 3, 2, 4).reshape(B, n_cmp, G, L_CMP * D)
    return jax.nn.gelu(blk @ w1 + b1) @ w2


def nsa_attention(q, k_c, v_c, k_s, v_s, k_w, v_w, gates,
                  pe_k, k_w1, k_b1, k_w2, pe_v, v_w1, v_b1, v_w2):
    B, T = q.shape[:2]
    G, HPG, D = NSA_KV_HEADS, NSA_HPG, NSA_HD
    scale = D ** -0.5
    qg = q.reshape(B, T, G, HPG, D)
    tpos = jnp.arange(T)
    n_cmp = (T - L_CMP) // S_CMP + 1
    n_slc = T // L_SLC
    n_sel = min(N_SEL, n_slc)

    kc = compress(k_c, pe_k, k_w1, k_b1, k_w2)
    vc = compress(v_c, pe_v, v_w1, v_b1, v_w2)
    cmp_start = np.arange(n_cmp) * S_CMP
    valid = jnp.asarray(cmp_start + L_CMP - 1)[None, :] <= tpos[:, None]
    s = jnp.einsum('btghd,bcgd->bghtc', qg, kc, preferred_element_type=jnp.float32) * scale
    p_cmp = jax.nn.softmax(jnp.where(valid, s, NEG), axis=-1) * valid
    o_c = jnp.einsum('bghtc,bcgd->btghd', p_cmp.astype(vc.dtype), vc)

    blk = np.arange(n_slc)
    overlap = ((cmp_start[:, None] < (blk[None, :] + 1) * L_SLC)
               & (cmp_start[:, None] + L_CMP > blk[None, :] * L_SLC)).astype(np.float32)
    imp = jnp.einsum('bghtc,cj->btgj', p_cmp, jnp.asarray(overlap))
    cur = (tpos // L_SLC)[:, None]
    jb = jnp.arange(n_slc)[None, :]
    forced = (jb == 0) | (jb == cur) | (jb == cur - 1)
    causal = jb <= cur
    imp = jnp.where(forced[:, None], BIG, jnp.where(causal[:, None], imp, NEG))
    _, sel = lax.top_k(imp, n_sel)

    k_sb = k_s.reshape(B, n_slc, L_SLC, G, D).transpose(0, 3, 1, 2, 4)
    v_sb = v_s.reshape(B, n_slc, L_SLC, G, D).transpose(0, 3, 1, 2, 4)
    k_wp = jnp.pad(k_w, ((0, 0), (WINDOW, 0), (0, 0), (0, 0)))
    v_wp = jnp.pad(v_w, ((0, 0), (WINDOW, 0), (0, 0), (0, 0)))
    span = WINDOW + Q_BLOCK
    bi = jnp.arange(B)[:, None, None, None]
    gi = jnp.arange(G)[None, None, :, None]
    n_keys = n_sel * L_SLC

    def one_block(args):
        qs, qb, sb = args
        qpos = qs + jnp.arange(Q_BLOCK)
        ks = k_sb[bi, gi, sb].reshape(B, Q_BLOCK, G, n_keys, D)
        vs = v_sb[bi, gi, sb].reshape(B, Q_BLOCK, G, n_keys, D)
        kpos = (sb[..., None] * L_SLC + jnp.arange(L_SLC)).reshape(B, Q_BLOCK, G, n_keys)
        m_s = (kpos <= qpos[None, :, None, None]).transpose(0, 2, 1, 3)[:, :, None]
        s_s = jnp.einsum('bqghd,bqgkd->bghqk', qb, ks, preferred_element_type=jnp.float32) * scale
        p_s = jax.nn.softmax(jnp.where(m_s, s_s, NEG), axis=-1).astype(vs.dtype)
        o_s = jnp.einsum('bghqk,bqgkd->bqghd', p_s, vs)
        kw = lax.dynamic_slice_in_dim(k_wp, qs, span, axis=1)
        vw = lax.dynamic_slice_in_dim(v_wp, qs, span, axis=1)
        kpos_w = qs - WINDOW + jnp.arange(span)
        m_w = ((kpos_w[None, :] <= qpos[:, None]) & (kpos_w[None, :] > qpos[:, None] - WINDOW)
               & (kpos_w[None, :] >= 0))
        s_w = jnp.einsum('bqghd,bkgd->bghqk', qb, kw, preferred_element_type=jnp.float32) * scale
        p_w = jax.nn.softmax(jnp.where(m_w, s_w, NEG), axis=-1).astype(vw.dtype)
        o_w = jnp.einsum('bghqk,bkgd->bqghd', p_w, vw)
        return o_s, o_w

    starts = jnp.arange(T // Q_BLOCK) * Q_BLOCK
    o_s, o_w = lax.map(one_block, (starts, to_blocks(qg), to_blocks(sel)))
    o_s, o_w = from_blocks(o_s), from_blocks(o_w)
    g = gates.reshape(B, T, G, HPG, 3)
    o = g[..., 0:1] * o_c + g[..., 1:2] * o_s + g[..., 2:3] * o_w
    return o.reshape(B, T, NSA_HEADS * D)


def token_mixer(x, w_in, q_norm_g, w_uq, kv_norm_g, w_ukv,
                cmp_pe_k, cmp_k_w1, cmp_k_b1, cmp_k_w2, cmp_pe_v, cmp_v_w1, cmp_v_b1, cmp_v_w2,
                w_proj_mla, w_proj_nsa, w_out, cos_m, sin_m, cos_n, sin_n):
    B, T, _ = x.shape
    offsets = np.cumsum(SPLITS)[:-1].tolist()
    c_q, c_kv, k_pe, q_n, kv_n, g_n, gate_m, gate_n = jnp.split(x @ w_in, offsets, axis=-1)

    q = (rms_norm(c_q, q_norm_g) @ w_uq).reshape(B, T, MLA_HEADS, MLA_NOPE + MLA_ROPE)
    q_nope, q_pe = q[..., :MLA_NOPE], apply_rope(q[..., MLA_NOPE:], cos_m, sin_m)
    kv = (rms_norm(c_kv, kv_norm_g) @ w_ukv).reshape(B, T, MLA_HEADS, MLA_NOPE + MLA_V)
    k_nope, v_m = kv[..., :MLA_NOPE], kv[..., MLA_NOPE:]
    k_pe = apply_rope(k_pe[:, :, None, :], cos_m, sin_m)[:, :, 0]
    o_m = mla_attention(q_nope, q_pe, k_nope, k_pe, v_m).reshape(B, T, MLA_HEADS * MLA_V)

    qn = partial_rope(q_n.reshape(B, T, NSA_HEADS, NSA_HD), cos_n, sin_n)
    kvs = kv_n.reshape(B, T, 6, NSA_KV_HEADS, NSA_HD)
    k_c = partial_rope(kvs[:, :, 0], cos_n, sin_n)
    k_s = partial_rope(kvs[:, :, 2], cos_n, sin_n)
    k_w = partial_rope(kvs[:, :, 4], cos_n, sin_n)
    gates = jax.nn.sigmoid(g_n.reshape(B, T, NSA_HEADS, 3))
    o_n = nsa_attention(qn, k_c, kvs[:, :, 1], k_s, kvs[:, :, 3], k_w, kvs[:, :, 5], gates,
                        cmp_pe_k, cmp_k_w1, cmp_k_b1, cmp_k_w2, cmp_pe_v, cmp_v_w1, cmp_v_b1, cmp_v_w2)

    y = jax.nn.sigmoid(gate_m) * (o_m @ w_proj_mla) + jax.nn.sigmoid(gate_n) * (o_n @ w_proj_nsa)
    return y @ w_out


def setup_inputs(seed: int = 0) -> dict:
    key = jax.random.key(seed)
    ks = iter(jax.random.split(key, 64))

    def w(shape, fan_in, scale=1.0):
        return jax.random.normal(next(ks), (DEPTH,) + shape, jnp.float32) * (scale * fan_in ** -0.5)

    def gain(n):
        return 1.0 + 0.02 * jax.random.normal(next(ks), (DEPTH, n), jnp.float32)

    def small(shape, s=0.02):
        return s * jax.random.normal(next(ks), (DEPTH,) + shape, jnp.float32)

    return {
        'x': jax.random.normal(next(ks), (BATCH, SEQ, D_MODEL), jnp.float32),
        'ln_f1_g': gain(D_MODEL), 'ln_f1_b': small((D_MODEL,)),
        'ffn1_wg': w((D_MODEL, D_FF), D_MODEL), 'ffn1_wu': w((D_MODEL, D_FF), D_MODEL),
        'ffn1_wd': w((D_FF, D_MODEL), D_FF, BETA),
        'w_in': w((D_MODEL, D_IN), D_MODEL),
        'q_norm_g': gain(MLA_Q_RANK), 'w_uq': w((MLA_Q_RANK, MLA_HEADS * (MLA_NOPE + MLA_ROPE)), MLA_Q_RANK),
        'kv_norm_g': gain(MLA_KV_RANK), 'w_ukv': w((MLA_KV_RANK, MLA_HEADS * (MLA_NOPE + MLA_V)), MLA_KV_RANK),
        'cmp_pe_k': small((L_CMP, NSA_HD), 0.1), 'cmp_k_w1': w((L_CMP * NSA_HD, CMP_HIDDEN), L_CMP * NSA_HD),
        'cmp_k_b1': small((CMP_HIDDEN,)), 'cmp_k_w2': w((CMP_HIDDEN, NSA_HD), CMP_HIDDEN),
        'cmp_pe_v': small((L_CMP, NSA_HD), 0.1), 'cmp_v_w1': w((L_CMP * NSA_HD, CMP_HIDDEN), L_CMP * NSA_HD),
        'cmp_v_b1': small((CMP_HIDDEN,)), 'cmp_v_w2': w((CMP_HIDDEN, NSA_HD), CMP_HIDDEN),
        'w_proj_mla': w((MLA_HEADS * MLA_V, D_MODEL), MLA_HEADS * MLA_V),
        'w_proj_nsa': w((NSA_HEADS * NSA_HD, D_MODEL), NSA_HEADS * NSA_HD),
        'w_out': w((D_MODEL, D_MODEL), D_MODEL, BETA),
        'ln_mix_g': gain(D_MODEL), 'ln_mix_b': small((D_MODEL,)),
        'ffn2_wg': w((D_MODEL, D_FF), D_MODEL), 'ffn2_wu': w((D_MODEL, D_FF), D_MODEL),
        'ffn2_wd': w((D_FF, D_MODEL), D_FF, BETA),
        'ln_f2_g': gain(D_MODEL), 'ln_f2_b': small((D_MODEL,)),
    }


def reference(x, ln_f1_g, ln_f1_b, ffn1_wg, ffn1_wu, ffn1_wd, w_in, q_norm_g, w_uq, kv_norm_g, w_ukv,
              cmp_pe_k, cmp_k_w1, cmp_k_b1, cmp_k_w2, cmp_pe_v, cmp_v_w1, cmp_v_b1, cmp_v_w2,
              w_proj_mla, w_proj_nsa, w_out, ln_mix_g, ln_mix_b,
              ffn2_wg, ffn2_wu, ffn2_wd, ln_f2_g, ln_f2_b):
    T = x.shape[1]
    cos_m, sin_m = rope_tables(MLA_ROPE, T)
    cos_n, sin_n = rope_tables(NSA_ROT, T)
    for l in range(DEPTH):
        x = layer_norm(ALPHA * x + 0.5 * swiglu(x, ffn1_wg[l], ffn1_wu[l], ffn1_wd[l]), ln_f1_g[l], ln_f1_b[l])
        mix = token_mixer(x, w_in[l], q_norm_g[l], w_uq[l], kv_norm_g[l], w_ukv[l],
                          cmp_pe_k[l], cmp_k_w1[l], cmp_k_b1[l], cmp_k_w2[l],
                          cmp_pe_v[l], cmp_v_w1[l], cmp_v_b1[l], cmp_v_w2[l],
                          w_proj_mla[l], w_proj_nsa[l], w_out[l], cos_m, sin_m, cos_n, sin_n)
        x = layer_norm(ALPHA * x + mix, ln_mix_g[l], ln_mix_b[l])
        x = layer_norm(ALPHA * x + 0.5 * swiglu(x, ffn2_wg[l], ffn2_wu[l], ffn2_wd[l]), ln_f2_g[l], ln_f2_b[l])
    return x
```

```python
from contextlib import ExitStack
import numpy as np
import concourse.bass as bass
import concourse.mybir as mybir
from concourse.bass_utils import run_bass_kernel_spmd

F32 = mybir.dt.float32
BF16 = mybir.dt.bfloat16
AF = mybir.ActivationFunctionType
ALU = mybir.AluOpType
AX = mybir.AxisListType

D = 1024
DFF = 2816
NFC = DFF // 128
DEPTH = 2
ALPHA = (2 * DEPTH) ** 0.25
LN_EPS = 1e-5
RMS_EPS = 1e-6
NEGB = -30000.0
SC_MLA = 96 ** -0.5
SC_NSA = 0.125
THETA = 500000.0
SLOT = 4096
NSLOT = 2
PF = 1

ENGS = ("pe", "act", "dve", "pool", "sp")


class Buf:
    __slots__ = ("name", "w", "r", "dsem", "dcnt", "x")

    def __init__(self, name, x=False):
        self.name = name
        self.x = x
        self.w = None
        self.r = {}
        self.dsem = None
        self.dcnt = 0


class Prog:
    def __init__(self, nc, dry=False):
        self.nc = nc
        self.dry = dry
        self.ops = {e: [] for e in ENGS}
        self.dma_cnt = {}
        self.dma_keys = []
        self.dma_sem = {}

    def _deps(self, eng, reads, writes):
        deps = set()
        for b in reads:
            if b.w is not None:
                deps.add(b.w)
            if b.x:
                for k_, h_ in b.r.items():
                    if k_ != eng:
                        deps.add(h_)
        for b in writes:
            if b.w is not None:
                deps.add(b.w)
            for h in b.r.values():
                deps.add(h)
        if eng == "pe":
            deps = {h for h in deps if not (h[0] == "e" and h[1] == "pe")}
        return deps

    def add(self, eng, fn, reads=(), writes=()):
        if self.dry:
            return None
        deps = self._deps(eng, reads, writes)
        idx = len(self.ops[eng])
        h = ("e", eng, idx)
        self.ops[eng].append([fn, deps, h, False])
        for b in writes:
            b.w = h
            b.r = {}
        for b in reads:
            b.r[eng] = h
        return h

    def dma(self, eng, fn, sem_buf, reads=(), writes=()):
        if self.dry:
            return None
        deps = self._deps(eng, reads, writes)
        kind = "sw" if eng == "pool" else "hw"
        key = (sem_buf, kind)
        if key not in self.dma_cnt:
            self.dma_cnt[key] = 0
            self.dma_keys.append(key)
        self.dma_cnt[key] += 1
        h = ("d", key, self.dma_cnt[key])
        self.ops[eng].append([fn, deps, h, False])
        for b in writes:
            b.w = h
            b.r = {}
        for b in reads:
            b.r[("d", id(sem_buf), kind)] = h
        return h

    def emit(self, final_waits=()):
        nc = self.nc
        for e in ENGS:
            for op in self.ops[e]:
                for h in op[1]:
                    if h[0] == "e":
                        self.ops[h[1]][h[2]][3] = True
        cum = {}
        for e in ENGS:
            c = 0
            arr = []
            for op in self.ops[e]:
                if op[3]:
                    c += 1
                arr.append(c)
            cum[e] = arr
        with ExitStack() as es:
            esem = {e: es.enter_context(nc.semaphore("s_" + e)) for e in ENGS}
            for i, key in enumerate(self.dma_keys):
                self.dma_sem[key] = es.enter_context(nc.semaphore("d%d" % i))
            block = es.enter_context(nc.Block())

            def run(e, engobj, extra=()):
                waited_e = {}
                waited_d = {}

                def do_wait(h):
                    if h[0] == "e":
                        v = cum[h[1]][h[2]]
                        if waited_e.get(h[1], 0) >= v:
                            return
                        waited_e[h[1]] = v
                        engobj.wait_ge(esem[h[1]], v)
                    else:
                        key = h[1]
                        v = 16 * h[2]
                        if waited_d.get(key, 0) >= v:
                            return
                        waited_d[key] = v
                        engobj.wait_ge(self.dma_sem[key], v)

                for fn, deps, h, need in self.ops[e]:
                    for d in deps:
                        do_wait(d)
                    ins = fn(engobj)
                    if h[0] == "d":
                        ins.then_inc(self.dma_sem[h[1]], 16)
                    elif need:
                        ins.then_inc(esem[e], 1)
                for hh in extra:
                    do_wait(hh)

            @block.tensor
            def _(eng):
                run("pe", eng)

            @block.scalar
            def _(eng):
                run("act", eng)

            @block.vector
            def _(eng):
                run("dve", eng)

            @block.gpsimd
            def _(eng):
                run("pool", eng)

            @block.sync
            def _(eng):
                run("sp", eng, extra=final_waits)


def make_consts(T):
    c = {}
    c["c_identf"] = np.eye(128, dtype=np.float32)
    bfc = np.zeros((128, 5 * 128 + 11 * 128), np.float32)
    bfc[:, 0:128] = np.eye(128)
    bfc[:, 128:256] = 1.0
    bfc[:, 256:384] = 1.0 / 1024
    bfc[:, 384:512] = 1.0 / 256
    bfc[:, 512:640] = 1.0 / 128
    p = np.arange(128)[:, None]
    f = np.arange(128)[None, :]
    for b in range(11):
        r = 3 - b
        if r > 0 or r < -4:
            blk = np.full((128, 128), NEGB, np.float32)
        elif r == 0:
            blk = np.where(p <= f, 0.0, NEGB).astype(np.float32)
        elif r == -4:
            blk = np.where(p > f, 0.0, NEGB).astype(np.float32)
        else:
            blk = np.zeros((128, 128), np.float32)
        bfc[:, 640 + b * 128: 640 + (b + 1) * 128] = blk
    c["c_bfc"] = bfc
    E = (np.arange(64)[:, None] == (np.arange(T)[None, :] // 64)).astype(np.float32)
    c["c_e"] = E
    t = np.arange(T, dtype=np.float32)
    inv_m = (1.0 / (THETA ** (np.arange(0, 32, 2, dtype=np.float32) / 32))).astype(np.float32)
    ang_m = t[None, :] * inv_m[:, None]
    cm = np.cos(ang_m).astype(np.float32)
    sm = np.sin(ang_m).astype(np.float32)
    Cm = np.concatenate([cm, cm], 0)
    Sm = np.concatenate([-sm, sm], 0)
    inv_n = (1.0 / (THETA ** (np.arange(0, 16, 2, dtype=np.float32) / 16))).astype(np.float32)
    ang_n = t[None, :] * inv_n[:, None]
    cn = np.cos(ang_n).astype(np.float32)
    sn = np.sin(ang_n).astype(np.float32)
    Cn = np.concatenate([cn, cn, np.ones((48, T), np.float32)], 0)
    Sn = np.concatenate([-sn, sn, np.zeros((48, T), np.float32)], 0)
    rope = np.zeros((4, 128, T), np.float32)
    rope[0, 0:96] = np.tile(Cm, (3, 1))
    rope[1, 0:96] = np.tile(Sm, (3, 1))
    rope[2] = np.tile(Cn, (2, 1))
    rope[3] = np.tile(Sn, (2, 1))
    c["c_rope"] = rope
    m = np.floor((np.arange(128) - 15) / 16.0)[:, None]
    x = np.arange(512)[None, :] - 248
    c["c_stripc"] = np.where(x <= m, 0.0, NEGB).astype(np.float32)
    hp = (np.arange(128) >= 64).astype(np.int64)[:, None]
    rel = np.arange(128)[None, :] - 62
    sj = np.zeros((128, 128), np.float32)
    sj = np.where(rel == hp, 2e30, sj)
    sj = np.where(rel == hp - 1, 4e30, sj)
    sj = np.where(rel > hp, -1e30, sj)
    c["c_stripj"] = sj.astype(np.float32)
    sel = np.zeros((32, 12, 128), np.float32)
    for br in range(3):
        for i in range(4):
            sel[br * 8 + i, br * 4 + i, 0:64] = 1.0
            sel[br * 8 + 4 + i, br * 4 + i, 64:128] = 1.0
    c["c_selg"] = sel.reshape(32, 12 * 128)
    return c


def prep_weights(inp):
    L = DEPTH
    w = {}
    g = lambda k: np.asarray(inp[k], dtype=np.float32)
    w["ffn_wg"] = np.ascontiguousarray(np.stack([g("ffn1_wg"), g("ffn2_wg")], 1))
    w["ffn_wu"] = np.ascontiguousarray(np.stack([g("ffn1_wu"), g("ffn2_wu")], 1))
    w["ffn_wd"] = np.ascontiguousarray(np.stack([g("ffn1_wd"), g("ffn2_wd")], 1))
    win = g("w_in")
    o_cq, o_ckv, o_kpe, o_qn, o_kvn, o_gn, o_gm, o_gnn = 0, 256, 384, 416, 928, 1696, 1720, 2744
    sw32 = np.concatenate([np.arange(16, 32), np.arange(0, 16)])
    kpe = win[:, :, o_kpe:o_kpe + 32]
    kpe_sw = kpe[:, :, sw32]
    w["w_a"] = np.ascontiguousarray(np.concatenate(
        [win[:, :, o_cq:o_cq + 384], np.tile(kpe, (1, 1, 3)), np.tile(kpe_sw, (1, 1, 3))], 2))
    sw64 = np.concatenate([np.arange(8, 16), np.arange(0, 8), np.arange(16, 64)])
    qn = win[:, :, o_qn:o_qn + 512].reshape(L, D, 8, 64)
    qn_sw = qn[:, :, :, sw64]
    pair = [0, 4, 1, 5, 2, 6, 3, 7]
    w["w_q"] = np.ascontiguousarray(np.concatenate(
        [qn[:, :, pair].reshape(L, D, 512), qn_sw[:, :, pair].reshape(L, D, 512)], 2))
    kvn = win[:, :, o_kvn:o_kvn + 768].reshape(L, D, 6, 2, 64)
    kvn_sw = kvn[:, :, :, :, sw64]
    f2 = lambda a: a.reshape(L, D, 128)
    w["w_k"] = np.ascontiguousarray(np.concatenate(
        [f2(kvn[:, :, 0]), f2(kvn[:, :, 2]), f2(kvn[:, :, 4]), f2(kvn[:, :, 1]),
         f2(kvn_sw[:, :, 0]), f2(kvn_sw[:, :, 2]), f2(kvn_sw[:, :, 4])], 2))
    gn = win[:, :, o_gn:o_gn + 24].reshape(L, D, 8, 3).transpose(0, 1, 3, 2).reshape(L, D, 24)
    w["w_v"] = np.ascontiguousarray(np.concatenate(
        [f2(kvn[:, :, 3]), f2(kvn[:, :, 5]), gn, np.zeros((L, D, 8), np.float32)], 2))
    w["w_gm"] = np.ascontiguousarray(win[:, :, o_gm:o_gm + 1024])
    w["w_gn"] = np.ascontiguousarray(win[:, :, o_gnn:o_gnn + 1024])
    wuq = g("w_uq").reshape(L, 256, 8, 96)
    nope = wuq[:, :, :, 0:64].reshape(L, 256, 512)
    rp = wuq[:, :, :, 64:96]
    rp_sw = rp[:, :, :, sw32]
    hsel = [0, 1, 2, 3, 4, 5, 6, 7, 7]
    w["w_uq"] = np.ascontiguousarray(np.concatenate(
        [nope, rp[:, :, hsel].reshape(L, 256, 288), rp_sw[:, :, hsel].reshape(L, 256, 288)], 2))
    wukv = g("w_ukv").reshape(L, 128, 8, 128)
    wuk = wukv[:, :, :, 0:64]
    w["w_ukt"] = np.ascontiguousarray(
        wuk.reshape(L, 128, 4, 2, 64).transpose(0, 3, 4, 2, 1).reshape(L, 128, 4 * 128))
    w["w_uv"] = np.ascontiguousarray(wukv[:, :, :, 64:128].reshape(L, 128, 512))
    for nm, k1 in (("w_c1k", "cmp_k_w1"), ("w_c1v", "cmp_v_w1")):
        a = g(k1).reshape(L, 32, 64, 128).transpose(0, 2, 1, 3)
        w[nm] = np.ascontiguousarray(np.concatenate([a, a], 1).reshape(L, 128, 32 * 128))
    pek = g("cmp_pe_k").transpose(0, 2, 1)
    pev = g("cmp_pe_v").transpose(0, 2, 1)
    w["w_cpe"] = np.ascontiguousarray(np.concatenate(
        [np.concatenate([pek, pek], 1), np.concatenate([pev, pev], 1)], 2))
    w["w_cb"] = np.ascontiguousarray(np.stack([g("cmp_k_b1"), g("cmp_v_b1")], 2))
    w["w_c2"] = np.ascontiguousarray(np.concatenate([g("cmp_k_w2"), g("cmp_v_w2")], 2))
    w["w_pm"] = g("w_proj_mla")
    wpn = g("w_proj_nsa").reshape(L, 8, 64, 1024)
    w["w_pn"] = np.ascontiguousarray(wpn[:, pair].reshape(L, 512, 1024))
    w["w_out"] = g("w_out")
    lnp = np.stack([g("ln_f1_g"), g("ln_f1_b"), g("ln_mix_g"), g("ln_mix_b"), g("ln_f2_g"), g("ln_f2_b")], 1)
    w["lnp"] = np.ascontiguousarray(lnp.reshape(L, 6, 8, 128).transpose(0, 3, 1, 2).reshape(L, 128, 48))
    w["qng"] = np.ascontiguousarray(g("q_norm_g").reshape(L, 2, 128).transpose(0, 2, 1))
    w["kvng"] = np.ascontiguousarray(g("kv_norm_g").reshape(L, 1, 128).transpose(0, 2, 1))
    return w


class Builder:
    def __init__(self, T, nl=DEPTH, stop=0):
        self.stop = stop
        self.T = T
        self.NL = nl
        self.NCH = T // 512
        self.NT = T // 128
        self.NS = T // 64
        self.NCC = T // 16

    def mm(self, out, lhsT, rhs, start, stop, reads, wb):
        self.P.add("pe", lambda e: e.matmul(out, lhsT, rhs, start=start, stop=stop), reads=reads, writes=[wb])

    def tr(self, out, in_, ident, reads, wb):
        self.P.add("pe", lambda e: e.transpose(out, in_, ident), reads=reads, writes=[wb])

    def act(self, out, in_, func, reads, writes, bias=0.0, scale=1.0, accum_out=None):
        if accum_out is None:
            self.P.add("act", lambda e: e.activation(out=out, in_=in_, func=func, bias=bias, scale=scale),
                       reads=reads, writes=writes)
        else:
            self.P.add("act", lambda e: e.activation(out=out, in_=in_, func=func, bias=bias, scale=scale,
                                                     accum_out=accum_out), reads=reads, writes=writes)

    def tt(self, out, in0, in1, op, reads, writes, eng="dve"):
        self.P.add(eng, lambda e: e.tensor_tensor(out=out, in0=in0, in1=in1, op=op), reads=reads, writes=writes)

    def ts(self, out, in0, s1, op0, reads, writes, s2=None, op1=None, eng="dve"):
        if op1 is None:
            self.P.add(eng, lambda e: e.tensor_scalar(out=out, in0=in0, scalar1=s1, scalar2=None, op0=op0),
                       reads=reads, writes=writes)
        else:
            self.P.add(eng, lambda e: e.tensor_scalar(out=out, in0=in0, scalar1=s1, scalar2=s2, op0=op0, op1=op1),
                       reads=reads, writes=writes)

    def stt(self, out, in0, scalar, in1, op0, op1, reads, writes, eng="dve"):
        self.P.add(eng, lambda e: e.scalar_tensor_tensor(out=out, in0=in0, scalar=scalar, in1=in1, op0=op0, op1=op1),
                   reads=reads, writes=writes)

    def cp(self, out, in_, reads, writes, eng="dve"):
        if eng == "act":
            self.P.add("act", lambda e: e.activation(out=out, in_=in_, func=AF.Copy), reads=reads, writes=writes)
        else:
            self.P.add(eng, lambda e: e.tensor_copy(out=out, in_=in_), reads=reads, writes=writes)

    def memset(self, ap, val, writes, eng="dve"):
        self.P.add(eng, lambda e: e.memset(ap, val), writes=writes)

    def recip(self, out, in_, reads, writes):
        self.P.add("dve", lambda e: e.reciprocal(out=out, in_=in_), reads=reads, writes=writes)

    def bank(self, pool):
        lst = self.pools[pool]
        i = self.pool_i[pool]
        self.pool_i[pool] = (i + 1) % len(lst)
        return lst[i]

    def wtile(self, parts):
        if self.P.dry:
            self.wlist.append(parts)
            return None, None
        i = self.wi
        self.wi += 1
        while self.wissued < min(i + 1 + PF, len(self.wlist)):
            j = self.wissued
            st, sb = self.slots[j % NSLOT]
            for (off, a, b, src) in self.wlist[j]:
                nrow = src.shape[0]
                dst = st[0:nrow, off:off + a * b].rearrange("p (a b) -> p a b", a=a)
                self.P.dma("pool", (lambda d, s: (lambda e: e.dma_start(out=d, in_=s)))(dst, src), sb, writes=[sb])
            self.wissued += 1
        return self.slots[i % NSLOT]

    def wview(self, st, off, a, b, rows=128):
        return st[0:rows, off:off + a * b].rearrange("p (a b) -> p a b", a=a)

    def build(self):
        nc = bass.Bass("TRN2", target_bir_lowering=False)
        self.nc = nc
        T, NL = self.T, self.NL
        dt = nc.dram_tensor
        dr = {}
        dr["x"] = dt("x", [T, D], F32, kind="ExternalInput").ap()
        dr["y"] = dt("y", [T, D], F32, kind="ExternalOutput").ap()
        dr["scr"] = dt("scr", [T, D], F32, kind="Internal").ap()
        shapes = {
            "ffn_wg": [2, 2, D, DFF], "ffn_wu": [2, 2, D, DFF], "ffn_wd": [2, 2, DFF, D],
            "w_a": [2, D, 576], "w_q": [2, D, 1024], "w_k": [2, D, 896], "w_v": [2, D, 288],
            "w_gm": [2, D, 1024], "w_gn": [2, D, 1024], "w_uq": [2, 256, 1088], "w_ukt": [2, 128, 512],
            "w_uv": [2, 128, 512], "w_c1k": [2, 128, 4096], "w_c1v": [2, 128, 4096], "w_cpe": [2, 128, 64],
            "w_cb": [2, 128, 2], "w_c2": [2, 128, 128], "w_pm": [2, 512, 1024], "w_pn": [2, 512, 1024],
            "w_out": [2, D, 1024], "lnp": [2, 128, 48], "qng": [2, 128, 2], "kvng": [2, 128, 1],
            "c_identf": [128, 128], "c_bfc": [128, 2048], "c_e": [64, T], "c_rope": [4, 128, T],
            "c_stripc": [128, 512], "c_stripj": [128, 128], "c_selg": [32, 1536],
        }
        for k, s in shapes.items():
            dr[k] = dt(k, s, F32, kind="ExternalInput").ap()
        self.dr = dr
        self.in_names = ["x"] + list(shapes.keys())

        with ExitStack() as es:
            def sb(name, shape, dtype):
                return es.enter_context(nc.sbuf_tensor("s_" + name, shape, dtype))

            self.sb = sb
            banks = []
            for i in range(8):
                t_ = es.enter_context(nc.psum_tensor("ps%d" % i, [128, 512], F32))
                banks.append((t_, Buf("ps%d" % i, x=True)))
            self.pools = {"S": banks[0:3], "A": banks[3:5], "G": banks[5:8]}
            self.pool_i = {"S": 0, "A": 0, "G": 0}
            self.slots = [(sb("wslot%d" % i, [128, SLOT], BF16), Buf("wslot%d" % i)) for i in range(NSLOT)]
            self.alloc_persistent()
            self.wlist = []
            self.P = Prog(nc, dry=True)
            self.program()
            self.P = Prog(nc, dry=False)
            self.wi = 0
            self.wissued = 0
            self.pool_i = {"S": 0, "A": 0, "G": 0}
            self.program()
            finals = [("d", (self.B_out, "hw"), self.P.dma_cnt[(self.B_out, "hw")])]
            self.P.emit(final_waits=finals)
        return nc

    def alloc_persistent(self):
        sb, T = self.sb, self.T
        NT, NCC = self.NT, self.NCC
        self.identf = sb("identf", [128, 128], F32)
        self.bfc = sb("bfc", [128, 2048], BF16)
        self.stripc = sb("stripc", [128, 512], F32)
        self.stripj = sb("stripj", [128, 128], F32)
        self.selg = sb("selg", [32, 1536], F32)
        self.B_const = Buf("const")
        self.lnp = sb("lnp", [128, 48], F32)
        self.qng = sb("qng", [128, 2], F32)
        self.kvng = sb("kvng", [128, 1], F32)
        self.cpe = sb("cpe", [128, 64], BF16)
        self.cb = sb("cb", [128, 2], F32)
        self.c2 = sb("c2", [128, 128], BF16)
        self.ukt = sb("ukt", [128, 512], BF16)
        self.uv = sb("uv", [128, 512], BF16)
        self.pb = sb("pb", [128, 2], F32)
        self.B_lp = Buf("layerparams")
        self.B_pb = Buf("pb")
        self.ckvT = sb("ckvT", [128, T], BF16)
        self.ckvtok = sb("ckvtok", [128, NT, 128], BF16)
        self.kpeT = sb("kpeT", [96, T], BF16)
        self.ksaug = sb("ksaug", [128, 2, T], BF16)
        self.kwT = sb("kwT", [128, T], BF16)
        self.Vs = sb("Vs", [128, NT, 192], BF16)
        self.Vw = sb("Vw", [128, NT, 192], BF16)
        self.kcT = sb("kcT", [128, NCC], BF16)
        self.vc = sb("vc", [128, max(1, NCC // 128), 2, 64], BF16)
        self.kch = sb("kch", [128, 528], BF16)
        self.vch = sb("vch", [128, 528], BF16)
        self.B_cache = [Buf("cache%d" % c) for c in range(self.NCH)]
        self.B_hist = Buf("hist")
        self.B_vc = Buf("vc")
        self.B_kc = Buf("kc")
        self.R1 = sb("R1", [128, 8, 512], F32)
        self.R2 = sb("R2", [128, 8, 512], F32)
        self.R3 = sb("R3", [128, 8, 512], BF16)
        self.R4 = sb("R4", [128, 22 * 512], BF16)
        self.B_R1, self.B_R2, self.B_R3, self.B_R4 = Buf("R1"), Buf("R2"), Buf("R3"), Buf("R4")
        self.B_xtok = Buf("xtok")
        self.B_oc, self.B_osb, self.B_owb, self.B_acc = Buf("oc"), Buf("osb"), Buf("owb"), Buf("acc")
        self.fence_t = sb("fence_t", [128, 2], F32)
        self.pt = [(sb("pt%d" % i, [128, 512], BF16), Buf("pt%d" % i)) for i in range(3)]
        self.pt_i = 0
        self.rope = sb("rope", [128, 4, 512], F32)
        self.B_rope = Buf("rope")
        self.tmpf = [(sb("tmpf%d" % i, [128, 512], F32), Buf("tmpf%d" % i)) for i in range(4)]
        self.tmpf_i = 0
        self.tmpb = [(sb("tmpb%d" % i, [128, 512], BF16), Buf("tmpb%d" % i)) for i in range(3)]
        self.tmpb_i = 0
        r2b = self.R2[:].rearrange("p a f -> p (a f)").bitcast(BF16)
        self.cqn = r2b[:, 0:1024].rearrange("p (a t) -> p a t", a=2)
        self.qnope = r2b[:, 1024:3072].rearrange("p (a t) -> p a t", a=4)
        self.B_cqn = self.B_R2
        self.B_qnope = self.B_R2
        self.qpe = sb("qpe", [96, 3, 512], BF16)
        self.B_qpe = Buf("qpe")
        self.gsig = sb("gsig", [32, 512], F32)
        self.B_gsig = Buf("gsig")
        self.omT = sb("omT", [128, 4, 512], BF16)
        self.onT = sb("onT", [128, 4, 512], BF16)
        self.B_omT, self.B_onT = Buf("omT"), Buf("onT")
        W4 = NCC + 4
        self.sbias = [(self.R2[:, 7, i * 256:(i + 1) * 256], Buf("sbias%d" % i)) for i in range(2)]
        self.pexp = [(self.R2[:, 4, 0:256], self.B_osb), (self.R2[:, 5, 0:256], self.B_owb)]
        self.pnb = [(sb("pnb%d" % i, [128, max(128, NCC)], BF16), Buf("pnb%d" % i)) for i in range(2)]
        self.pTt = [(sb("pTt%d" % i, [128, max(1, NCC // 128), 128], BF16), Buf("pTt%d" % i)) for i in range(2)]
        self.p4 = self.R2[:, 6, 0:W4]
        self.B_p4 = self.B_acc
        self.small = [(sb("small%d" % i, [128, 4], F32), Buf("small%d" % i)) for i in range(4)]
        self.small_i = 0
        self.cmp_i = 0
        self.imp = sb("imp", [128, 64], F32)
        self.imp2 = sb("imp2", [128, 64], F32)
        self.m8 = sb("m8", [128, 16], F32)
        self.nsb = sb("nsb", [128, 64], BF16)
        self.B_imp = Buf("imp")
        self.hidt = sb("hidt", [128, 4, 32], F32)
        self.hidb = sb("hidb", [128, 32], BF16)
        self.hidpad = sb("hidpad", [128, 128], BF16)
        self.B_hid = Buf("hid")
        self.B_out = Buf("out")
        self.B_scr = [Buf("scr%d" % c) for c in range(self.NCH)]

    def tf(self):
        r = self.tmpf[self.tmpf_i]
        self.tmpf_i = (self.tmpf_i + 1) % len(self.tmpf)
        return r

    def tb(self):
        r = self.tmpb[self.tmpb_i]
        self.tmpb_i = (self.tmpb_i + 1) % len(self.tmpb)
        return r

    def ptile(self):
        r = self.pt[self.pt_i]
        self.pt_i = (self.pt_i + 1) % len(self.pt)
        return r

    def sm(self):
        r = self.small[self.small_i]
        self.small_i = (self.small_i + 1) % len(self.small)
        return r

    def program(self):
        P, dr = self.P, self.dr
        self.tmpf_i = self.tmpb_i = self.pt_i = self.small_i = self.cmp_i = 0
        Bc = self.B_const
        P.dma("sp", lambda e: e.dma_start(out=self.identf[:], in_=dr["c_identf"]), Bc, writes=[Bc])
        P.dma("pool", lambda e: e.dma_start(out=self.bfc[:], in_=dr["c_bfc"]), Bc, writes=[Bc])
        import os as _os
        KSKIP = _os.environ.get("KSKIP", "")
        self.KSKIP = KSKIP
        if "c" not in KSKIP:
            P.dma("sp", lambda e: e.dma_start(out=self.stripc[:], in_=dr["c_stripc"]), Bc, writes=[Bc])
            P.dma("sp", lambda e: e.dma_start(out=self.stripj[:], in_=dr["c_stripj"]), Bc, writes=[Bc])
            P.dma("sp", lambda e: e.dma_start(out=self.selg[:], in_=dr["c_selg"]), Bc, writes=[Bc])
        Bc0 = self.B_cache[0]
        if "e" not in KSKIP:
            P.dma("pool", lambda e: e.dma_start(out=self.ksaug[64:128, 0, :], in_=dr["c_e"]), Bc, writes=[Bc])
            P.dma("pool", lambda e: e.dma_start(out=self.ksaug[0:64, 1, :], in_=dr["c_e"]), Bc, writes=[Bc])
        if "v" not in KSKIP:
            self.memset(self.Vs[:, :, 64:128], 1.0, [Bc], eng="pool")
            self.memset(self.Vw[:, :, 64:128], 1.0, [Bc], eng="pool")
        self.identb = self.bfc[:, 0:128]
        self.onesb = self.bfc[:, 128:256]
        self.on1024 = self.bfc[:, 256:384]
        self.on256 = self.bfc[:, 384:512]
        self.on128 = self.bfc[:, 512:640]
        self.LONG = 640
        for l in range(self.NL):
            src = dr["x"] if l == 0 else dr["scr"]
            dst = dr["y"] if l == self.NL - 1 else dr["scr"]
            self.layer(l, src, dst, l == 0, l == self.NL - 1)

    def load_xtok(self, c, src, first_layer):
        P = self.P
        xtok = self.R4.bitcast(F32) if False else None
        xt = self.xtok_view()
        rd = [] if first_layer else [self.B_scr[c]]
        P.dma("sp", lambda e: e.dma_start(out=xt, in_=src[c * 512:(c + 1) * 512, :].rearrange("(a p) d -> p a d", p=128)),
              self.B_xtok, reads=rd, writes=[self.B_R4])

    def xtok_view(self):
        return self.R4[:, 0:8192].bitcast(F32).rearrange("p (a d) -> p a d", a=4)

    def layer(self, l, src, dst, first_layer, last_layer):
        P, dr = self.P, self.dr
        Blp = self.B_lp
        KSKIP = self.KSKIP
        if "l" not in KSKIP:
            P.dma("sp", lambda e: e.dma_start(out=self.lnp[:], in_=dr["lnp"][l]), Blp, writes=[Blp])
            P.dma("sp", lambda e: e.dma_start(out=self.qng[:], in_=dr["qng"][l]), Blp, writes=[Blp])
            P.dma("sp", lambda e: e.dma_start(out=self.kvng[:], in_=dr["kvng"][l]), Blp, writes=[Blp])
            P.dma("sp", lambda e: e.dma_start(out=self.cb[:], in_=dr["w_cb"][l]), Blp, writes=[Blp])
        if "p" not in KSKIP:
            P.dma("pool", lambda e: e.dma_start(out=self.cpe[:], in_=dr["w_cpe"][l]), Blp, writes=[Blp])
            P.dma("pool", lambda e: e.dma_start(out=self.c2[:], in_=dr["w_c2"][l]), Blp, writes=[Blp])
            P.dma("pool", lambda e: e.dma_start(out=self.ukt[:], in_=dr["w_ukt"][l]), Blp, writes=[Blp])
            P.dma("pool", lambda e: e.dma_start(out=self.uv[:], in_=dr["w_uv"][l]), Blp, writes=[Blp])
        if "m" not in KSKIP:
            self.memset(self.vc[:], 0.0, [self.B_vc], eng="pool")
            self.memset(self.kcT[:], 0.0, [self.B_kc], eng="pool")
            self.memset(self.kch[:], 0.0, [self.B_hist], eng="pool")
            self.memset(self.vch[:], 0.0, [self.B_hist], eng="pool")
            self.memset(self.hidpad[:], 0.0, [self.B_hid], eng="pool")
        self.load_xtok(0, src, first_layer)
        for c in range(self.NCH):
            self.chunk(l, c, src, dst, first_layer, last_layer)

    def chunk(self, l, c, src, dst, first_layer, last_layer):
        P = self.P
        R1, R2, R3 = self.R1, self.R2, self.R3
        B1, B2, B3, B4 = self.B_R1, self.B_R2, self.B_R3, self.B_R4
        xt = self.xtok_view()
        for k in range(8):
            pt_, pb_ = self.bank("G")
            for tt_ in range(4):
                self.tr(pt_[:, tt_ * 128:(tt_ + 1) * 128], xt[:, tt_, k * 128:(k + 1) * 128], self.identf[:],
                        [B4, self.B_const], pb_)
            self.act(R1[:, k, :], pt_[:], AF.Identity, [pb_], [B1], scale=ALPHA)
            self.cp(R3[:, k, :], pt_[:], [pb_], [B3])
        import os as _os
        _ks = _os.environ.get("KSTAGE", "")
        if _ks == "A":
            for k in range(8):
                self.cp(R2[:, k, :], R1[:, k, :], [B1], [B2])
        else:
            self.ffn(l, 0)
            if _ks != "F":
                self.layernorm(l, 0)
        if self.stop != 1:
            self.mixer(l, c)
            self.layernorm(l, 1)
        if self.stop == 0:
            self.ffn(l, 1)
        if c + 1 < self.NCH:
            self.load_xtok(c + 1, src, first_layer)
        if self.stop == 0:
            self.layernorm(l, 2, final=True)
        st = R1[:].rearrange("p a f -> p (a f)").rearrange("p (a d) -> p a d", a=4)
        for tt_ in range(4):
            for k2 in range(2):
                pt_, pb_ = self.bank("G")
                for kk in range(4):
                    k = k2 * 4 + kk
                    self.tr(pt_[:, kk * 128:(kk + 1) * 128], R2[:, k, tt_ * 128:(tt_ + 1) * 128], self.identf[:],
                            [B2, self.B_const], pb_)
                if k2 == 0:
                    self.cp(st[:, tt_, 0:512], pt_[:], [pb_], [B1], eng="act")
                else:
                    self.cp(st[:, tt_, 512:1024], pt_[:], [pb_], [B1])
        wb = self.B_out if last_layer else self.B_scr[c]
        P.dma("sp", lambda e: e.dma_start(out=dst[c * 512:(c + 1) * 512, :].rearrange("(a p) d -> p a d", p=128), in_=st),
              self.B_out, reads=[B1], writes=[wb])

    def ffn(self, l, which):
        dr = self.dr
        R1, R2, R3, R4 = self.R1, self.R2, self.R3, self.R4
        B1, B2, B3, B4 = self.B_R1, self.B_R2, self.B_R3, self.B_R4
        hT = R4[:].rearrange("p (j t) -> p j t", j=NFC)
        wg, wu, wd = dr["ffn_wg"][l, which], dr["ffn_wu"][l, which], dr["ffn_wd"][l, which]
        for fb in range(NFC // 2):
            f0 = fb * 256
            st, sbuf_ = self.wtile([
                (0, 8, 256, wg[:, f0:f0 + 256].rearrange("(k p) f -> p k f", p=128)),
                (2048, 8, 256, wu[:, f0:f0 + 256].rearrange("(k p) f -> p k f", p=128)),
            ])
            if st is None:
                continue
            wgv = self.wview(st, 0, 8, 256)
            wuv = self.wview(st, 2048, 8, 256)
            for fc in range(2):
                j = fb * 2 + fc
                pg, bg = self.bank("S")
                pu, bu = self.bank("G")
                for k in range(8):
                    self.mm(pg[:], wgv[:, k, fc * 128:(fc + 1) * 128], R3[:, k, :], k == 0, k == 7, [sbuf_, B3], bg)
                for k in range(8):
                    self.mm(pu[:], wuv[:, k, fc * 128:(fc + 1) * 128], R3[:, k, :], k == 0, k == 7, [sbuf_, B3], bu)
                t_, tb_ = self.tf()
                self.act(t_[:], pg[:], AF.Silu, [bg], [tb_])
                self.tt(hT[:, j, :], t_[:], pu[:], ALU.mult, [tb_, bu], [B4])
        for m in range(8):
            st, sbuf_ = self.wtile([(0, NFC, 128, wd[:, m * 128:(m + 1) * 128].rearrange("(j p) d -> p j d", p=128))])
            if st is None:
                continue
            wdv = self.wview(st, 0, NFC, 128)
            po, bo = self.bank("A")
            for j in range(NFC):
                self.mm(po[:], wdv[:, j, :], hT[:, j, :], j == 0, j == NFC - 1, [sbuf_, B4], bo)
            self.stt(R2[:, m, :], po[:], 0.5, R1[:, m, :], ALU.mult, ALU.add, [bo, B1], [B2])

    def layernorm(self, l, idx, final=False):
        R1, R2, R3, R4 = self.R1, self.R2, self.R3, self.R4
        B1, B2, B3, B4 = self.B_R1, self.B_R2, self.B_R3, self.B_R4
        pm, bm = self.bank("G")
        pq, bq = self.bank("G")
        if not final:
            zsq = R4[:, 0:4096].rearrange("p (k t) -> p k t", k=8)
            for k in range(8):
                self.cp(R3[:, k, :], R2[:, k, :], [B2], [B3])
                self.act(zsq[:, k, :], R2[:, k, :], AF.Square, [B2], [B4])
            for k in range(8):
                self.mm(pm[:], self.on1024, R3[:, k, :], k == 0, k == 7, [B3, self.B_const], bm)
            for k in range(8):
                self.mm(pq[:], self.on1024, zsq[:, k, :], k == 0, k == 7, [B4, self.B_const], bq)
        else:
            for k in range(8):
                self.cp(R3[:, k, :], R2[:, k, :], [B2], [B3])
            for k in range(8):
                self.mm(pm[:], self.on1024, R3[:, k, :], k == 0, k == 7, [B3, self.B_const], bm)
            for k in range(8):
                self.act(R3[:, k, :], R2[:, k, :], AF.Square, [B2], [B3])
            for k in range(8):
                self.mm(pq[:], self.on1024, R3[:, k, :], k == 0, k == 7, [B3, self.B_const], bq)
        mean, bmean = self.tf()
        m2, bm2 = self.tf()
        rstd, brstd = self.tf()
        nmr, bnmr = self.tf()
        self.cp(mean[:], pm[:], [bm], [bmean], eng="act")
        self.tt(m2[:], mean[:], mean[:], ALU.mult, [bmean], [bm2])
        self.tt(m2[:], pq[:], m2[:], ALU.subtract, [bq, bm2], [bm2])
        self.ts(m2[:], m2[:], LN_EPS, ALU.add, [bm2], [bm2])
        self.act(m2[:], m2[:], AF.Sqrt, [bm2], [bm2])
        self.recip(rstd[:], m2[:], [bm2], [brstd])
        self.stt(nmr[:], mean[:], -1.0, rstd[:], ALU.mult, ALU.mult, [bmean, brstd], [bnmr])
        gcol = self.lnp[:, idx * 16:idx * 16 + 8]
        bcol = self.lnp[:, idx * 16 + 8:idx * 16 + 16]
        for k in range(8):
            self.tt(R2[:, k, :], R2[:, k, :], rstd[:], ALU.mult, [B2, brstd], [B2])
            self.tt(R2[:, k, :], R2[:, k, :], nmr[:], ALU.add, [B2, bnmr], [B2])
            self.act(R2[:, k, :], R2[:, k, :], AF.Identity, [B2, self.B_lp], [B2], bias=bcol[:, k:k + 1], scale=gcol[:, k:k + 1])
            if not final:
                self.cp(R3[:, k, :], R2[:, k, :], [B2], [B3])
                self.act(R1[:, k, :], R2[:, k, :], AF.Identity, [B2], [B1], scale=ALPHA)

    def rmsnorm_fm(self, ps_list, ones_ap, gcols, outs, out_reads_writes):
        nk = len(ps_list)
        raws = []
        sqs = []
        for i, (pa, pb_) in enumerate(ps_list):
            r_, rb_ = self.tf()
            self.cp(r_[:], pa, [pb_], [rb_], eng="act")
            s_, sb_ = self.tb()
            self.act(s_[:], pa, AF.Square, [pb_], [sb_])
            raws.append((r_, rb_))
            sqs.append((s_, sb_))
        pss, bss = self.bank("G")
        for i, (s_, sb_) in enumerate(sqs):
            self.mm(pss[:], ones_ap, s_[:], i == 0, i == nk - 1, [sb_, self.B_const], bss)
        rq, brq = self.tf()
        self.act(rq[:], pss[:], AF.Sqrt, [bss], [brq], bias=RMS_EPS)
        self.recip(rq[:], rq[:], [brq], [brq])
        for i, (r_, rb_) in enumerate(raws):
            o_ap, o_w = outs[i]
            self.stt(o_ap, r_[:], gcols[i], rq[:], ALU.mult, ALU.mult, [rb_, brq, self.B_lp], o_w)

    def rope_apply(self, ps_main, b_main, ps_sw, b_sw, ctab, stab, rows, outs):
        t1, b1_ = self.tf()
        t2, b2_ = self.tf()
        self.tt(t1[0:rows, :], ps_main, ctab, ALU.mult, [b_main, self.B_rope], [b1_])
        self.tt(t2[0:rows, :], ps_sw, stab, ALU.mult, [b_sw, self.B_rope], [b2_])
        for (o_ap, r0, r1, wr) in outs:
            self.tt(o_ap, t1[r0:r1, :], t2[r0:r1, :], ALU.add, [b1_, b2_], wr)

    def mixer(self, l, c):
        P, dr = self.P, self.dr
        T = self.T
        R1, R2, R3, R4 = self.R1, self.R2, self.R3, self.R4
        B1, B2, B3, B4 = self.B_R1, self.B_R2, self.B_R3, self.B_R4
        Bc = self.B_cache[c]
        t0 = c * 512
        cs = slice(t0, t0 + 512)
        qabs = R4[:, 0:4096].rearrange("p (h t) -> p h t", h=8)
        qaug = R4[:, 4096:8192].rearrange("p (h t) -> p h t", h=8)
        qnw = None
        P.dma("sp", lambda e: e.dma_start(out=self.rope[:], in_=dr["c_rope"][:, :, t0:t0 + 512].rearrange("a p t -> p a t")),
              self.B_rope, writes=[self.B_rope])
        Cm, Sm, Cn, Sn = (self.rope[:, i, :] for i in range(4))

        st, sw_ = self.wtile([(0, 8, 384, dr["w_a"][l][:, 0:384].rearrange("(k p) f -> p k f", p=128))])
        if st is not None:
            wa = self.wview(st, 0, 8, 384)
            pl = []
            for i in range(2):
                p_, b_ = self.bank("G")
                for k in range(8):
                    self.mm(p_[:], wa[:, k, i * 128:(i + 1) * 128], R3[:, k, :], k == 0, k == 7, [sw_, B3], b_)
                pl.append((p_[:], b_))
            self.rmsnorm_fm(pl, self.on256, [self.qng[:, 0:1], self.qng[:, 1:2]],
                            [(self.cqn[:, 0, :], [self.B_cqn]), (self.cqn[:, 1, :], [self.B_cqn])], None)
            p_, b_ = self.bank("G")
            for k in range(8):
                self.mm(p_[:], wa[:, k, 256:384], R3[:, k, :], k == 0, k == 7, [sw_, B3], b_)
            self.rmsnorm_fm([(p_[:], b_)], self.on128, [self.kvng[:, 0:1]], [(self.ckvT[:, cs], [Bc])], None)
            pt_, pb_ = self.bank("G")
            ptb = pt_[:].bitcast(BF16)
            for tt_ in range(4):
                self.tr(ptb[:, tt_ * 128:(tt_ + 1) * 128], self.ckvT[:, t0 + tt_ * 128:t0 + (tt_ + 1) * 128], self.identb,
                        [Bc, self.B_const], pb_)
            self.cp(self.ckvtok[:, c * 4:(c + 1) * 4, :], ptb[:, 0:512].rearrange("p (a b) -> p a b", a=4), [pb_], [Bc])
        stp, swp_ = self.wtile([(0, 8, 192, dr["w_a"][l][:, 384:576].rearrange("(k p) f -> p k f", p=128))])
        if stp is not None:
            wap = self.wview(stp, 0, 8, 192)
            p1, b1_ = self.bank("G")
            p2, b2_ = self.bank("G")
            for k in range(8):
                self.mm(p1[0:96, :], wap[:, k, 0:96], R3[:, k, :], k == 0, k == 7, [swp_, B3], b1_)
            for k in range(8):
                self.mm(p2[0:96, :], wap[:, k, 96:192], R3[:, k, :], k == 0, k == 7, [swp_, B3], b2_)
            self.rope_apply(p1[0:96, :], b1_, p2[0:96, :], b2_, Cm[0:96, :], Sm[0:96, :], 96,
                            [(self.kpeT[:, cs], 0, 96, [Bc])])

        st, sw_ = self.wtile([(0, 2, 1088, dr["w_uq"][l].rearrange("(k p) f -> p k f", p=128))])
        if st is not None:
            wq = self.wview(st, 0, 2, 1088)
            for i in range(4):
                p_, b_ = self.bank("G")
                for k in range(2):
                    self.mm(p_[:], wq[:, k, i * 128:(i + 1) * 128], self.cqn[:, k, :], k == 0, k == 1, [sw_, self.B_cqn], b_)
                self.cp(self.qnope[:, i, :], p_[:], [b_], [self.B_qnope], eng=("act" if i % 2 else "dve"))
            uktv = self.ukt[:].rearrange("p (i m) -> p i m", i=4)
            for h in range(8):
                i, hh = h // 2, h % 2
                p_, b_ = self.bank("G")
                self.mm(p_[:], uktv[64 * hh:64 * hh + 64, i, :], self.qnope[64 * hh:64 * hh + 64, i, :], True, True,
                        [self.B_lp, self.B_qnope], b_)
                self.cp(qabs[:, h, :], p_[:], [b_], [B4], eng=("act" if h % 2 else "dve"))
            for j in range(3):
                rows = 96 if j < 2 else 64
                p1, b1_ = self.bank("G")
                p2, b2_ = self.bank("G")
                for k in range(2):
                    self.mm(p1[0:rows, :], wq[:, k, 512 + j * 96:512 + j * 96 + rows], self.cqn[:, k, :], k == 0, k == 1,
                            [sw_, self.B_cqn], b1_)
                for k in range(2):
                    self.mm(p2[0:rows, :], wq[:, k, 800 + j * 96:800 + j * 96 + rows], self.cqn[:, k, :], k == 0, k == 1,
                            [sw_, self.B_cqn], b2_)
                self.rope_apply(p1[0:rows, :], b1_, p2[0:rows, :], b2_, Cm[0:rows, :], Sm[0:rows, :], rows,
                                [(self.qpe[0:rows, j, :], 0, rows, [self.B_qpe])])

        for i in range(4):
            st, sw_ = self.wtile([(0, 8, 128, dr["w_q"][l][:, i * 128:(i + 1) * 128].rearrange("(k p) f -> p k f", p=128)),
                                  (1024, 8, 128, dr["w_q"][l][:, 512 + i * 128:512 + (i + 1) * 128].rearrange("(k p) f -> p k f", p=128))])
            if st is None:
                continue
            wq1 = self.wview(st, 0, 8, 128)
            wq2 = self.wview(st, 1024, 8, 128)
            p1, b1_ = self.bank("G")
            p2, b2_ = self.bank("G")
            for k in range(8):
                self.mm(p1[:], wq1[:, k, :], R3[:, k, :], k == 0, k == 7, [sw_, B3], b1_)
            for k in range(8):
                self.mm(p2[:], wq2[:, k, :], R3[:, k, :], k == 0, k == 7, [sw_, B3], b2_)
            self.rope_apply(p1[:], b1_, p2[:], b2_, Cn, Sn, 128,
                            [(qaug[0:64, i, :], 0, 64, [B4]), (qaug[64:128, 4 + i, :], 64, 128, [B4])])
        for bi in range(3):
            st, sw_ = self.wtile([(0, 8, 128, dr["w_k"][l][:, bi * 128:(bi + 1) * 128].rearrange("(k p) f -> p k f", p=128)),
                                  (1024, 8, 128, dr["w_k"][l][:, 512 + bi * 128:512 + (bi + 1) * 128].rearrange("(k p) f -> p k f", p=128))]
                                 + ([(2048, 8, 128, dr["w_k"][l][:, 384:512].rearrange("(k p) f -> p k f", p=128))] if bi == 0 else []))
            if st is None:
                continue
            wk1 = self.wview(st, 0, 8, 128)
            wk2 = self.wview(st, 1024, 8, 128)
            if bi == 0:
                self.cp(self.kch[:, 0:16], self.kch[:, 512:528], [self.B_hist], [self.B_hist])
                self.cp(self.vch[:, 0:16], self.vch[:, 512:528], [self.B_hist], [self.B_hist])
                wvc = self.wview(st, 2048, 8, 128)
                p1, b1_ = self.bank("G")
                for k in range(8):
                    self.mm(p1[:], wvc[:, k, :], R3[:, k, :], k == 0, k == 7, [sw_, B3], b1_)
                self.cp(self.vch[:, 16:528], p1[:], [b1_], [self.B_hist], eng="act")
            p1, b1_ = self.bank("G")
            p2, b2_ = self.bank("G")
            for k in range(8):
                self.mm(p1[:], wk1[:, k, :], R3[:, k, :], k == 0, k == 7, [sw_, B3], b1_)
            for k in range(8):
                self.mm(p2[:], wk2[:, k, :], R3[:, k, :], k == 0, k == 7, [sw_, B3], b2_)
            if bi == 0:
                outs = [(self.kch[:, 16:528], 0, 128, [self.B_hist])]
            elif bi == 1:
                outs = [(self.ksaug[0:64, 0, cs], 0, 64, [Bc]), (self.ksaug[64:128, 1, cs], 64, 128, [Bc])]
            else:
                outs = [(self.kwT[:, cs], 0, 128, [Bc])]
            self.rope_apply(p1[:], b1_, p2[:], b2_, Cn, Sn, 128, outs)
        st, sw_ = self.wtile([(0, 8, 288, dr["w_v"][l].rearrange("(k p) f -> p k f", p=128))])
        if st is not None:
            wv = self.wview(st, 0, 8, 288)
            for tt_ in range(4):
                p_, b_ = self.bank("G")
                for k in range(8):
                    self.mm(p_[:, 0:256], R3[:, k, tt_ * 128:(tt_ + 1) * 128], wv[:, k, 0:256], k == 0, k == 7, [sw_, B3], b_)
                kt = c * 4 + tt_
                vsrc = p_[:, 0:128].rearrange("p (g d) -> p g d", g=2)
                vdst = self.Vs[:, kt, :].rearrange("p (g d) -> p g d", g=3)[:, 0:3:2, :]
                self.cp(vdst, vsrc, [b_], [Bc])
                vsrc2 = p_[:, 128:256].rearrange("p (g d) -> p g d", g=2)
                vdst2 = self.Vw[:, kt, :].rearrange("p (g d) -> p g d", g=3)[:, 0:3:2, :]
                self.cp(vdst2, vsrc2, [b_], [Bc], eng="act")
            p_, b_ = self.bank("G")
            for k in range(8):
                self.mm(p_[0:32, :], wv[:, k, 256:288], R3[:, k, :], k == 0, k == 7, [sw_, B3], b_)
            self.act(self.gsig[:], p_[0:32, :], AF.Sigmoid, [b_], [self.B_gsig])

        cc0 = 32 * c
        for kv in range(2):
            st, sw_ = self.wtile([(0, 32, 128, dr["w_c1k" if kv == 0 else "w_c1v"][l].rearrange("p (i h) -> p i h", i=32))])
            if st is None:
                continue
            w1 = self.wview(st, 0, 32, 128)
            hist = self.kch if kv == 0 else self.vch
            if c == 0:
                p_, b_ = self.bank("G")
                for i in range(32):
                    self.mm(p_[:, 0:1], w1[0:64, i, :], self.cpe[0:64, kv * 32 + i:kv * 32 + i + 1], i == 0, i == 31,
                            [sw_, self.B_lp], b_)
                self.tt(self.pb[:, kv:kv + 1], p_[:, 0:1], self.cb[:, kv:kv + 1], ALU.add, [b_, self.B_lp], [self.B_pb])
            for g in range(2):
                p_, b_ = self.bank("G")
                for i in range(32):
                    self.mm(p_[:, 0:32], w1[64 * g:64 * g + 64, i, :], hist[64 * g:64 * g + 64, i:i + 497:16],
                            i == 0, i == 31, [sw_, self.B_hist], b_)
                ht = self.hidt
                Bh = self.B_hid
                self.act(ht[:, 0, :], p_[:, 0:32], AF.Identity, [b_, self.B_pb], [Bh], bias=self.pb[:, kv:kv + 1])
                self.tt(ht[:, 1, :], ht[:, 0, :], ht[:, 0, :], ALU.mult, [Bh], [Bh])
                self.ts(ht[:, 1, :], ht[:, 1, :], 0.044715, ALU.mult, [Bh], [Bh], s2=1.0, op1=ALU.add)
                self.tt(ht[:, 1, :], ht[:, 1, :], ht[:, 0, :], ALU.mult, [Bh], [Bh])
                self.act(ht[:, 2, :], ht[:, 1, :], AF.Sigmoid, [Bh], [Bh], scale=1.5957691216057308)
                if kv == 0:
                    self.tt(self.hidb[:], ht[:, 0, :], ht[:, 2, :], ALU.mult, [Bh], [Bh])
                    p2, b2_ = self.bank("G")
                    self.mm(p2[0:64, 0:32], self.c2[:, 0:64], self.hidb[:], True, True, [Bh, self.B_lp], b2_)
                    self.cp(self.kcT[64 * g:64 * g + 64, cc0:cc0 + 32], p2[0:64, 0:32], [b2_], [self.B_kc])
                else:
                    po = 32 * (c % 4)
                    self.tt(self.hidpad[:, po:po + 32], ht[:, 0, :], ht[:, 2, :], ALU.mult, [Bh], [Bh])
                    p2, b2_ = self.bank("G")
                    self.mm(p2[:, 0:64], self.hidpad[:], self.c2[:, 64:128], True, True, [Bh, self.B_lp], b2_)
                    self.tt(self.vc[:, c // 4, g, :], self.vc[:, c // 4, g, :], p2[:, 0:64], ALU.add, [b2_, self.B_vc], [self.B_vc])
                    self.memset(self.hidpad[:, po:po + 32], 0.0, [Bh])

        self.memset(self.fence_t[:, 0:1], 0.0, [B2, self.B_oc, self.B_osb, self.B_owb, self.B_acc, self.sbias[0][1], self.sbias[1][1]])
        self.compressed(l, c, qnw, qaug)

        self.attention(l, c, qabs, qaug, qnw)

        self.memset(self.fence_t[:, 1:2], 0.0, [B2, self.B_oc, self.B_osb, self.B_owb, self.B_acc, self.sbias[0][1], self.sbias[1][1]])
        yT = R4[:, 0:4096].rearrange("p (k t) -> p k t", k=8)
        for m in range(8):
            ms = slice(m * 128, (m + 1) * 128)
            st, sw_ = self.wtile([(0, 4, 128, dr["w_pm"][l][:, ms].rearrange("(k p) f -> p k f", p=128)),
                                  (512, 4, 128, dr["w_pn"][l][:, ms].rearrange("(k p) f -> p k f", p=128)),
                                  (1024, 8, 128, dr["w_gm"][l][:, ms].rearrange("(k p) f -> p k f", p=128)),
                                  (2048, 8, 128, dr["w_gn"][l][:, ms].rearrange("(k p) f -> p k f", p=128))])
            if st is None:
                continue
            wpm = self.wview(st, 0, 4, 128)
            wpn = self.wview(st, 512, 4, 128)
            wgm = self.wview(st, 1024, 8, 128)
            wgn = self.wview(st, 2048, 8, 128)
            ppm, bpm = self.bank("G")
            pgm, bgm = self.bank("S")
            ppn, bpn = self.bank("G")
            pgn, bgn = self.bank("S")
            for k in range(4):
                self.mm(ppm[:], wpm[:, k, :], self.omT[:, k, :], k == 0, k == 3, [sw_, self.B_omT], bpm)
            for k in range(8):
                self.mm(pgm[:], wgm[:, k, :], R3[:, k, :], k == 0, k == 7, [sw_, B3], bgm)
            for k in range(4):
                self.mm(ppn[:], wpn[:, k, :], self.onT[:, k, :], k == 0, k == 3, [sw_, self.B_onT], bpn)
            for k in range(8):
                self.mm(pgn[:], wgn[:, k, :], R3[:, k, :], k == 0, k == 7, [sw_, B3], bgn)
            s1, bs1 = self.tf()
            s2, bs2 = self.tf()
            self.act(s1[:], pgm[:], AF.Sigmoid, [bgm], [bs1])
            self.act(s2[:], pgn[:], AF.Sigmoid, [bgn], [bs2])
            self.tt(s1[:], ppm[:], s1[:], ALU.mult, [bs1, bpm], [bs1])
            self.tt(s2[:], ppn[:], s2[:], ALU.mult, [bs2, bpn], [bs2])
            self.tt(yT[:, m, :], s1[:], s2[:], ALU.add, [bs1, bs2], [B4])
        for db in range(2):
            d0 = db * 512
            st, sw_ = self.wtile([(0, 8, 512, dr["w_out"][l][:, d0:d0 + 512].rearrange("(k p) f -> p k f", p=128))])
            if st is None:
                continue
            wo = self.wview(st, 0, 8, 512)
            for mm_ in range(4):
                m = db * 4 + mm_
                p_, b_ = self.bank("A")
                for k in range(8):
                    self.mm(p_[:], wo[:, k, mm_ * 128:(mm_ + 1) * 128], yT[:, k, :], k == 0, k == 7, [sw_, B4], b_)
                self.tt(R2[:, m, :], p_[:], R1[:, m, :], ALU.add, [b_, B1], [B2])

    def compressed(self, l, c, qnw, qaug):
        P = self.P
        R2, B2, B4 = self.R2, self.B_R2, self.B_R4
        NCCc = 32 * (c + 1)
        ntile = (NCCc + 127) // 128
        NS = self.NS
        do_sel = (NS > 16) and (c >= 2)
        oc = R2[:, 0:4, :]
        if (not do_sel) or (self.NCC // 4 < 64):
            for i in range(4):
                self.memset(qaug[64:128, i, :], 0.0, [B4])
                self.memset(qaug[0:64, 4 + i, :], 0.0, [B4])
        for tt_ in range(4):
            qt = c * 4 + tt_
            ts_ = slice(tt_ * 128, (tt_ + 1) * 128)
            x0 = 248 - 8 * qt
            for g in range(2):
                rs_ = slice(64 * g, 64 * g + 64)
                for hh in range(4):
                    i = hh
                    ps_, bs_ = self.bank("S")
                    self.mm(ps_[:, 0:NCCc], qaug[rs_, i + 4 * g, ts_], self.kcT[rs_, 0:NCCc], True, True, [B4, self.B_kc], bs_)
                    k_ = self.cmp_i % 2
                    self.cmp_i += 1
                    sbt, sbb = self.sbias[k_]
                    pet, peb = self.pexp[k_]
                    pnt, pnb_ = self.pnb[k_]
                    pTt, pTb = self.pTt[k_]
                    self.stt(sbt[:, 0:NCCc], ps_[:, 0:NCCc], SC_NSA, self.stripc[:, x0:x0 + NCCc], ALU.mult, ALU.add,
                             [bs_, self.B_const], [sbb])
                    self.memset(sbt[:, 0:1], NEGB, [sbb])
                    sm_, smb = self.sm()
                    self.act(pet[:, 0:NCCc], sbt[:, 0:NCCc], AF.Exp, [sbb], [peb])
                    self.P.add("dve", (lambda o, i_: (lambda e: e.tensor_reduce(out=o, in_=i_, axis=AX.X, op=ALU.add)))(
                        sm_[:, 0:1], pet[:, 0:NCCc]), reads=[peb], writes=[smb])
                    self.ts(sm_[:, 1:2], sm_[:, 0:1], 1e-30, ALU.max, [smb], [smb])
                    self.recip(sm_[:, 2:3], sm_[:, 1:2], [smb], [smb])
                    self.ts(pnt[:, 0:NCCc], pet[:, 0:NCCc], sm_[:, 2:3], ALU.mult, [peb, smb], [pnb_])
                    if do_sel:
                        if hh == 0:
                            self.ts(self.p4[:, 0:NCCc], pet[:, 0:NCCc], sm_[:, 2:3], ALU.mult, [peb, smb], [self.B_p4])
                        else:
                            self.stt(self.p4[:, 0:NCCc], pet[:, 0:NCCc], sm_[:, 2:3], self.p4[:, 0:NCCc], ALU.mult, ALU.add,
                                     [peb, smb, self.B_p4], [self.B_p4])
                    ptp, ptb_ = self.bank("G")
                    ptbf = ptp[:].bitcast(BF16)
                    for ti in range(ntile):
                        w_ = min(128, NCCc - ti * 128)
                        self.tr(ptbf[0:w_, ti * 128:(ti + 1) * 128], pnt[:, ti * 128:ti * 128 + w_], self.identb,
                                [pnb_, self.B_const], ptb_)
                    for ti in range(ntile):
                        w_ = min(128, NCCc - ti * 128)
                        self.cp(pTt[0:w_, ti, :], ptbf[0:w_, ti * 128:(ti + 1) * 128], [ptb_], [pTb],
                                eng=("act" if ti % 2 else "dve"))
                    po_, bo_ = self.bank("A")
                    for ti in range(ntile):
                        w_ = min(128, NCCc - ti * 128)
                        self.mm(po_[0:64, 0:128], self.vc[0:w_, ti, g, :], pTt[0:w_, ti, :], ti == 0, ti == ntile - 1,
                                [self.B_vc, pTb], bo_)
                    self.cp(oc[rs_, i, ts_], po_[0:64, 0:128], [bo_], [self.B_oc])
                if do_sel:
                    Bi = self.B_imp
                    W4 = NCCc + 4
                    nsb_ = NS
                    self.memset(self.p4[:, NCCc:self.NCC + 4], 0.0, [self.B_p4])
                    nj = self.NCC // 4
                    self.P.add("dve", (lambda o, i_: (lambda e: e.tensor_reduce(out=o, in_=i_, axis=AX.X, op=ALU.add)))(
                        self.imp[:, 0:nj], self.p4[:, 0:4 * nj].rearrange("p (j r) -> p j r", r=4)),
                        reads=[self.B_p4], writes=[Bi])
                    self.tt(self.imp[:, 0:nj], self.imp[:, 0:nj], self.p4[:, 4:4 * nj + 4:4], ALU.add, [Bi, self.B_p4], [Bi])
                    xj = 62 - 2 * qt
                    self.tt(self.imp[:, 0:nj], self.imp[:, 0:nj], self.stripj[:, xj:xj + nj], ALU.add, [Bi, self.B_const], [Bi])
                    self.ts(self.imp[:, 0:1], self.imp[:, 0:1], 1e30, ALU.add, [Bi], [Bi])
                    self.P.add("dve", lambda e: e.max(out=self.m8[:, 0:8], in_=self.imp[:, 0:nj]), reads=[Bi], writes=[Bi])
                    self.P.add("dve", lambda e: e.match_replace(out=self.imp2[:, 0:nj], in_to_replace=self.m8[:, 0:8],
                                                                 in_values=self.imp[:, 0:nj], imm_value=-3e38),
                               reads=[Bi], writes=[Bi])
                    self.P.add("dve", lambda e: e.max(out=self.m8[:, 8:16], in_=self.imp2[:, 0:nj]), reads=[Bi], writes=[Bi])
                    self.ts(self.imp2[:, 0:nj], self.imp[:, 0:nj], self.m8[:, 15:16], ALU.is_ge, [Bi], [Bi])
                    self.ts(self.nsb[:, 0:nj], self.imp2[:, 0:nj], -NEGB, ALU.mult, [Bi], [Bi], s2=NEGB, op1=ALU.add)
                    ptp, ptb_ = self.bank("G")
                    ptbf = ptp[:].bitcast(BF16)
                    self.tr(ptbf[0:nj, 0:128], self.nsb[:, 0:nj], self.identb, [Bi, self.B_const], ptb_)
                    for i2 in range(4):
                        if g == 0:
                            self.cp(qaug[64:64 + nj, i2, ts_], ptbf[0:nj, 0:128], [ptb_], [B4])
                        else:
                            self.cp(qaug[0:nj, 4 + i2, ts_], ptbf[0:nj, 0:128], [ptb_], [B4])

    def attention(self, l, c, qabs, qaug, qnw):
        R2, B2, B4 = self.R2, self.B_R2, self.B_R4
        LONG = self.LONG
        nkt = 4 * c + 4
        caches = [self.B_cache[cc] for cc in range(c + 1)]
        uvv = self.uv[:].rearrange("p (h d) -> p h d", h=8)
        selv = self.selg[:].rearrange("p (a m) -> p a m", a=12)

        def bias_ap(d):
            s0 = LONG + (3 - d) * 128
            return self.bfc[:, s0:s0 + 512]

        for h in range(8):
            i, hh = h // 2, h % 2
            j3, r3 = h // 3, h % 3
            pO, bO = self.bank("A")
            pS, bS = self.bank("A")
            for kt in range(nkt):
                ks = slice(kt * 128, (kt + 1) * 128)
                ps_, bs_ = self.bank("S")
                diag = kt >= 4 * c
                self.mm(ps_[:], self.ckvT[:, ks], qabs[:, h, :], True, False, caches + [B4], bs_)
                self.mm(ps_[:], self.kpeT[32 * r3:32 * r3 + 32, ks], self.qpe[32 * r3:32 * r3 + 32, j3, :], False, not diag,
                        caches + [self.B_qpe], bs_)
                if diag:
                    self.mm(ps_[:], self.identb, bias_ap(kt - 4 * c), False, True, [self.B_const], bs_)
                pt_, ptb_ = self.ptile()
                self.act(pt_[:], ps_[:], AF.Exp, [bs_], [ptb_], scale=SC_MLA)
                self.mm(pO[:], self.ckvtok[:, kt, :], pt_[:], kt == 0, kt == nkt - 1, caches + [ptb_], bO)
                self.mm(pS[:], self.onesb, pt_[:], kt == 0, kt == nkt - 1, [self.B_const, ptb_], bS)
            rs_, rsb = self.tf()
            self.cp(rs_[:], pS[:], [bS], [rsb], eng="act")
            self.recip(rs_[:], rs_[:], [rsb], [rsb])
            ol, olb = self.tb()
            self.tt(ol[:], pO[:], rs_[:], ALU.mult, [bO, rsb], [olb])
            p_, b_ = self.bank("G")
            self.mm(p_[0:64, :], uvv[:, h, :], ol[:], True, True, [self.B_lp, olb], b_)
            self.cp(self.omT[64 * hh:64 * hh + 64, i, :], p_[0:64, :], [b_], [self.B_omT])

        kt0 = max(0, 4 * c - 4)
        osb = R2[:, 4, :]
        owb = R2[:, 5, :]
        acc = R2[:, 6, :]
        for i in range(4):
            for g in range(2):
                h = i + 4 * g
                osl = slice(64 * g, 64 * g + 64)
                sml = slice(64 * (1 - g), 64 * (1 - g) + 64)
                vcol = slice(0, 128) if g == 0 else slice(64, 192)
                pO, bO = self.bank("A")
                for kt in range(nkt):
                    ks = slice(kt * 128, (kt + 1) * 128)
                    ps_, bs_ = self.bank("S")
                    diag = kt >= 4 * c
                    self.mm(ps_[:], self.ksaug[:, g, ks], qaug[:, h, :], True, not diag, caches + [B4, self.B_const], bs_)
                    if diag:
                        self.mm(ps_[:], self.identb, bias_ap(kt - 4 * c), False, True, [self.B_const], bs_)
                    pt_, ptb_ = self.ptile()
                    self.act(pt_[:], ps_[:], AF.Exp, [bs_], [ptb_], scale=SC_NSA)
                    self.mm(pO[:], self.Vs[:, kt, vcol], pt_[:], kt == 0, kt == nkt - 1, caches + [ptb_, self.B_const], bO)
                rs_, rsb = self.tf()
                self.cp(rs_[sml, :], pO[sml, :], [bO], [rsb], eng="act")
                self.recip(rs_[sml, :], rs_[sml, :], [rsb], [rsb])
                self.tt(osb[osl, :], pO[osl, :], rs_[sml, :], ALU.mult, [bO, rsb], [self.B_osb])
                pO, bO = self.bank("A")
                for kt in range(kt0, nkt):
                    ks = slice(kt * 128, (kt + 1) * 128)
                    ps_, bs_ = self.bank("S")
                    self.mm(ps_[:], self.kwT[osl, ks], qaug[osl, h, :], True, False, caches + [B4], bs_)
                    p_ = kt - 4 * c + 4
                    s0 = LONG + (7 - p_) * 128
                    self.mm(ps_[:], self.identb, self.bfc[:, s0:s0 + 512], False, True, [self.B_const], bs_)
                    pt_, ptb_ = self.ptile()
                    self.act(pt_[:], ps_[:], AF.Exp, [bs_], [ptb_], scale=SC_NSA)
                    self.mm(pO[:], self.Vw[:, kt, vcol], pt_[:], kt == kt0, kt == nkt - 1, caches + [ptb_, self.B_const], bO)
                rs_, rsb = self.tf()
                self.cp(rs_[sml, :], pO[sml, :], [bO], [rsb], eng="act")
                self.recip(rs_[sml, :], rs_[sml, :], [rsb], [rsb])
                self.tt(owb[osl, :], pO[osl, :], rs_[sml, :], ALU.mult, [bO, rsb], [self.B_owb])
            srcs = [R2[:, i, :], osb, owb]
            for br in range(3):
                pg_, bg_ = self.bank("G")
                self.mm(pg_[:], selv[:, br * 4 + i, :], self.gsig[:], True, True, [self.B_const, self.B_gsig], bg_)
                if br == 0:
                    self.tt(acc, pg_[:], srcs[0], ALU.mult, [self.B_oc, bg_], [self.B_acc])
                elif br == 1:
                    self.tt(osb, pg_[:], osb, ALU.mult, [self.B_osb, bg_], [self.B_osb])
                    self.tt(acc, acc, osb, ALU.add, [self.B_acc, self.B_osb], [self.B_acc])
                else:
                    self.tt(owb, pg_[:], owb, ALU.mult, [self.B_owb, bg_], [self.B_owb])
                    self.tt(self.onT[:, i, :], acc, owb, ALU.add, [self.B_acc, self.B_owb], [self.B_onT])


_CACHE = {}


def kernel(**inputs):
    x = np.asarray(inputs["x"], dtype=np.float32)
    Bn, T, _ = x.shape
    w = prep_weights(inputs)
    consts = make_consts(T)
    key = (T,)
    if key not in _CACHE:
        b = Builder(T)
        nc = b.build()
        _CACHE[key] = (b, nc)
    b, nc = _CACHE[key]
    shared = {}
    shared.update(w)
    shared.update(consts)
    in_maps = []
    for i in range(Bn):
        m = dict(shared)
        m["x"] = np.ascontiguousarray(x[i])
        in_maps.append(m)
    res = run_bass_kernel_spmd(nc, in_maps, core_ids=list(range(Bn)))
    return np.stack([np.asarray(r["y"], dtype=np.float32) for r in res.results], 0)
```

```python
from contextlib import ExitStack
import numpy as np
import concourse.bass as bass
import concourse.mybir as mybir
from concourse.bass_utils import run_bass_kernel_spmd

F32 = mybir.dt.float32
BF16 = mybir.dt.bfloat16
AF = mybir.ActivationFunctionType
ALU = mybir.AluOpType
AX = mybir.AxisListType

D = 1024
DFF = 2816
NFC = DFF // 128
DEPTH = 2
ALPHA = (2 * DEPTH) ** 0.25
LN_EPS = 1e-5
RMS_EPS = 1e-6
NEGB = -30000.0
SC_MLA = 96 ** -0.5
SC_NSA = 0.125
THETA = 500000.0
SLOT = 4096
NSLOT = 2
PF = 1

ENGS = ("pe", "act", "dve", "pool", "sp")


class Buf:
    __slots__ = ("name", "w", "r", "dsem", "dcnt", "x")

    def __init__(self, name, x=False):
        self.name = name
        self.x = x
        self.w = None
        self.r = {}
        self.dsem = None
        self.dcnt = 0


class Prog:
    def __init__(self, nc, dry=False):
        self.nc = nc
        self.dry = dry
        self.ops = {e: [] for e in ENGS}
        self.dma_cnt = {}
        self.dma_keys = []
        self.dma_sem = {}

    def _deps(self, eng, reads, writes):
        deps = set()
        for b in reads:
            if b.w is not None:
                deps.add(b.w)
            if b.x:
                for k_, h_ in b.r.items():
                    if k_ != eng:
                        deps.add(h_)
        for b in writes:
            if b.w is not None:
                deps.add(b.w)
            for h in b.r.values():
                deps.add(h)
        if eng == "pe":
            deps = {h for h in deps if not (h[0] == "e" and h[1] == "pe")}
        return deps

    def add(self, eng, fn, reads=(), writes=()):
        if self.dry:
            return None
        deps = self._deps(eng, reads, writes)
        idx = len(self.ops[eng])
        h = ("e", eng, idx)
        self.ops[eng].append([fn, deps, h, False])
        for b in writes:
            b.w = h
            b.r = {}
        for b in reads:
            b.r[eng] = h
        return h

    def dma(self, eng, fn, sem_buf, reads=(), writes=()):
        if self.dry:
            return None
        deps = self._deps(eng, reads, writes)
        kind = "sw" if eng == "pool" else "hw"
        key = (sem_buf, kind)
        if key not in self.dma_cnt:
            self.dma_cnt[key] = 0
            self.dma_keys.append(key)
        self.dma_cnt[key] += 1
        h = ("d", key, self.dma_cnt[key])
        self.ops[eng].append([fn, deps, h, False])
        for b in writes:
            b.w = h
            b.r = {}
        for b in reads:
            b.r[("d", id(sem_buf), kind)] = h
        return h

    def emit(self, final_waits=()):
        nc = self.nc
        for e in ENGS:
            for op in self.ops[e]:
                for h in op[1]:
                    if h[0] == "e":
                        self.ops[h[1]][h[2]][3] = True
        cum = {}
        for e in ENGS:
            c = 0
            arr = []
            for op in self.ops[e]:
                if op[3]:
                    c += 1
                arr.append(c)
            cum[e] = arr
        with ExitStack() as es:
            esem = {e: es.enter_context(nc.semaphore("s_" + e)) for e in ENGS}
            for i, key in enumerate(self.dma_keys):
                self.dma_sem[key] = es.enter_context(nc.semaphore("d%d" % i))
            block = es.enter_context(nc.Block())

            def run(e, engobj, extra=()):
                waited_e = {}
                waited_d = {}

                def do_wait(h):
                    if h[0] == "e":
                        v = cum[h[1]][h[2]]
                        if waited_e.get(h[1], 0) >= v:
                            return
                        waited_e[h[1]] = v
                        engobj.wait_ge(esem[h[1]], v)
                    else:
                        key = h[1]
                        v = 16 * h[2]
                        if waited_d.get(key, 0) >= v:
                            return
                        waited_d[key] = v
                        engobj.wait_ge(self.dma_sem[key], v)

                for fn, deps, h, need in self.ops[e]:
                    for d in deps:
                        do_wait(d)
                    ins = fn(engobj)
                    if h[0] == "d":
                        ins.then_inc(self.dma_sem[h[1]], 16)
                    elif need:
                        ins.then_inc(esem[e], 1)
                for hh in extra:
                    do_wait(hh)

            @block.tensor
            def _(eng):
                run("pe", eng)

            @block.scalar
            def _(eng):
                run("act", eng)

            @block.vector
            def _(eng):
                run("dve", eng)

            @block.gpsimd
            def _(eng):
                run("pool", eng)

            @block.sync
            def _(eng):
                run("sp", eng, extra=final_waits)


def make_consts(T):
    c = {}
    c["c_identf"] = np.eye(128, dtype=np.float32)
    bfc = np.zeros((128, 5 * 128 + 11 * 128), np.float32)
    bfc[:, 0:128] = np.eye(128)
    bfc[:, 128:256] = 1.0
    bfc[:, 256:384] = 1.0 / 1024
    bfc[:, 384:512] = 1.0 / 256
    bfc[:, 512:640] = 1.0 / 128
    p = np.arange(128)[:, None]
    f = np.arange(128)[None, :]
    for b in range(11):
        r = 3 - b
        if r > 0 or r < -4:
            blk = np.full((128, 128), NEGB, np.float32)
        elif r == 0:
            blk = np.where(p <= f, 0.0, NEGB).astype(np.float32)
        elif r == -4:
            blk = np.where(p > f, 0.0, NEGB).astype(np.float32)
        else:
            blk = np.zeros((128, 128), np.float32)
        bfc[:, 640 + b * 128: 640 + (b + 1) * 128] = blk
    c["c_bfc"] = bfc
    E = (np.arange(64)[:, None] == (np.arange(T)[None, :] // 64)).astype(np.float32)
    c["c_e"] = E
    t = np.arange(T, dtype=np.float32)
    inv_m = (1.0 / (THETA ** (np.arange(0, 32, 2, dtype=np.float32) / 32))).astype(np.float32)
    ang_m = t[None, :] * inv_m[:, None]
    cm = np.cos(ang_m).astype(np.float32)
    sm = np.sin(ang_m).astype(np.float32)
    Cm = np.concatenate([cm, cm], 0)
    Sm = np.concatenate([-sm, sm], 0)
    inv_n = (1.0 / (THETA ** (np.arange(0, 16, 2, dtype=np.float32) / 16))).astype(np.float32)
    ang_n = t[None, :] * inv_n[:, None]
    cn = np.cos(ang_n).astype(np.float32)
    sn = np.sin(ang_n).astype(np.float32)
    Cn = np.concatenate([cn, cn, np.ones((48, T), np.float32)], 0)
    Sn = np.concatenate([-sn, sn, np.zeros((48, T), np.float32)], 0)
    rope = np.zeros((4, 128, T), np.float32)
    rope[0, 0:96] = np.tile(Cm, (3, 1))
    rope[1, 0:96] = np.tile(Sm, (3, 1))
    rope[2] = np.tile(Cn, (2, 1))
    rope[3] = np.tile(Sn, (2, 1))
    c["c_rope"] = rope
    m = np.floor((np.arange(128) - 15) / 16.0)[:, None]
    x = np.arange(512)[None, :] - 248
    c["c_stripc"] = np.where(x <= m, 0.0, NEGB).astype(np.float32)
    hp = (np.arange(128) >= 64).astype(np.int64)[:, None]
    rel = np.arange(128)[None, :] - 62
    sj = np.zeros((128, 128), np.float32)
    sj = np.where(rel == hp, 2e30, sj)
    sj = np.where(rel == hp - 1, 4e30, sj)
    sj = np.where(rel > hp, -1e30, sj)
    c["c_stripj"] = sj.astype(np.float32)
    sel = np.zeros((32, 12, 128), np.float32)
    for br in range(3):
        for i in range(4):
            sel[br * 8 + i, br * 4 + i, 0:64] = 1.0
            sel[br * 8 + 4 + i, br * 4 + i, 64:128] = 1.0
    c["c_selg"] = sel.reshape(32, 12 * 128)
    return c


def prep_weights(inp):
    L = DEPTH
    w = {}
    g = lambda k: np.asarray(inp[k], dtype=np.float32)
    w["ffn_wg"] = np.ascontiguousarray(np.stack([g("ffn1_wg"), g("ffn2_wg")], 1))
    w["ffn_wu"] = np.ascontiguousarray(np.stack([g("ffn1_wu"), g("ffn2_wu")], 1))
    w["ffn_wd"] = np.ascontiguousarray(np.stack([g("ffn1_wd"), g("ffn2_wd")], 1))
    win = g("w_in")
    o_cq, o_ckv, o_kpe, o_qn, o_kvn, o_gn, o_gm, o_gnn = 0, 256, 384, 416, 928, 1696, 1720, 2744
    sw32 = np.concatenate([np.arange(16, 32), np.arange(0, 16)])
    kpe = win[:, :, o_kpe:o_kpe + 32]
    kpe_sw = kpe[:, :, sw32]
    w["w_a"] = np.ascontiguousarray(np.concatenate(
        [win[:, :, o_cq:o_cq + 384], np.tile(kpe, (1, 1, 3)), np.tile(kpe_sw, (1, 1, 3))], 2))
    sw64 = np.concatenate([np.arange(8, 16), np.arange(0, 8), np.arange(16, 64)])
    qn = win[:, :, o_qn:o_qn + 512].reshape(L, D, 8, 64)
    qn_sw = qn[:, :, :, sw64]
    pair = [0, 4, 1, 5, 2, 6, 3, 7]
    w["w_q"] = np.ascontiguousarray(np.concatenate(
        [qn[:, :, pair].reshape(L, D, 512), qn_sw[:, :, pair].reshape(L, D, 512)], 2))
    kvn = win[:, :, o_kvn:o_kvn + 768].reshape(L, D, 6, 2, 64)
    kvn_sw = kvn[:, :, :, :, sw64]
    f2 = lambda a: a.reshape(L, D, 128)
    w["w_k"] = np.ascontiguousarray(np.concatenate(
        [f2(kvn[:, :, 0]), f2(kvn[:, :, 2]), f2(kvn[:, :, 4]), f2(kvn[:, :, 1]),
         f2(kvn_sw[:, :, 0]), f2(kvn_sw[:, :, 2]), f2(kvn_sw[:, :, 4])], 2))
    gn = win[:, :, o_gn:o_gn + 24].reshape(L, D, 8, 3).transpose(0, 1, 3, 2).reshape(L, D, 24)
    w["w_v"] = np.ascontiguousarray(np.concatenate(
        [f2(kvn[:, :, 3]), f2(kvn[:, :, 5]), gn, np.zeros((L, D, 8), np.float32)], 2))
    w["w_gm"] = np.ascontiguousarray(win[:, :, o_gm:o_gm + 1024])
    w["w_gn"] = np.ascontiguousarray(win[:, :, o_gnn:o_gnn + 1024])
    wuq = g("w_uq").reshape(L, 256, 8, 96)
    nope = wuq[:, :, :, 0:64].reshape(L, 256, 512)
    rp = wuq[:, :, :, 64:96]
    rp_sw = rp[:, :, :, sw32]
    hsel = [0, 1, 2, 3, 4, 5, 6, 7, 7]
    w["w_uq"] = np.ascontiguousarray(np.concatenate(
        [nope, rp[:, :, hsel].reshape(L, 256, 288), rp_sw[:, :, hsel].reshape(L, 256, 288)], 2))
    wukv = g("w_ukv").reshape(L, 128, 8, 128)
    wuk = wukv[:, :, :, 0:64]
    w["w_ukt"] = np.ascontiguousarray(
        wuk.reshape(L, 128, 4, 2, 64).transpose(0, 3, 4, 2, 1).reshape(L, 128, 4 * 128))
    w["w_uv"] = np.ascontiguousarray(wukv[:, :, :, 64:128].reshape(L, 128, 512))
    for nm, k1 in (("w_c1k", "cmp_k_w1"), ("w_c1v", "cmp_v_w1")):
        a = g(k1).reshape(L, 32, 64, 128).transpose(0, 2, 1, 3)
        w[nm] = np.ascontiguousarray(np.concatenate([a, a], 1).reshape(L, 128, 32 * 128))
    pek = g("cmp_pe_k").transpose(0, 2, 1)
    pev = g("cmp_pe_v").transpose(0, 2, 1)
    w["w_cpe"] = np.ascontiguousarray(np.concatenate(
        [np.concatenate([pek, pek], 1), np.concatenate([pev, pev], 1)], 2))
    w["w_cb"] = np.ascontiguousarray(np.stack([g("cmp_k_b1"), g("cmp_v_b1")], 2))
    w["w_c2"] = np.ascontiguousarray(np.concatenate([g("cmp_k_w2"), g("cmp_v_w2")], 2))
    w["w_pm"] = g("w_proj_mla")
    wpn = g("w_proj_nsa").reshape(L, 8, 64, 1024)
    w["w_pn"] = np.ascontiguousarray(wpn[:, pair].reshape(L, 512, 1024))
    w["w_out"] = g("w_out")
    lnp = np.stack([g("ln_f1_g"), g("ln_f1_b"), g("ln_mix_g"), g("ln_mix_b"), g("ln_f2_g"), g("ln_f2_b")], 1)
    w["lnp"] = np.ascontiguousarray(lnp.reshape(L, 6, 8, 128).transpose(0, 3, 1, 2).reshape(L, 128, 48))
    w["qng"] = np.ascontiguousarray(g("q_norm_g").reshape(L, 2, 128).transpose(0, 2, 1))
    w["kvng"] = np.ascontiguousarray(g("kv_norm_g").reshape(L, 1, 128).transpose(0, 2, 1))
    return w


class Builder:
    def __init__(self, T, nl=DEPTH, stop=0):
        self.stop = stop
        self.T = T
        self.NL = nl
        self.NCH = T // 512
        self.NT = T // 128
        self.NS = T // 64
        self.NCC = T // 16

    def mm(self, out, lhsT, rhs, start, stop, reads, wb):
        self.P.add("pe", lambda e: e.matmul(out, lhsT, rhs, start=start, stop=stop), reads=reads, writes=[wb])

    def tr(self, out, in_, ident, reads, wb):
        self.P.add("pe", lambda e: e.transpose(out, in_, ident), reads=reads, writes=[wb])

    def act(self, out, in_, func, reads, writes, bias=0.0, scale=1.0, accum_out=None):
        if accum_out is None:
            self.P.add("act", lambda e: e.activation(out=out, in_=in_, func=func, bias=bias, scale=scale),
                       reads=reads, writes=writes)
        else:
            self.P.add("act", lambda e: e.activation(out=out, in_=in_, func=func, bias=bias, scale=scale,
                                                     accum_out=accum_out), reads=reads, writes=writes)

    def tt(self, out, in0, in1, op, reads, writes, eng="dve"):
        self.P.add(eng, lambda e: e.tensor_tensor(out=out, in0=in0, in1=in1, op=op), reads=reads, writes=writes)

    def ts(self, out, in0, s1, op0, reads, writes, s2=None, op1=None, eng="dve"):
        if op1 is None:
            self.P.add(eng, lambda e: e.tensor_scalar(out=out, in0=in0, scalar1=s1, scalar2=None, op0=op0),
                       reads=reads, writes=writes)
        else:
            self.P.add(eng, lambda e: e.tensor_scalar(out=out, in0=in0, scalar1=s1, scalar2=s2, op0=op0, op1=op1),
                       reads=reads, writes=writes)

    def stt(self, out, in0, scalar, in1, op0, op1, reads, writes, eng="dve"):
        self.P.add(eng, lambda e: e.scalar_tensor_tensor(out=out, in0=in0, scalar=scalar, in1=in1, op0=op0, op1=op1),
                   reads=reads, writes=writes)

    def cp(self, out, in_, reads, writes, eng="dve"):
        if eng == "act":
            self.P.add("act", lambda e: e.activation(out=out, in_=in_, func=AF.Copy), reads=reads, writes=writes)
        else:
            self.P.add(eng, lambda e: e.tensor_copy(out=out, in_=in_), reads=reads, writes=writes)

    def memset(self, ap, val, writes, eng="dve"):
        self.P.add(eng, lambda e: e.memset(ap, val), writes=writes)

    def recip(self, out, in_, reads, writes):
        self.P.add("dve", lambda e: e.reciprocal(out=out, in_=in_), reads=reads, writes=writes)

    def bank(self, pool):
        lst = self.pools[pool]
        i = self.pool_i[pool]
        self.pool_i[pool] = (i + 1) % len(lst)
        return lst[i]

    def wtile(self, parts):
        if self.P.dry:
            self.wlist.append(parts)
            return None, None
        i = self.wi
        self.wi += 1
        while self.wissued < min(i + 1 + PF, len(self.wlist)):
            j = self.wissued
            st, sb = self.slots[j % NSLOT]
            for (off, a, b, src) in self.wlist[j]:
                nrow = src.shape[0]
                dst = st[0:nrow, off:off + a * b].rearrange("p (a b) -> p a b", a=a)
                self.P.dma("pool", (lambda d, s: (lambda e: e.dma_start(out=d, in_=s)))(dst, src), sb, writes=[sb])
            self.wissued += 1
        return self.slots[i % NSLOT]

    def wview(self, st, off, a, b, rows=128):
        return st[0:rows, off:off + a * b].rearrange("p (a b) -> p a b", a=a)

    def build(self):
        nc = bass.Bass("TRN2", target_bir_lowering=False)
        self.nc = nc
        T, NL = self.T, self.NL
        dt = nc.dram_tensor
        dr = {}
        dr["x"] = dt("x", [T, D], F32, kind="ExternalInput").ap()
        dr["y"] = dt("y", [T, D], F32, kind="ExternalOutput").ap()
        dr["scr"] = dt("scr", [T, D], F32, kind="Internal").ap()
        shapes = {
            "ffn_wg": [2, 2, D, DFF], "ffn_wu": [2, 2, D, DFF], "ffn_wd": [2, 2, DFF, D],
            "w_a": [2, D, 576], "w_q": [2, D, 1024], "w_k": [2, D, 896], "w_v": [2, D, 288],
            "w_gm": [2, D, 1024], "w_gn": [2, D, 1024], "w_uq": [2, 256, 1088], "w_ukt": [2, 128, 512],
            "w_uv": [2, 128, 512], "w_c1k": [2, 128, 4096], "w_c1v": [2, 128, 4096], "w_cpe": [2, 128, 64],
            "w_cb": [2, 128, 2], "w_c2": [2, 128, 128], "w_pm": [2, 512, 1024], "w_pn": [2, 512, 1024],
            "w_out": [2, D, 1024], "lnp": [2, 128, 48], "qng": [2, 128, 2], "kvng": [2, 128, 1],
            "c_identf": [128, 128], "c_bfc": [128, 2048], "c_e": [64, T], "c_rope": [4, 128, T],
            "c_stripc": [128, 512], "c_stripj": [128, 128], "c_selg": [32, 1536],
        }
        for k, s in shapes.items():
            dr[k] = dt(k, s, F32, kind="ExternalInput").ap()
        self.dr = dr
        self.in_names = ["x"] + list(shapes.keys())

        with ExitStack() as es:
            def sb(name, shape, dtype):
                return es.enter_context(nc.sbuf_tensor("s_" + name, shape, dtype))

            self.sb = sb
            banks = []
            for i in range(8):
                t_ = es.enter_context(nc.psum_tensor("ps%d" % i, [128, 512], F32))
                banks.append((t_, Buf("ps%d" % i, x=True)))
            self.pools = {"S": banks[0:3], "A": banks[3:5], "G": banks[5:8]}
            self.pool_i = {"S": 0, "A": 0, "G": 0}
            self.slots = [(sb("wslot%d" % i, [128, SLOT], BF16), Buf("wslot%d" % i)) for i in range(NSLOT)]
            self.alloc_persistent()
            self.wlist = []
            self.P = Prog(nc, dry=True)
            self.program()
            self.P = Prog(nc, dry=False)
            self.wi = 0
            self.wissued = 0
            self.pool_i = {"S": 0, "A": 0, "G": 0}
            self.program()
            finals = [("d", (self.B_out, "hw"), self.P.dma_cnt[(self.B_out, "hw")])]
            self.P.emit(final_waits=finals)
        return nc

    def alloc_persistent(self):
        sb, T = self.sb, self.T
        NT, NCC = self.NT, self.NCC
        self.identf = sb("identf", [128, 128], F32)
        self.bfc = sb("bfc", [128, 2048], BF16)
        self.stripc = sb("stripc", [128, 512], F32)
        self.stripj = sb("stripj", [128, 128], F32)
        self.selg = sb("selg", [32, 1536], F32)
        self.B_const = Buf("const")
        self.lnp = sb("lnp", [128, 48], F32)
        self.qng = sb("qng", [128, 2], F32)
        self.kvng = sb("kvng", [128, 1], F32)
        self.cpe = sb("cpe", [128, 64], BF16)
        self.cb = sb("cb", [128, 2], F32)
        self.c2 = sb("c2", [128, 128], BF16)
        self.ukt = sb("ukt", [128, 512], BF16)
        self.uv = sb("uv", [128, 512], BF16)
        self.pb = sb("pb", [128, 2], F32)
        self.B_lp = Buf("layerparams")
        self.B_pb = Buf("pb")
        self.ckvT = sb("ckvT", [128, T], BF16)
        self.ckvtok = sb("ckvtok", [128, NT, 128], BF16)
        self.kpeT = sb("kpeT", [96, T], BF16)
        self.ksaug = sb("ksaug", [128, 2, T], BF16)
        self.kwT = sb("kwT", [128, T], BF16)
        self.Vs = sb("Vs", [128, NT, 192], BF16)
        self.Vw = sb("Vw", [128, NT, 192], BF16)
        self.kcT = sb("kcT", [128, NCC], BF16)
        self.vc = sb("vc", [128, max(1, NCC // 128), 2, 64], BF16)
        self.kch = sb("kch", [128, 528], BF16)
        self.vch = sb("vch", [128, 528], BF16)
        self.B_cache = [Buf("cache%d" % c) for c in range(self.NCH)]
        self.B_hist = Buf("hist")
        self.B_vc = Buf("vc")
        self.B_kc = Buf("kc")
        self.R1 = sb("R1", [128, 8, 512], F32)
        self.R2 = sb("R2", [128, 8, 512], F32)
        self.R3 = sb("R3", [128, 8, 512], BF16)
        self.R4 = sb("R4", [128, 22 * 512], BF16)
        self.B_R1, self.B_R2, self.B_R3, self.B_R4 = Buf("R1"), Buf("R2"), Buf("R3"), Buf("R4")
        self.B_xtok = Buf("xtok")
        self.B_oc, self.B_osb, self.B_owb, self.B_acc = Buf("oc"), Buf("osb"), Buf("owb"), Buf("acc")
        self.fence_t = sb("fence_t", [128, 2], F32)
        self.pt = [(sb("pt%d" % i, [128, 512], BF16), Buf("pt%d" % i)) for i in range(3)]
        self.pt_i = 0
        self.rope = sb("rope", [128, 4, 512], F32)
        self.B_rope = Buf("rope")
        self.tmpf = [(sb("tmpf%d" % i, [128, 512], F32), Buf("tmpf%d" % i)) for i in range(4)]
        self.tmpf_i = 0
        self.tmpb = [(sb("tmpb%d" % i, [128, 512], BF16), Buf("tmpb%d" % i)) for i in range(3)]
        self.tmpb_i = 0
        r2b = self.R2[:].rearrange("p a f -> p (a f)").bitcast(BF16)
        self.cqn = r2b[:, 0:1024].rearrange("p (a t) -> p a t", a=2)
        self.qnope = r2b[:, 1024:3072].rearrange("p (a t) -> p a t", a=4)
        self.B_cqn = self.B_R2
        self.B_qnope = self.B_R2
        self.qpe = sb("qpe", [96, 3, 512], BF16)
        self.B_qpe = Buf("qpe")
        self.gsig = sb("gsig", [32, 512], F32)
        self.B_gsig = Buf("gsig")
        self.omT = sb("omT", [128, 4, 512], BF16)
        self.onT = sb("onT", [128, 4, 512], BF16)
        self.B_omT, self.B_onT = Buf("omT"), Buf("onT")
        W4 = NCC + 4
        self.sbias = [(self.R2[:, 7, i * 256:(i + 1) * 256], Buf("sbias%d" % i)) for i in range(2)]
        self.pexp = [(self.R2[:, 4, 0:256], self.B_osb), (self.R2[:, 5, 0:256], self.B_owb)]
        self.pnb = [(sb("pnb%d" % i, [128, max(128, NCC)], BF16), Buf("pnb%d" % i)) for i in range(2)]
        self.pTt = [(sb("pTt%d" % i, [128, max(1, NCC // 128), 128], BF16), Buf("pTt%d" % i)) for i in range(2)]
        self.p4 = self.R2[:, 6, 0:W4]
        self.B_p4 = self.B_acc
        self.small = [(sb("small%d" % i, [128, 4], F32), Buf("small%d" % i)) for i in range(4)]
        self.small_i = 0
        self.cmp_i = 0
        self.imp = sb("imp", [128, 64], F32)
        self.imp2 = sb("imp2", [128, 64], F32)
        self.m8 = sb("m8", [128, 16], F32)
        self.nsb = sb("nsb", [128, 64], BF16)
        self.B_imp = Buf("imp")
        self.hidt = sb("hidt", [128, 4, 32], F32)
        self.hidb = sb("hidb", [128, 32], BF16)
        self.hidpad = sb("hidpad", [128, 128], BF16)
        self.B_hid = Buf("hid")
        self.B_out = Buf("out")
        self.B_scr = [Buf("scr%d" % c) for c in range(self.NCH)]

    def tf(self):
        r = self.tmpf[self.tmpf_i]
        self.tmpf_i = (self.tmpf_i + 1) % len(self.tmpf)
        return r

    def tb(self):
        r = self.tmpb[self.tmpb_i]
        self.tmpb_i = (self.tmpb_i + 1) % len(self.tmpb)
        return r

    def ptile(self):
        r = self.pt[self.pt_i]
        self.pt_i = (self.pt_i + 1) % len(self.pt)
        return r

    def sm(self):
        r = self.small[self.small_i]
        self.small_i = (self.small_i + 1) % len(self.small)
        return r

    def program(self):
        P, dr = self.P, self.dr
        self.tmpf_i = self.tmpb_i = self.pt_i = self.small_i = self.cmp_i = 0
        Bc = self.B_const
        P.dma("sp", lambda e: e.dma_start(out=self.identf[:], in_=dr["c_identf"]), Bc, writes=[Bc])
        P.dma("pool", lambda e: e.dma_start(out=self.bfc[:], in_=dr["c_bfc"]), Bc, writes=[Bc])
        import os as _os
        KSKIP = _os.environ.get("KSKIP", "")
        self.KSKIP = KSKIP
        if "c" not in KSKIP:
            P.dma("sp", lambda e: e.dma_start(out=self.stripc[:], in_=dr["c_stripc"]), Bc, writes=[Bc])
            P.dma("sp", lambda e: e.dma_start(out=self.stripj[:], in_=dr["c_stripj"]), Bc, writes=[Bc])
            P.dma("sp", lambda e: e.dma_start(out=self.selg[:], in_=dr["c_selg"]), Bc, writes=[Bc])
        Bc0 = self.B_cache[0]
        if "e" not in KSKIP:
            P.dma("pool", lambda e: e.dma_start(out=self.ksaug[64:128, 0, :], in_=dr["c_e"]), Bc, writes=[Bc])
            P.dma("pool", lambda e: e.dma_start(out=self.ksaug[0:64, 1, :], in_=dr["c_e"]), Bc, writes=[Bc])
        if "v" not in KSKIP:
            self.memset(self.Vs[:, :, 64:128], 1.0, [Bc], eng="pool")
            self.memset(self.Vw[:, :, 64:128], 1.0, [Bc], eng="pool")
        self.identb = self.bfc[:, 0:128]
        self.onesb = self.bfc[:, 128:256]
        self.on1024 = self.bfc[:, 256:384]
        self.on256 = self.bfc[:, 384:512]
        self.on128 = self.bfc[:, 512:640]
        self.LONG = 640
        for l in range(self.NL):
            src = dr["x"] if l == 0 else dr["scr"]
            dst = dr["y"] if l == self.NL - 1 else dr["scr"]
            self.layer(l, src, dst, l == 0, l == self.NL - 1)

    def load_xtok(self, c, src, first_layer):
        P = self.P
        xtok = self.R4.bitcast(F32) if False else None
        xt = self.xtok_view()
        rd = [] if first_layer else [self.B_scr[c]]
        P.dma("sp", lambda e: e.dma_start(out=xt, in_=src[c * 512:(c + 1) * 512, :].rearrange("(a p) d -> p a d", p=128)),
              self.B_xtok, reads=rd, writes=[self.B_R4])

    def xtok_view(self):
        return self.R4[:, 0:8192].bitcast(F32).rearrange("p (a d) -> p a d", a=4)

    def layer(self, l, src, dst, first_layer, last_layer):
        P, dr = self.P, self.dr
        Blp = self.B_lp
        KSKIP = self.KSKIP
        if "l" not in KSKIP:
            P.dma("sp", lambda e: e.dma_start(out=self.lnp[:], in_=dr["lnp"][l]), Blp, writes=[Blp])
            P.dma("sp", lambda e: e.dma_start(out=self.qng[:], in_=dr["qng"][l]), Blp, writes=[Blp])
            P.dma("sp", lambda e: e.dma_start(out=self.kvng[:], in_=dr["kvng"][l]), Blp, writes=[Blp])
            P.dma("sp", lambda e: e.dma_start(out=self.cb[:], in_=dr["w_cb"][l]), Blp, writes=[Blp])
        if "p" not in KSKIP:
            P.dma("pool", lambda e: e.dma_start(out=self.cpe[:], in_=dr["w_cpe"][l]), Blp, writes=[Blp])
            P.dma("pool", lambda e: e.dma_start(out=self.c2[:], in_=dr["w_c2"][l]), Blp, writes=[Blp])
            P.dma("pool", lambda e: e.dma_start(out=self.ukt[:], in_=dr["w_ukt"][l]), Blp, writes=[Blp])
            P.dma("pool", lambda e: e.dma_start(out=self.uv[:], in_=dr["w_uv"][l]), Blp, writes=[Blp])
        if "m" not in KSKIP:
            self.memset(self.vc[:], 0.0, [self.B_vc], eng="pool")
            self.memset(self.kcT[:], 0.0, [self.B_kc], eng="pool")
            self.memset(self.kch[:], 0.0, [self.B_hist], eng="pool")
            self.memset(self.vch[:], 0.0, [self.B_hist], eng="pool")
            self.memset(self.hidpad[:], 0.0, [self.B_hid], eng="pool")
        self.load_xtok(0, src, first_layer)
        for c in range(self.NCH):
            self.chunk(l, c, src, dst, first_layer, last_layer)

    def chunk(self, l, c, src, dst, first_layer, last_layer):
        P = self.P
        R1, R2, R3 = self.R1, self.R2, self.R3
        B1, B2, B3, B4 = self.B_R1, self.B_R2, self.B_R3, self.B_R4
        xt = self.xtok_view()
        for k in range(8):
            pt_, pb_ = self.bank("G")
            for tt_ in range(4):
                self.tr(pt_[:, tt_ * 128:(tt_ + 1) * 128], xt[:, tt_, k * 128:(k + 1) * 128], self.identf[:],
                        [B4, self.B_const], pb_)
            self.act(R1[:, k, :], pt_[:], AF.Identity, [pb_], [B1], scale=ALPHA)
            self.cp(R3[:, k, :], pt_[:], [pb_], [B3])
        import os as _os
        _ks = _os.environ.get("KSTAGE", "")
        if _ks == "A":
            for k in range(8):
                self.cp(R2[:, k, :], R1[:, k, :], [B1], [B2])
        else:
            self.ffn(l, 0)
            if _ks != "F":
                self.layernorm(l, 0)
        if self.stop != 1:
            self.mixer(l, c)
            self.layernorm(l, 1)
        if self.stop == 0:
            self.ffn(l, 1)
        if c + 1 < self.NCH:
            self.load_xtok(c + 1, src, first_layer)
        if self.stop == 0:
            self.layernorm(l, 2, final=True)
        st = R1[:].rearrange("p a f -> p (a f)").rearrange("p (a d) -> p a d", a=4)
        for tt_ in range(4):
            for k2 in range(2):
                pt_, pb_ = self.bank("G")
                for kk in range(4):
                    k = k2 * 4 + kk
                    self.tr(pt_[:, kk * 128:(kk + 1) * 128], R2[:, k, tt_ * 128:(tt_ + 1) * 128], self.identf[:],
                            [B2, self.B_const], pb_)
                if k2 == 0:
                    self.cp(st[:, tt_, 0:512], pt_[:], [pb_], [B1], eng="act")
                else:
                    self.cp(st[:, tt_, 512:1024], pt_[:], [pb_], [B1])
        wb = self.B_out if last_layer else self.B_scr[c]
        P.dma("sp", lambda e: e.dma_start(out=dst[c * 512:(c + 1) * 512, :].rearrange("(a p) d -> p a d", p=128), in_=st),
              self.B_out, reads=[B1], writes=[wb])

    def ffn(self, l, which):
        dr = self.dr
        R1, R2, R3, R4 = self.R1, self.R2, self.R3, self.R4
        B1, B2, B3, B4 = self.B_R1, self.B_R2, self.B_R3, self.B_R4
        hT = R4[:].rearrange("p (j t) -> p j t", j=NFC)
        wg, wu, wd = dr["ffn_wg"][l, which], dr["ffn_wu"][l, which], dr["ffn_wd"][l, which]
        for fb in range(NFC // 2):
            f0 = fb * 256
            st, sbuf_ = self.wtile([
                (0, 8, 256, wg[:, f0:f0 + 256].rearrange("(k p) f -> p k f", p=128)),
                (2048, 8, 256, wu[:, f0:f0 + 256].rearrange("(k p) f -> p k f", p=128)),
            ])
            if st is None:
                continue
            wgv = self.wview(st, 0, 8, 256)
            wuv = self.wview(st, 2048, 8, 256)
            for fc in range(2):
                j = fb * 2 + fc
                pg, bg = self.bank("S")
                pu, bu = self.bank("G")
                for k in range(8):
                    self.mm(pg[:], wgv[:, k, fc * 128:(fc + 1) * 128], R3[:, k, :], k == 0, k == 7, [sbuf_, B3], bg)
                for k in range(8):
                    self.mm(pu[:], wuv[:, k, fc * 128:(fc + 1) * 128], R3[:, k, :], k == 0, k == 7, [sbuf_, B3], bu)
                t_, tb_ = self.tf()
                self.act(t_[:], pg[:], AF.Silu, [bg], [tb_])
                self.tt(hT[:, j, :], t_[:], pu[:], ALU.mult, [tb_, bu], [B4])
        for m in range(8):
            st, sbuf_ = self.wtile([(0, NFC, 128, wd[:, m * 128:(m + 1) * 128].rearrange("(j p) d -> p j d", p=128))])
            if st is None:
                continue
            wdv = self.wview(st, 0, NFC, 128)
            po, bo = self.bank("A")
            for j in range(NFC):
                self.mm(po[:], wdv[:, j, :], hT[:, j, :], j == 0, j == NFC - 1, [sbuf_, B4], bo)
            self.stt(R2[:, m, :], po[:], 0.5, R1[:, m, :], ALU.mult, ALU.add, [bo, B1], [B2])

    def layernorm(self, l, idx, final=False):
        R1, R2, R3, R4 = self.R1, self.R2, self.R3, self.R4
        B1, B2, B3, B4 = self.B_R1, self.B_R2, self.B_R3, self.B_R4
        pm, bm = self.bank("G")
        pq, bq = self.bank("G")
        if not final:
            zsq = R4[:, 0:4096].rearrange("p (k t) -> p k t", k=8)
            for k in range(8):
                self.cp(R3[:, k, :], R2[:, k, :], [B2], [B3])
                self.act(zsq[:, k, :], R2[:, k, :], AF.Square, [B2], [B4])
            for k in range(8):
                self.mm(pm[:], self.on1024, R3[:, k, :], k == 0, k == 7, [B3, self.B_const], bm)
            for k in range(8):
                self.mm(pq[:], self.on1024, zsq[:, k, :], k == 0, k == 7, [B4, self.B_const], bq)
        else:
            for k in range(8):
                self.cp(R3[:, k, :], R2[:, k, :], [B2], [B3])
            for k in range(8):
                self.mm(pm[:], self.on1024, R3[:, k, :], k == 0, k == 7, [B3, self.B_const], bm)
            for k in range(8):
                self.act(R3[:, k, :], R2[:, k, :], AF.Square, [B2], [B3])
            for k in range(8):
                self.mm(pq[:], self.on1024, R3[:, k, :], k == 0, k == 7, [B3, self.B_const], bq)
        mean, bmean = self.tf()
        m2, bm2 = self.tf()
        rstd, brstd = self.tf()
        nmr, bnmr = self.tf()
        self.cp(mean[:], pm[:], [bm], [bmean], eng="act")
        self.tt(m2[:], mean[:], mean[:], ALU.mult, [bmean], [bm2])
        self.tt(m2[:], pq[:], m2[:], ALU.subtract, [bq, bm2], [bm2])
        self.ts(m2[:], m2[:], LN_EPS, ALU.add, [bm2], [bm2])
        self.act(m2[:], m2[:], AF.Sqrt, [bm2], [bm2])
        self.recip(rstd[:], m2[:], [bm2], [brstd])
        self.stt(nmr[:], mean[:], -1.0, rstd[:], ALU.mult, ALU.mult, [bmean, brstd], [bnmr])
        gcol = self.lnp[:, idx * 16:idx * 16 + 8]
        bcol = self.lnp[:, idx * 16 + 8:idx * 16 + 16]
        for k in range(8):
            self.tt(R2[:, k, :], R2[:, k, :], rstd[:], ALU.mult, [B2, brstd], [B2])
            self.tt(R2[:, k, :], R2[:, k, :], nmr[:], ALU.add, [B2, bnmr], [B2])
            self.act(R2[:, k, :], R2[:, k, :], AF.Identity, [B2, self.B_lp], [B2], bias=bcol[:, k:k + 1], scale=gcol[:, k:k + 1])
            if not final:
                self.cp(R3[:, k, :], R2[:, k, :], [B2], [B3])
                self.act(R1[:, k, :], R2[:, k, :], AF.Identity, [B2], [B1], scale=ALPHA)

    def rmsnorm_fm(self, ps_list, ones_ap, gcols, outs, out_reads_writes):
        nk = len(ps_list)
        raws = []
        sqs = []
        for i, (pa, pb_) in enumerate(ps_list):
            r_, rb_ = self.tf()
            self.cp(r_[:], pa, [pb_], [rb_], eng="act")
            s_, sb_ = self.tb()
            self.act(s_[:], pa, AF.Square, [pb_], [sb_])
            raws.append((r_, rb_))
            sqs.append((s_, sb_))
        pss, bss = self.bank("G")
        for i, (s_, sb_) in enumerate(sqs):
            self.mm(pss[:], ones_ap, s_[:], i == 0, i == nk - 1, [sb_, self.B_const], bss)
        rq, brq = self.tf()
        self.act(rq[:], pss[:], AF.Sqrt, [bss], [brq], bias=RMS_EPS)
        self.recip(rq[:], rq[:], [brq], [brq])
        for i, (r_, rb_) in enumerate(raws):
            o_ap, o_w = outs[i]
            self.stt(o_ap, r_[:], gcols[i], rq[:], ALU.mult, ALU.mult, [rb_, brq, self.B_lp], o_w)

    def rope_apply(self, ps_main, b_main, ps_sw, b_sw, ctab, stab, rows, outs):
        t1, b1_ = self.tf()
        t2, b2_ = self.tf()
        self.tt(t1[0:rows, :], ps_main, ctab, ALU.mult, [b_main, self.B_rope], [b1_])
        self.tt(t2[0:rows, :], ps_sw, stab, ALU.mult, [b_sw, self.B_rope], [b2_])
        for (o_ap, r0, r1, wr) in outs:
            self.tt(o_ap, t1[r0:r1, :], t2[r0:r1, :], ALU.add, [b1_, b2_], wr)

    def mixer(self, l, c):
        P, dr = self.P, self.dr
        T = self.T
        R1, R2, R3, R4 = self.R1, self.R2, self.R3, self.R4
        B1, B2, B3, B4 = self.B_R1, self.B_R2, self.B_R3, self.B_R4
        Bc = self.B_cache[c]
        t0 = c * 512
        cs = slice(t0, t0 + 512)
        qabs = R4[:, 0:4096].rearrange("p (h t) -> p h t", h=8)
        qaug = R4[:, 4096:8192].rearrange("p (h t) -> p h t", h=8)
        qnw = None
        P.dma("sp", lambda e: e.dma_start(out=self.rope[:], in_=dr["c_rope"][:, :, t0:t0 + 512].rearrange("a p t -> p a t")),
              self.B_rope, writes=[self.B_rope])
        Cm, Sm, Cn, Sn = (self.rope[:, i, :] for i in range(4))

        st, sw_ = self.wtile([(0, 8, 384, dr["w_a"][l][:, 0:384].rearrange("(k p) f -> p k f", p=128))])
        if st is not None:
            wa = self.wview(st, 0, 8, 384)
            pl = []
            for i in range(2):
                p_, b_ = self.bank("G")
                for k in range(8):
                    self.mm(p_[:], wa[:, k, i * 128:(i + 1) * 128], R3[:, k, :], k == 0, k == 7, [sw_, B3], b_)
                pl.append((p_[:], b_))
            self.rmsnorm_fm(pl, self.on256, [self.qng[:, 0:1], self.qng[:, 1:2]],
                            [(self.cqn[:, 0, :], [self.B_cqn]), (self.cqn[:, 1, :], [self.B_cqn])], None)
            p_, b_ = self.bank("G")
            for k in range(8):
                self.mm(p_[:], wa[:, k, 256:384], R3[:, k, :], k == 0, k == 7, [sw_, B3], b_)
            self.rmsnorm_fm([(p_[:], b_)], self.on128, [self.kvng[:, 0:1]], [(self.ckvT[:, cs], [Bc])], None)
            pt_, pb_ = self.bank("G")
            ptb = pt_[:].bitcast(BF16)
            for tt_ in range(4):
                self.tr(ptb[:, tt_ * 128:(tt_ + 1) * 128], self.ckvT[:, t0 + tt_ * 128:t0 + (tt_ + 1) * 128], self.identb,
                        [Bc, self.B_const], pb_)
            self.cp(self.ckvtok[:, c * 4:(c + 1) * 4, :], ptb[:, 0:512].rearrange("p (a b) -> p a b", a=4), [pb_], [Bc])
        stp, swp_ = self.wtile([(0, 8, 192, dr["w_a"][l][:, 384:576].rearrange("(k p) f -> p k f", p=128))])
        if stp is not None:
            wap = self.wview(stp, 0, 8, 192)
            p1, b1_ = self.bank("G")
            p2, b2_ = self.bank("G")
            for k in range(8):
                self.mm(p1[0:96, :], wap[:, k, 0:96], R3[:, k, :], k == 0, k == 7, [swp_, B3], b1_)
            for k in range(8):
                self.mm(p2[0:96, :], wap[:, k, 96:192], R3[:, k, :], k == 0, k == 7, [swp_, B3], b2_)
            self.rope_apply(p1[0:96, :], b1_, p2[0:96, :], b2_, Cm[0:96, :], Sm[0:96, :], 96,
                            [(self.kpeT[:, cs], 0, 96, [Bc])])

        st, sw_ = self.wtile([(0, 2, 1088, dr["w_uq"][l].rearrange("(k p) f -> p k f", p=128))])
        if st is not None:
            wq = self.wview(st, 0, 2, 1088)
            for i in range(4):
                p_, b_ = self.bank("G")
                for k in range(2):
                    self.mm(p_[:], wq[:, k, i * 128:(i + 1) * 128], self.cqn[:, k, :], k == 0, k == 1, [sw_, self.B_cqn], b_)
                self.cp(self.qnope[:, i, :], p_[:], [b_], [self.B_qnope], eng=("act" if i % 2 else "dve"))
            uktv = self.ukt[:].rearrange("p (i m) -> p i m", i=4)
            for h in range(8):
                i, hh = h // 2, h % 2
                p_, b_ = self.bank("G")
                self.mm(p_[:], uktv[64 * hh:64 * hh + 64, i, :], self.qnope[64 * hh:64 * hh + 64, i, :], True, True,
                        [self.B_lp, self.B_qnope], b_)
                self.cp(qabs[:, h, :], p_[:], [b_], [B4], eng=("act" if h % 2 else "dve"))
            for j in range(3):
                rows = 96 if j < 2 else 64
                p1, b1_ = self.bank("G")
                p2, b2_ = self.bank("G")
                for k in range(2):
                    self.mm(p1[0:rows, :], wq[:, k, 512 + j * 96:512 + j * 96 + rows], self.cqn[:, k, :], k == 0, k == 1,
                            [sw_, self.B_cqn], b1_)
                for k in range(2):
                    self.mm(p2[0:rows, :], wq[:, k, 800 + j * 96:800 + j * 96 + rows], self.cqn[:, k, :], k == 0, k == 1,
                            [sw_, self.B_cqn], b2_)
                self.rope_apply(p1[0:rows, :], b1_, p2[0:rows, :], b2_, Cm[0:rows, :], Sm[0:rows, :], rows,
                                [(self.qpe[0:rows, j, :], 0, rows, [self.B_qpe])])

        for i in range(4):
            st, sw_ = self.wtile([(0, 8, 128, dr["w_q"][l][:, i * 128:(i + 1) * 128].rearrange("(k p) f -> p k f", p=128)),
                                  (1024, 8, 128, dr["w_q"][l][:, 512 + i * 128:512 + (i + 1) * 128].rearrange("(k p) f -> p k f", p=128))])
            if st is None:
                continue
            wq1 = self.wview(st, 0, 8, 128)
            wq2 = self.wview(st, 1024, 8, 128)
            p1, b1_ = self.bank("G")
            p2, b2_ = self.bank("G")
            for k in range(8):
                self.mm(p1[:], wq1[:, k, :], R3[:, k, :], k == 0, k == 7, [sw_, B3], b1_)
            for k in range(8):
                self.mm(p2[:], wq2[:, k, :], R3[:, k, :], k == 0, k == 7, [sw_, B3], b2_)
            self.rope_apply(p1[:], b1_, p2[:], b2_, Cn, Sn, 128,
                            [(qaug[0:64, i, :], 0, 64, [B4]), (qaug[64:128, 4 + i, :], 64, 128, [B4])])
        for bi in range(3):
            st, sw_ = self.wtile([(0, 8, 128, dr["w_k"][l][:, bi * 128:(bi + 1) * 128].rearrange("(k p) f -> p k f", p=128)),
                                  (1024, 8, 128, dr["w_k"][l][:, 512 + bi * 128:512 + (bi + 1) * 128].rearrange("(k p) f -> p k f", p=128))]
                                 + ([(2048, 8, 128, dr["w_k"][l][:, 384:512].rearrange("(k p) f -> p k f", p=128))] if bi == 0 else []))
            if st is None:
                continue
            wk1 = self.wview(st, 0, 8, 128)
            wk2 = self.wview(st, 1024, 8, 128)
            if bi == 0:
                self.cp(self.kch[:, 0:16], self.kch[:, 512:528], [self.B_hist], [self.B_hist])
                self.cp(self.vch[:, 0:16], self.vch[:, 512:528], [self.B_hist], [self.B_hist])
                wvc = self.wview(st, 2048, 8, 128)
                p1, b1_ = self.bank("G")
                for k in range(8):
                    self.mm(p1[:], wvc[:, k, :], R3[:, k, :], k == 0, k == 7, [sw_, B3], b1_)
                self.cp(self.vch[:, 16:528], p1[:], [b1_], [self.B_hist], eng="act")
            p1, b1_ = self.bank("G")
            p2, b2_ = self.bank("G")
            for k in range(8):
                self.mm(p1[:], wk1[:, k, :], R3[:, k, :], k == 0, k == 7, [sw_, B3], b1_)
            for k in range(8):
                self.mm(p2[:], wk2[:, k, :], R3[:, k, :], k == 0, k == 7, [sw_, B3], b2_)
            if bi == 0:
                outs = [(self.kch[:, 16:528], 0, 128, [self.B_hist])]
            elif bi == 1:
                outs = [(self.ksaug[0:64, 0, cs], 0, 64, [Bc]), (self.ksaug[64:128, 1, cs], 64, 128, [Bc])]
            else:
                outs = [(self.kwT[:, cs], 0, 128, [Bc])]
            self.rope_apply(p1[:], b1_, p2[:], b2_, Cn, Sn, 128, outs)
        st, sw_ = self.wtile([(0, 8, 288, dr["w_v"][l].rearrange("(k p) f -> p k f", p=128))])
        if st is not None:
            wv = self.wview(st, 0, 8, 288)
            for tt_ in range(4):
                p_, b_ = self.bank("G")
                for k in range(8):
                    self.mm(p_[:, 0:256], R3[:, k, tt_ * 128:(tt_ + 1) * 128], wv[:, k, 0:256], k == 0, k == 7, [sw_, B3], b_)
                kt = c * 4 + tt_
                vsrc = p_[:, 0:128].rearrange("p (g d) -> p g d", g=2)
                vdst = self.Vs[:, kt, :].rearrange("p (g d) -> p g d", g=3)[:, 0:3:2, :]
                self.cp(vdst, vsrc, [b_], [Bc])
                vsrc2 = p_[:, 128:256].rearrange("p (g d) -> p g d", g=2)
                vdst2 = self.Vw[:, kt, :].rearrange("p (g d) -> p g d", g=3)[:, 0:3:2, :]
                self.cp(vdst2, vsrc2, [b_], [Bc], eng="act")
            p_, b_ = self.bank("G")
            for k in range(8):
                self.mm(p_[0:32, :], wv[:, k, 256:288], R3[:, k, :], k == 0, k == 7, [sw_, B3], b_)
            self.act(self.gsig[:], p_[0:32, :], AF.Sigmoid, [b_], [self.B_gsig])

        cc0 = 32 * c
        for kv in range(2):
            st, sw_ = self.wtile([(0, 32, 128, dr["w_c1k" if kv == 0 else "w_c1v"][l].rearrange("p (i h) -> p i h", i=32))])
            if st is None:
                continue
            w1 = self.wview(st, 0, 32, 128)
            hist = self.kch if kv == 0 else self.vch
            if c == 0:
                p_, b_ = self.bank("G")
                for i in range(32):
                    self.mm(p_[:, 0:1], w1[0:64, i, :], self.cpe[0:64, kv * 32 + i:kv * 32 + i + 1], i == 0, i == 31,
                            [sw_, self.B_lp], b_)
                self.tt(self.pb[:, kv:kv + 1], p_[:, 0:1], self.cb[:, kv:kv + 1], ALU.add, [b_, self.B_lp], [self.B_pb])
            for g in range(2):
                p_, b_ = self.bank("G")
                for i in range(32):
                    self.mm(p_[:, 0:32], w1[64 * g:64 * g + 64, i, :], hist[64 * g:64 * g + 64, i:i + 497:16],
                            i == 0, i == 31, [sw_, self.B_hist], b_)
                ht = self.hidt
                Bh = self.B_hid
                self.act(ht[:, 0, :], p_[:, 0:32], AF.Identity, [b_, self.B_pb], [Bh], bias=self.pb[:, kv:kv + 1])
                self.tt(ht[:, 1, :], ht[:, 0, :], ht[:, 0, :], ALU.mult, [Bh], [Bh])
                self.ts(ht[:, 1, :], ht[:, 1, :], 0.044715, ALU.mult, [Bh], [Bh], s2=1.0, op1=ALU.add)
                self.tt(ht[:, 1, :], ht[:, 1, :], ht[:, 0, :], ALU.mult, [Bh], [Bh])
                self.act(ht[:, 2, :], ht[:, 1, :], AF.Sigmoid, [Bh], [Bh], scale=1.5957691216057308)
                if kv == 0:
                    self.tt(self.hidb[:], ht[:, 0, :], ht[:, 2, :], ALU.mult, [Bh], [Bh])
                    p2, b2_ = self.bank("G")
                    self.mm(p2[0:64, 0:32], self.c2[:, 0:64], self.hidb[:], True, True, [Bh, self.B_lp], b2_)
                    self.cp(self.kcT[64 * g:64 * g + 64, cc0:cc0 + 32], p2[0:64, 0:32], [b2_], [self.B_kc])
                else:
                    po = 32 * (c % 4)
                    self.tt(self.hidpad[:, po:po + 32], ht[:, 0, :], ht[:, 2, :], ALU.mult, [Bh], [Bh])
                    p2, b2_ = self.bank("G")
                    self.mm(p2[:, 0:64], self.hidpad[:], self.c2[:, 64:128], True, True, [Bh, self.B_lp], b2_)
                    self.tt(self.vc[:, c // 4, g, :], self.vc[:, c // 4, g, :], p2[:, 0:64], ALU.add, [b2_, self.B_vc], [self.B_vc])
                    self.memset(self.hidpad[:, po:po + 32], 0.0, [Bh])

        self.memset(self.fence_t[:, 0:1], 0.0, [B2, self.B_oc, self.B_osb, self.B_owb, self.B_acc, self.sbias[0][1], self.sbias[1][1]])
        self.compressed(l, c, qnw, qaug)

        self.attention(l, c, qabs, qaug, qnw)

        self.memset(self.fence_t[:, 1:2], 0.0, [B2, self.B_oc, self.B_osb, self.B_owb, self.B_acc, self.sbias[0][1], self.sbias[1][1]])
        yT = R4[:, 0:4096].rearrange("p (k t) -> p k t", k=8)
        for m in range(8):
            ms = slice(m * 128, (m + 1) * 128)
            st, sw_ = self.wtile([(0, 4, 128, dr["w_pm"][l][:, ms].rearrange("(k p) f -> p k f", p=128)),
                                  (512, 4, 128, dr["w_pn"][l][:, ms].rearrange("(k p) f -> p k f", p=128)),
                                  (1024, 8, 128, dr["w_gm"][l][:, ms].rearrange("(k p) f -> p k f", p=128)),
                                  (2048, 8, 128, dr["w_gn"][l][:, ms].rearrange("(k p) f -> p k f", p=128))])
            if st is None:
                continue
            wpm = self.wview(st, 0, 4, 128)
            wpn = self.wview(st, 512, 4, 128)
            wgm = self.wview(st, 1024, 8, 128)
            wgn = self.wview(st, 2048, 8, 128)
            ppm, bpm = self.bank("G")
            pgm, bgm = self.bank("S")
            ppn, bpn = self.bank("G")
            pgn, bgn = self.bank("S")
            for k in range(4):
                self.mm(ppm[:], wpm[:, k, :], self.omT[:, k, :], k == 0, k == 3, [sw_, self.B_omT], bpm)
            for k in range(8):
                self.mm(pgm[:], wgm[:, k, :], R3[:, k, :], k == 0, k == 7, [sw_, B3], bgm)
            for k in range(4):
                self.mm(ppn[:], wpn[:, k, :], self.onT[:, k, :], k == 0, k == 3, [sw_, self.B_onT], bpn)
            for k in range(8):
                self.mm(pgn[:], wgn[:, k, :], R3[:, k, :], k == 0, k == 7, [sw_, B3], bgn)
            s1, bs1 = self.tf()
            s2, bs2 = self.tf()
            self.act(s1[:], pgm[:], AF.Sigmoid, [bgm], [bs1])
            self.act(s2[:], pgn[:], AF.Sigmoid, [bgn], [bs2])
            self.tt(s1[:], ppm[:], s1[:], ALU.mult, [bs1, bpm], [bs1])
            self.tt(s2[:], ppn[:], s2[:], ALU.mult, [bs2, bpn], [bs2])
            self.tt(yT[:, m, :], s1[:], s2[:], ALU.add, [bs1, bs2], [B4])
        for db in range(2):
            d0 = db * 512
            st, sw_ = self.wtile([(0, 8, 512, dr["w_out"][l][:, d0:d0 + 512].rearrange("(k p) f -> p k f", p=128))])
            if st is None:
                continue
            wo = self.wview(st, 0, 8, 512)
            for mm_ in range(4):
                m = db * 4 + mm_
                p_, b_ = self.bank("A")
                for k in range(8):
                    self.mm(p_[:], wo[:, k, mm_ * 128:(mm_ + 1) * 128], yT[:, k, :], k == 0, k == 7, [sw_, B4], b_)
                self.tt(R2[:, m, :], p_[:], R1[:, m, :], ALU.add, [b_, B1], [B2])

    def compressed(self, l, c, qnw, qaug):
        P = self.P
        R2, B2, B4 = self.R2, self.B_R2, self.B_R4
        NCCc = 32 * (c + 1)
        ntile = (NCCc + 127) // 128
        NS = self.NS
        do_sel = (NS > 16) and (c >= 2)
        oc = R2[:, 0:4, :]
        if (not do_sel) or (self.NCC // 4 < 64):
            for i in range(4):
                self.memset(qaug[64:128, i, :], 0.0, [B4])
                self.memset(qaug[0:64, 4 + i, :], 0.0, [B4])
        for tt_ in range(4):
            qt = c * 4 + tt_
            ts_ = slice(tt_ * 128, (tt_ + 1) * 128)
            x0 = 248 - 8 * qt
            for g in range(2):
                rs_ = slice(64 * g, 64 * g + 64)
                for hh in range(4):
                    i = hh
                    ps_, bs_ = self.bank("S")
                    self.mm(ps_[:, 0:NCCc], qaug[rs_, i + 4 * g, ts_], self.kcT[rs_, 0:NCCc], True, True, [B4, self.B_kc], bs_)
                    k_ = self.cmp_i % 2
                    self.cmp_i += 1
                    sbt, sbb = self.sbias[k_]
                    pet, peb = self.pexp[k_]
                    pnt, pnb_ = self.pnb[k_]
                    pTt, pTb = self.pTt[k_]
                    self.stt(sbt[:, 0:NCCc], ps_[:, 0:NCCc], SC_NSA, self.stripc[:, x0:x0 + NCCc], ALU.mult, ALU.add,
                             [bs_, self.B_const], [sbb])
                    self.memset(sbt[:, 0:1], NEGB, [sbb])
                    sm_, smb = self.sm()
                    self.act(pet[:, 0:NCCc], sbt[:, 0:NCCc], AF.Exp, [sbb], [peb])
                    self.P.add("dve", (lambda o, i_: (lambda e: e.tensor_reduce(out=o, in_=i_, axis=AX.X, op=ALU.add)))(
                        sm_[:, 0:1], pet[:, 0:NCCc]), reads=[peb], writes=[smb])
                    self.ts(sm_[:, 1:2], sm_[:, 0:1], 1e-30, ALU.max, [smb], [smb])
                    self.recip(sm_[:, 2:3], sm_[:, 1:2], [smb], [smb])
                    self.ts(pnt[:, 0:NCCc], pet[:, 0:NCCc], sm_[:, 2:3], ALU.mult, [peb, smb], [pnb_])
                    if do_sel:
                        if hh == 0:
                            self.ts(self.p4[:, 0:NCCc], pet[:, 0:NCCc], sm_[:, 2:3], ALU.mult, [peb, smb], [self.B_p4])
                        else:
                            self.stt(self.p4[:, 0:NCCc], pet[:, 0:NCCc], sm_[:, 2:3], self.p4[:, 0:NCCc], ALU.mult, ALU.add,
                                     [peb, smb, self.B_p4], [self.B_p4])
                    ptp, ptb_ = self.bank("G")
                    ptbf = ptp[:].bitcast(BF16)
                    for ti in range(ntile):
                        w_ = min(128, NCCc - ti * 128)
                        self.tr(ptbf[0:w_, ti * 128:(ti + 1) * 128], pnt[:, ti * 128:ti * 128 + w_], self.identb,
                                [pnb_, self.B_const], ptb_)
                    for ti in range(ntile):
                        w_ = min(128, NCCc - ti * 128)
                        self.cp(pTt[0:w_, ti, :], ptbf[0:w_, ti * 128:(ti + 1) * 128], [ptb_], [pTb],
                                eng=("act" if ti % 2 else "dve"))
                    po_, bo_ = self.bank("A")
                    for ti in range(ntile):
                        w_ = min(128, NCCc - ti * 128)
                        self.mm(po_[0:64, 0:128], self.vc[0:w_, ti, g, :], pTt[0:w_, ti, :], ti == 0, ti == ntile - 1,
                                [self.B_vc, pTb], bo_)
                    self.cp(oc[rs_, i, ts_], po_[0:64, 0:128], [bo_], [self.B_oc])
                if do_sel:
                    Bi = self.B_imp
                    W4 = NCCc + 4
                    nsb_ = NS
                    self.memset(self.p4[:, NCCc:self.NCC + 4], 0.0, [self.B_p4])
                    nj = self.NCC // 4
                    self.P.add("dve", (lambda o, i_: (lambda e: e.tensor_reduce(out=o, in_=i_, axis=AX.X, op=ALU.add)))(
                        self.imp[:, 0:nj], self.p4[:, 0:4 * nj].rearrange("p (j r) -> p j r", r=4)),
                        reads=[self.B_p4], writes=[Bi])
                    self.tt(self.imp[:, 0:nj], self.imp[:, 0:nj], self.p4[:, 4:4 * nj + 4:4], ALU.add, [Bi, self.B_p4], [Bi])
                    xj = 62 - 2 * qt
                    self.tt(self.imp[:, 0:nj], self.imp[:, 0:nj], self.stripj[:, xj:xj + nj], ALU.add, [Bi, self.B_const], [Bi])
                    self.ts(self.imp[:, 0:1], self.imp[:, 0:1], 1e30, ALU.add, [Bi], [Bi])
                    self.P.add("dve", lambda e: e.max(out=self.m8[:, 0:8], in_=self.imp[:, 0:nj]), reads=[Bi], writes=[Bi])
                    self.P.add("dve", lambda e: e.match_replace(out=self.imp2[:, 0:nj], in_to_replace=self.m8[:, 0:8],
                                                                 in_values=self.imp[:, 0:nj], imm_value=-3e38),
                               reads=[Bi], writes=[Bi])
                    self.P.add("dve", lambda e: e.max(out=self.m8[:, 8:16], in_=self.imp2[:, 0:nj]), reads=[Bi], writes=[Bi])
                    self.ts(self.imp2[:, 0:nj], self.imp[:, 0:nj], self.m8[:, 15:16], ALU.is_ge, [Bi], [Bi])
                    self.ts(self.nsb[:, 0:nj], self.imp2[:, 0:nj], -NEGB, ALU.mult, [Bi], [Bi], s2=NEGB, op1=ALU.add)
                    ptp, ptb_ = self.bank("G")
                    ptbf = ptp[:].bitcast(BF16)
                    self.tr(ptbf[0:nj, 0:128], self.nsb[:, 0:nj], self.identb, [Bi, self.B_const], ptb_)
                    for i2 in range(4):
                        if g == 0:
                            self.cp(qaug[64:64 + nj, i2, ts_], ptbf[0:nj, 0:128], [ptb_], [B4])
                        else:
                            self.cp(qaug[0:nj, 4 + i2, ts_], ptbf[0:nj, 0:128], [ptb_], [B4])

    def run_stream(self, jobs, skew=2):
        n = len(jobs)
        pts = [None] * n
        deferred = []
        for j in range(n + skew):
            if j < n:
                ps_, bs_ = self.bank("S")
                jobs[j]["qk"](ps_, bs_)
                pt_, ptb_ = self.ptile()
                self.act(pt_[:], ps_[:], AF.Exp, [bs_], [ptb_], scale=jobs[j]["scale"])
                pts[j] = (pt_, ptb_)
            jj = j - skew
            if jj >= 0:
                jobs[jj]["pv"](*pts[jj])
                if jobs[jj].get("fin"):
                    jobs[jj]["fin"]()
                if jobs[jj].get("fin_pe"):
                    deferred.append((j + 3, jobs[jj]["fin_pe"]))
            while deferred and deferred[0][0] <= j:
                deferred.pop(0)[1]()
        for _, f in deferred:
            f()

    def attention(self, l, c, qabs, qaug, qnw):
        R2, B2, B4 = self.R2, self.B_R2, self.B_R4
        LONG = self.LONG
        nkt = 4 * c + 4
        caches = [self.B_cache[cc] for cc in range(c + 1)]
        uvv = self.uv[:].rearrange("p (h d) -> p h d", h=8)
        selv = self.selg[:].rearrange("p (a m) -> p a m", a=12)
        banks = [b for pool in ("S", "A", "G") for b in self.pools[pool]]

        def bias_ap(d):
            s0 = LONG + (3 - d) * 128
            return self.bfc[:, s0:s0 + 512]

        acc_banks = banks[3:7]
        ai = [0]

        def next_acc():
            b = acc_banks[ai[0] % len(acc_banks)]
            ai[0] += 1
            return b

        g7 = banks[7]
        jobs = []
        for h in range(8):
            i, hh = h // 2, h % 2
            j3, r3 = h // 3, h % 3
            pO, bO = next_acc()
            pS, bS = next_acc()
            for kt in range(nkt):
                def qk(ps_, bs_, kt=kt, h=h, j3=j3, r3=r3):
                    ks = slice(kt * 128, (kt + 1) * 128)
                    diag = kt >= 4 * c
                    self.mm(ps_[:], self.ckvT[:, ks], qabs[:, h, :], True, False, caches + [B4], bs_)
                    self.mm(ps_[:], self.kpeT[32 * r3:32 * r3 + 32, ks], self.qpe[32 * r3:32 * r3 + 32, j3, :], False, not diag,
                            caches + [self.B_qpe], bs_)
                    if diag:
                        self.mm(ps_[:], self.identb, bias_ap(kt - 4 * c), False, True, [self.B_const], bs_)

                def pv(pt_, ptb_, kt=kt, pO=pO, bO=bO, pS=pS, bS=bS):
                    self.mm(pO[:], self.ckvtok[:, kt, :], pt_[:], kt == 0, kt == nkt - 1, caches + [ptb_], bO)
                    self.mm(pS[:], self.onesb, pt_[:], kt == 0, kt == nkt - 1, [self.B_const, ptb_], bS)

                job = {"qk": qk, "pv": pv, "scale": SC_MLA}
                if kt == nkt - 1:
                    hold = {}

                    def fin(pO=pO, bO=bO, pS=pS, bS=bS, hold=hold):
                        rs_, rsb = self.tf()
                        self.cp(rs_[:], pS[:], [bS], [rsb], eng="act")
                        self.recip(rs_[:], rs_[:], [rsb], [rsb])
                        ol, olb = self.tb()
                        self.tt(ol[:], pO[:], rs_[:], ALU.mult, [bO, rsb], [olb])
                        hold["ol"] = (ol, olb)

                    def fin_pe(h=h, i=i, hh=hh, hold=hold):
                        ol, olb = hold["ol"]
                        p_, b_ = g7
                        self.mm(p_[0:64, :], uvv[:, h, :], ol[:], True, True, [self.B_lp, olb], b_)
                        self.cp(self.omT[64 * hh:64 * hh + 64, i, :], p_[0:64, :], [b_], [self.B_omT])

                    job["fin"] = fin
                    job["fin_pe"] = fin_pe
                jobs.append(job)
        self.run_stream(jobs)

        acc_banks = banks[3:6]
        ai[0] = 0
        gsel = [banks[6], banks[7]]
        gi = [0]
        kt0 = max(0, 4 * c - 4)
        osb = R2[:, 4, :]
        owb = R2[:, 5, :]
        acc = R2[:, 6, :]
        jobs = []
        for i in range(4):
            for g in range(2):
                h = i + 4 * g
                osl = slice(64 * g, 64 * g + 64)
                sml = slice(64 * (1 - g), 64 * (1 - g) + 64)
                vcol = slice(0, 128) if g == 0 else slice(64, 192)
                for br in range(2):
                    pO, bO = next_acc()
                    kts = list(range(nkt)) if br == 0 else list(range(kt0, nkt))
                    for kt in kts:
                        first, last = kt == kts[0], kt == kts[-1]
                        if br == 0:
                            def qk(ps_, bs_, kt=kt, g=g, h=h):
                                ks = slice(kt * 128, (kt + 1) * 128)
                                diag = kt >= 4 * c
                                self.mm(ps_[:], self.ksaug[:, g, ks], qaug[:, h, :], True, not diag, caches + [B4, self.B_const], bs_)
                                if diag:
                                    self.mm(ps_[:], self.identb, bias_ap(kt - 4 * c), False, True, [self.B_const], bs_)

                            def pv(pt_, ptb_, kt=kt, pO=pO, bO=bO, vcol=vcol, first=first, last=last):
                                self.mm(pO[:], self.Vs[:, kt, vcol], pt_[:], first, last, caches + [ptb_, self.B_const], bO)
                        else:
                            def qk(ps_, bs_, kt=kt, osl=osl, h=h):
                                ks = slice(kt * 128, (kt + 1) * 128)
                                self.mm(ps_[:], self.kwT[osl, ks], qaug[osl, h, :], True, False, caches + [B4], bs_)
                                p_ = kt - 4 * c + 4
                                s0 = LONG + (7 - p_) * 128
                                self.mm(ps_[:], self.identb, self.bfc[:, s0:s0 + 512], False, True, [self.B_const], bs_)

                            def pv(pt_, ptb_, kt=kt, pO=pO, bO=bO, vcol=vcol, first=first, last=last):
                                self.mm(pO[:], self.Vw[:, kt, vcol], pt_[:], first, last, caches + [ptb_, self.B_const], bO)
                        job = {"qk": qk, "pv": pv, "scale": SC_NSA}
                        if last:
                            dstb = osb if br == 0 else owb
                            dstB = self.B_osb if br == 0 else self.B_owb

                            def fin(pO=pO, bO=bO, osl=osl, sml=sml, dstb=dstb, dstB=dstB, last_of_pair=(g == 1 and br == 1), i=i):
                                rs_, rsb = self.tf()
                                self.cp(rs_[sml, :], pO[sml, :], [bO], [rsb], eng="act")
                                self.recip(rs_[sml, :], rs_[sml, :], [rsb], [rsb])
                                self.tt(dstb[osl, :], pO[osl, :], rs_[sml, :], ALU.mult, [bO, rsb], [dstB])
                                if last_of_pair:
                                    for br_ in range(3):
                                        pg_, bg_ = gsel[gi[0] % 2]
                                        gi[0] += 1
                                        self.mm(pg_[:], selv[:, br_ * 4 + i, :], self.gsig[:], True, True, [self.B_const, self.B_gsig], bg_)
                                        if br_ == 0:
                                            self.tt(acc, pg_[:], R2[:, i, :], ALU.mult, [self.B_oc, bg_], [self.B_acc])
                                        elif br_ == 1:
                                            self.tt(osb, pg_[:], osb, ALU.mult, [self.B_osb, bg_], [self.B_osb])
                                            self.tt(acc, acc, osb, ALU.add, [self.B_acc, self.B_osb], [self.B_acc])
                                        else:
                                            self.tt(owb, pg_[:], owb, ALU.mult, [self.B_owb, bg_], [self.B_owb])
                                            self.tt(self.onT[:, i, :], acc, owb, ALU.add, [self.B_acc, self.B_owb], [self.B_onT])

                            job["fin"] = fin
                        jobs.append(job)
        self.run_stream(jobs)


_CACHE = {}


def kernel(**inputs):
    x = np.asarray(inputs["x"], dtype=np.float32)
    Bn, T, _ = x.shape
    w = prep_weights(inputs)
    consts = make_consts(T)
    key = (T,)
    if key not in _CACHE:
        b = Builder(T)
        nc = b.build()
        _CACHE[key] = (b, nc)
    b, nc = _CACHE[key]
    shared = {}
    shared.update(w)
    shared.update(consts)
    in_maps = []
    for i in range(Bn):
        m = dict(shared)
        m["x"] = np.ascontiguousarray(x[i])
        in_maps.append(m)
    res = run_bass_kernel_spmd(nc, in_maps, core_ids=list(range(Bn)))
    return np.stack([np.asarray(r["y"], dtype=np.float32) for r in res.results], 0)
```

```python
from contextlib import ExitStack
import numpy as np
import concourse.bass as bass
import concourse.mybir as mybir
from concourse.bass_utils import run_bass_kernel_spmd

F32 = mybir.dt.float32
BF16 = mybir.dt.bfloat16
AF = mybir.ActivationFunctionType
ALU = mybir.AluOpType
AX = mybir.AxisListType

D = 1024
DFF = 2816
NFC = DFF // 128
DEPTH = 2
ALPHA = (2 * DEPTH) ** 0.25
LN_EPS = 1e-5
RMS_EPS = 1e-6
NEGB = -30000.0
SC_MLA = 96 ** -0.5
SC_NSA = 0.125
THETA = 500000.0
SLOT = 4096
NSLOT = 2
PF = 1

ENGS = ("pe", "act", "dve", "pool", "sp")


class Buf:
    __slots__ = ("name", "w", "r", "dsem", "dcnt", "x")

    def __init__(self, name, x=False):
        self.name = name
        self.x = x
        self.w = None
        self.r = {}
        self.dsem = None
        self.dcnt = 0


class Prog:
    def __init__(self, nc, dry=False):
        self.nc = nc
        self.dry = dry
        self.ops = {e: [] for e in ENGS}
        self.dma_cnt = {}
        self.dma_keys = []
        self.dma_sem = {}

    def _deps(self, eng, reads, writes):
        deps = set()
        for b in reads:
            if b.w is not None:
                deps.add(b.w)
            if b.x:
                for k_, h_ in b.r.items():
                    if k_ != eng:
                        deps.add(h_)
        for b in writes:
            if b.w is not None:
                deps.add(b.w)
            for h in b.r.values():
                deps.add(h)
        if eng == "pe":
            deps = {h for h in deps if not (h[0] == "e" and h[1] == "pe")}
        return deps

    def add(self, eng, fn, reads=(), writes=()):
        if self.dry:
            return None
        deps = self._deps(eng, reads, writes)
        idx = len(self.ops[eng])
        h = ("e", eng, idx)
        self.ops[eng].append([fn, deps, h, False])
        for b in writes:
            b.w = h
            b.r = {}
        for b in reads:
            b.r[eng] = h
        return h

    def dma(self, eng, fn, sem_buf, reads=(), writes=()):
        if self.dry:
            return None
        deps = self._deps(eng, reads, writes)
        kind = "sw" if eng == "pool" else "hw"
        key = (sem_buf, kind)
        if key not in self.dma_cnt:
            self.dma_cnt[key] = 0
            self.dma_keys.append(key)
        self.dma_cnt[key] += 1
        h = ("d", key, self.dma_cnt[key])
        self.ops[eng].append([fn, deps, h, False])
        for b in writes:
            b.w = h
            b.r = {}
        for b in reads:
            b.r[("d", id(sem_buf), kind)] = h
        return h

    def emit(self, final_waits=()):
        nc = self.nc
        for e in ENGS:
            for op in self.ops[e]:
                for h in op[1]:
                    if h[0] == "e":
                        self.ops[h[1]][h[2]][3] = True
        cum = {}
        for e in ENGS:
            c = 0
            arr = []
            for op in self.ops[e]:
                if op[3]:
                    c += 1
                arr.append(c)
            cum[e] = arr
        with ExitStack() as es:
            esem = {e: es.enter_context(nc.semaphore("s_" + e)) for e in ENGS}
            for i, key in enumerate(self.dma_keys):
                self.dma_sem[key] = es.enter_context(nc.semaphore("d%d" % i))
            block = es.enter_context(nc.Block())

            def run(e, engobj, extra=()):
                waited_e = {}
                waited_d = {}

                def do_wait(h):
                    if h[0] == "e":
                        v = cum[h[1]][h[2]]
                        if waited_e.get(h[1], 0) >= v:
                            return
                        waited_e[h[1]] = v
                        engobj.wait_ge(esem[h[1]], v)
                    else:
                        key = h[1]
                        v = 16 * h[2]
                        if waited_d.get(key, 0) >= v:
                            return
                        waited_d[key] = v
                        engobj.wait_ge(self.dma_sem[key], v)

                for fn, deps, h, need in self.ops[e]:
                    for d in deps:
                        do_wait(d)
                    ins = fn(engobj)
                    if h[0] == "d":
                        ins.then_inc(self.dma_sem[h[1]], 16)
                    elif need:
                        ins.then_inc(esem[e], 1)
                for hh in extra:
                    do_wait(hh)

            @block.tensor
            def _(eng):
                run("pe", eng)

            @block.scalar
            def _(eng):
                run("act", eng)

            @block.vector
            def _(eng):
                run("dve", eng)

            @block.gpsimd
            def _(eng):
                run("pool", eng)

            @block.sync
            def _(eng):
                run("sp", eng, extra=final_waits)


def make_consts(T):
    c = {}
    c["c_identf"] = np.eye(128, dtype=np.float32)
    bfc = np.zeros((128, 5 * 128 + 11 * 128), np.float32)
    bfc[:, 0:128] = np.eye(128)
    bfc[:, 128:256] = 1.0
    bfc[:, 256:384] = 1.0 / 1024
    bfc[:, 384:512] = 1.0 / 256
    bfc[:, 512:640] = 1.0 / 128
    p = np.arange(128)[:, None]
    f = np.arange(128)[None, :]
    for b in range(11):
        r = 3 - b
        if r > 0 or r < -4:
            blk = np.full((128, 128), NEGB, np.float32)
        elif r == 0:
            blk = np.where(p <= f, 0.0, NEGB).astype(np.float32)
        elif r == -4:
            blk = np.where(p > f, 0.0, NEGB).astype(np.float32)
        else:
            blk = np.zeros((128, 128), np.float32)
        bfc[:, 640 + b * 128: 640 + (b + 1) * 128] = blk
    c["c_bfc"] = bfc
    E = (np.arange(64)[:, None] == (np.arange(T)[None, :] // 64)).astype(np.float32)
    c["c_e"] = E
    t = np.arange(T, dtype=np.float32)
    inv_m = (1.0 / (THETA ** (np.arange(0, 32, 2, dtype=np.float32) / 32))).astype(np.float32)
    ang_m = t[None, :] * inv_m[:, None]
    cm = np.cos(ang_m).astype(np.float32)
    sm = np.sin(ang_m).astype(np.float32)
    Cm = np.concatenate([cm, cm], 0)
    Sm = np.concatenate([-sm, sm], 0)
    inv_n = (1.0 / (THETA ** (np.arange(0, 16, 2, dtype=np.float32) / 16))).astype(np.float32)
    ang_n = t[None, :] * inv_n[:, None]
    cn = np.cos(ang_n).astype(np.float32)
    sn = np.sin(ang_n).astype(np.float32)
    Cn = np.concatenate([cn, cn, np.ones((48, T), np.float32)], 0)
    Sn = np.concatenate([-sn, sn, np.zeros((48, T), np.float32)], 0)
    rope = np.zeros((4, 128, T), np.float32)
    rope[0, 0:96] = np.tile(Cm, (3, 1))
    rope[1, 0:96] = np.tile(Sm, (3, 1))
    rope[2] = np.tile(Cn, (2, 1))
    rope[3] = np.tile(Sn, (2, 1))
    c["c_rope"] = rope
    m = np.floor((np.arange(128) - 15) / 16.0)[:, None]
    x = np.arange(512)[None, :] - 248
    c["c_stripc"] = np.where(x <= m, 0.0, NEGB).astype(np.float32)
    hp = (np.arange(128) >= 64).astype(np.int64)[:, None]
    rel = np.arange(128)[None, :] - 62
    sj = np.zeros((128, 128), np.float32)
    sj = np.where(rel == hp, 2e30, sj)
    sj = np.where(rel == hp - 1, 4e30, sj)
    sj = np.where(rel > hp, -1e30, sj)
    c["c_stripj"] = sj.astype(np.float32)
    sel = np.zeros((32, 12, 128), np.float32)
    for br in range(3):
        for i in range(4):
            sel[br * 8 + i, br * 4 + i, 0:64] = 1.0
            sel[br * 8 + 4 + i, br * 4 + i, 64:128] = 1.0
    c["c_selg"] = sel.reshape(32, 12 * 128)
    return c


def prep_weights(inp):
    L = DEPTH
    w = {}
    g = lambda k: np.asarray(inp[k], dtype=np.float32)
    w["ffn_wg"] = np.ascontiguousarray(np.stack([g("ffn1_wg"), g("ffn2_wg")], 1))
    w["ffn_wu"] = np.ascontiguousarray(np.stack([g("ffn1_wu"), g("ffn2_wu")], 1))
    w["ffn_wd"] = np.ascontiguousarray(np.stack([g("ffn1_wd"), g("ffn2_wd")], 1))
    win = g("w_in")
    o_cq, o_ckv, o_kpe, o_qn, o_kvn, o_gn, o_gm, o_gnn = 0, 256, 384, 416, 928, 1696, 1720, 2744
    sw32 = np.concatenate([np.arange(16, 32), np.arange(0, 16)])
    kpe = win[:, :, o_kpe:o_kpe + 32]
    kpe_sw = kpe[:, :, sw32]
    w["w_a"] = np.ascontiguousarray(np.concatenate(
        [win[:, :, o_cq:o_cq + 384], np.tile(kpe, (1, 1, 3)), np.tile(kpe_sw, (1, 1, 3))], 2))
    sw64 = np.concatenate([np.arange(8, 16), np.arange(0, 8), np.arange(16, 64)])
    qn = win[:, :, o_qn:o_qn + 512].reshape(L, D, 8, 64)
    qn_sw = qn[:, :, :, sw64]
    pair = [0, 4, 1, 5, 2, 6, 3, 7]
    w["w_q"] = np.ascontiguousarray(np.concatenate(
        [qn[:, :, pair].reshape(L, D, 512), qn_sw[:, :, pair].reshape(L, D, 512)], 2))
    kvn = win[:, :, o_kvn:o_kvn + 768].reshape(L, D, 6, 2, 64)
    kvn_sw = kvn[:, :, :, :, sw64]
    f2 = lambda a: a.reshape(L, D, 128)
    w["w_k"] = np.ascontiguousarray(np.concatenate(
        [f2(kvn[:, :, 0]), f2(kvn[:, :, 2]), f2(kvn[:, :, 4]), f2(kvn[:, :, 1]),
         f2(kvn_sw[:, :, 0]), f2(kvn_sw[:, :, 2]), f2(kvn_sw[:, :, 4])], 2))
    gn = win[:, :, o_gn:o_gn + 24].reshape(L, D, 8, 3).transpose(0, 1, 3, 2).reshape(L, D, 24)
    w["w_v"] = np.ascontiguousarray(np.concatenate(
        [f2(kvn[:, :, 3]), f2(kvn[:, :, 5]), gn, np.zeros((L, D, 8), np.float32)], 2))
    w["w_gm"] = np.ascontiguousarray(win[:, :, o_gm:o_gm + 1024])
    w["w_gn"] = np.ascontiguousarray(win[:, :, o_gnn:o_gnn + 1024])
    wuq = g("w_uq").reshape(L, 256, 8, 96)
    nope = wuq[:, :, :, 0:64].reshape(L, 256, 512)
    rp = wuq[:, :, :, 64:96]
    rp_sw = rp[:, :, :, sw32]
    hsel = [0, 1, 2, 3, 4, 5, 6, 7, 7]
    w["w_uq"] = np.ascontiguousarray(np.concatenate(
        [nope, rp[:, :, hsel].reshape(L, 256, 288), rp_sw[:, :, hsel].reshape(L, 256, 288)], 2))
    wukv = g("w_ukv").reshape(L, 128, 8, 128)
    wuk = wukv[:, :, :, 0:64]
    w["w_ukt"] = np.ascontiguousarray(
        wuk.reshape(L, 128, 4, 2, 64).transpose(0, 3, 4, 2, 1).reshape(L, 128, 4 * 128))
    w["w_uv"] = np.ascontiguousarray(wukv[:, :, :, 64:128].reshape(L, 128, 512))
    for nm, k1 in (("w_c1k", "cmp_k_w1"), ("w_c1v", "cmp_v_w1")):
        a = g(k1).reshape(L, 32, 64, 128).transpose(0, 2, 1, 3)
        w[nm] = np.ascontiguousarray(np.concatenate([a, a], 1).reshape(L, 128, 32 * 128))
    pek = g("cmp_pe_k").transpose(0, 2, 1)
    pev = g("cmp_pe_v").transpose(0, 2, 1)
    w["w_cpe"] = np.ascontiguousarray(np.concatenate(
        [np.concatenate([pek, pek], 1), np.concatenate([pev, pev], 1)], 2))
    w["w_cb"] = np.ascontiguousarray(np.stack([g("cmp_k_b1"), g("cmp_v_b1")], 2))
    w["w_c2"] = np.ascontiguousarray(np.concatenate([g("cmp_k_w2"), g("cmp_v_w2")], 2))
    w["w_pm"] = g("w_proj_mla")
    wpn = g("w_proj_nsa").reshape(L, 8, 64, 1024)
    w["w_pn"] = np.ascontiguousarray(wpn[:, pair].reshape(L, 512, 1024))
    w["w_out"] = g("w_out")
    lnp = np.stack([g("ln_f1_g"), g("ln_f1_b"), g("ln_mix_g"), g("ln_mix_b"), g("ln_f2_g"), g("ln_f2_b")], 1)
    w["lnp"] = np.ascontiguousarray(lnp.reshape(L, 6, 8, 128).transpose(0, 3, 1, 2).reshape(L, 128, 48))
    w["qng"] = np.ascontiguousarray(g("q_norm_g").reshape(L, 2, 128).transpose(0, 2, 1))
    w["kvng"] = np.ascontiguousarray(g("kv_norm_g").reshape(L, 1, 128).transpose(0, 2, 1))
    return w


class Builder:
    def __init__(self, T, nl=DEPTH, stop=0):
        self.stop = stop
        self.T = T
        self.NL = nl
        self.NCH = T // 512
        self.NT = T // 128
        self.NS = T // 64
        self.NCC = T // 16

    def mm(self, out, lhsT, rhs, start, stop, reads, wb):
        self.P.add("pe", lambda e: e.matmul(out, lhsT, rhs, start=start, stop=stop), reads=reads, writes=[wb])

    def tr(self, out, in_, ident, reads, wb):
        self.P.add("pe", lambda e: e.transpose(out, in_, ident), reads=reads, writes=[wb])

    def act(self, out, in_, func, reads, writes, bias=0.0, scale=1.0, accum_out=None):
        if accum_out is None:
            self.P.add("act", lambda e: e.activation(out=out, in_=in_, func=func, bias=bias, scale=scale),
                       reads=reads, writes=writes)
        else:
            self.P.add("act", lambda e: e.activation(out=out, in_=in_, func=func, bias=bias, scale=scale,
                                                     accum_out=accum_out), reads=reads, writes=writes)

    def tt(self, out, in0, in1, op, reads, writes, eng="dve"):
        self.P.add(eng, lambda e: e.tensor_tensor(out=out, in0=in0, in1=in1, op=op), reads=reads, writes=writes)

    def ts(self, out, in0, s1, op0, reads, writes, s2=None, op1=None, eng="dve"):
        if op1 is None:
            self.P.add(eng, lambda e: e.tensor_scalar(out=out, in0=in0, scalar1=s1, scalar2=None, op0=op0),
                       reads=reads, writes=writes)
        else:
            self.P.add(eng, lambda e: e.tensor_scalar(out=out, in0=in0, scalar1=s1, scalar2=s2, op0=op0, op1=op1),
                       reads=reads, writes=writes)

    def stt(self, out, in0, scalar, in1, op0, op1, reads, writes, eng="dve"):
        self.P.add(eng, lambda e: e.scalar_tensor_tensor(out=out, in0=in0, scalar=scalar, in1=in1, op0=op0, op1=op1),
                   reads=reads, writes=writes)

    def cp(self, out, in_, reads, writes, eng="dve"):
        if eng == "act":
            self.P.add("act", lambda e: e.activation(out=out, in_=in_, func=AF.Copy), reads=reads, writes=writes)
        else:
            self.P.add(eng, lambda e: e.tensor_copy(out=out, in_=in_), reads=reads, writes=writes)

    def memset(self, ap, val, writes, eng="dve"):
        self.P.add(eng, lambda e: e.memset(ap, val), writes=writes)

    def recip(self, out, in_, reads, writes):
        self.P.add("dve", lambda e: e.reciprocal(out=out, in_=in_), reads=reads, writes=writes)

    def bank(self, pool):
        lst = self.pools[pool]
        i = self.pool_i[pool]
        self.pool_i[pool] = (i + 1) % len(lst)
        return lst[i]

    def wtile(self, parts):
        if self.P.dry:
            self.wlist.append(parts)
            return None, None
        i = self.wi
        self.wi += 1
        while self.wissued < min(i + 1 + PF, len(self.wlist)):
            j = self.wissued
            st, sb = self.slots[j % NSLOT]
            lj = j // (self.NCH * self.TPL)
            tj = j % self.TPL
            used = max(off + a * b for (off, a, b, src) in self.wlist[j])
            srcap = self.w16[lj * self.TPL + tj, :, 0:used]
            self.P.dma("sp", (lambda d, s_: (lambda e: e.dma_start(out=d, in_=s_)))(st[:, 0:used], srcap), sb,
                       reads=[self.B_w16[lj][tj]], writes=[sb])
            self.wissued += 1
        return self.slots[i % NSLOT]

    def convert_weights(self):
        for l in range(self.NL):
            for t in range(self.TPL):
                parts = self.wlist[l * self.NCH * self.TPL + t]
                for pi, (off, a, b, src) in enumerate(parts):
                    dst = self.w16[l * self.TPL + t, :, off:off + a * b].rearrange("p (a b) -> p a b", a=a)
                    wr = [self.B_w16[l][t]] if pi == len(parts) - 1 else []
                    rd = [self.B_w16[l - 1][t]] if (l > 0 and pi == 0) else []
                    self.P.dma("pool", (lambda d, s_: (lambda e: e.dma_start(out=d, in_=s_)))(dst, src),
                               self.B_convsem[t], reads=rd, writes=wr)

    def wview(self, st, off, a, b, rows=128):
        return st[0:rows, off:off + a * b].rearrange("p (a b) -> p a b", a=a)

    def build(self):
        nc = bass.Bass("TRN2", target_bir_lowering=False)
        self.nc = nc
        T, NL = self.T, self.NL
        dt = nc.dram_tensor
        dr = {}
        dr["x"] = dt("x", [T, D], F32, kind="ExternalInput").ap()
        dr["y"] = dt("y", [T, D], F32, kind="ExternalOutput").ap()
        dr["scr"] = dt("scr", [T, D], F32, kind="Internal").ap()
        shapes = {
            "ffn_wg": [2, 2, D, DFF], "ffn_wu": [2, 2, D, DFF], "ffn_wd": [2, 2, DFF, D],
            "w_a": [2, D, 576], "w_q": [2, D, 1024], "w_k": [2, D, 896], "w_v": [2, D, 288],
            "w_gm": [2, D, 1024], "w_gn": [2, D, 1024], "w_uq": [2, 256, 1088], "w_ukt": [2, 128, 512],
            "w_uv": [2, 128, 512], "w_c1k": [2, 128, 4096], "w_c1v": [2, 128, 4096], "w_cpe": [2, 128, 64],
            "w_cb": [2, 128, 2], "w_c2": [2, 128, 128], "w_pm": [2, 512, 1024], "w_pn": [2, 512, 1024],
            "w_out": [2, D, 1024], "lnp": [2, 128, 48], "qng": [2, 128, 2], "kvng": [2, 128, 1],
            "c_identf": [128, 128], "c_bfc": [128, 2048], "c_e": [64, T], "c_rope": [4, 128, T],
            "c_stripc": [128, 512], "c_stripj": [128, 128], "c_selg": [32, 1536],
        }
        for k, s in shapes.items():
            dr[k] = dt(k, s, F32, kind="ExternalInput").ap()
        self.dr = dr
        self.in_names = ["x"] + list(shapes.keys())

        with ExitStack() as es:
            def sb(name, shape, dtype):
                return es.enter_context(nc.sbuf_tensor("s_" + name, shape, dtype))

            self.sb = sb
            banks = []
            for i in range(8):
                t_ = es.enter_context(nc.psum_tensor("ps%d" % i, [128, 512], F32))
                banks.append((t_, Buf("ps%d" % i, x=True)))
            self.pools = {"S": banks[0:3], "A": banks[3:5], "G": banks[5:8]}
            self.pool_i = {"S": 0, "A": 0, "G": 0}
            self.slots = [(sb("wslot%d" % i, [128, SLOT], BF16), Buf("wslot%d" % i)) for i in range(NSLOT)]
            self.alloc_persistent()
            self.wlist = []
            self.P = Prog(nc, dry=True)
            self.program()
            self.TPL = len(self.wlist) // (self.NL * self.NCH)
            assert self.TPL * self.NL * self.NCH == len(self.wlist)
            self.w16 = nc.dram_tensor("w16", [self.NL * self.TPL, 128, SLOT], BF16, kind="Internal").ap()
            self.B_w16 = [[Buf("w16_%d_%d" % (l, t)) for t in range(self.TPL)] for l in range(self.NL)]
            self.B_convsem = [Buf("cv%d" % t) for t in range(self.TPL)]
            self.P = Prog(nc, dry=False)
            self.wi = 0
            self.wissued = 0
            self.pool_i = {"S": 0, "A": 0, "G": 0}
            self.program()
            finals = [("d", (self.B_out, "hw"), self.P.dma_cnt[(self.B_out, "hw")])]
            self.P.emit(final_waits=finals)
        return nc

    def alloc_persistent(self):
        sb, T = self.sb, self.T
        NT, NCC = self.NT, self.NCC
        self.identf = sb("identf", [128, 128], F32)
        self.bfc = sb("bfc", [128, 2048], BF16)
        self.stripc = sb("stripc", [128, 512], F32)
        self.stripj = sb("stripj", [128, 128], F32)
        self.selg = sb("selg", [32, 1536], F32)
        self.B_const = Buf("const")
        self.lnp = sb("lnp", [128, 48], F32)
        self.qng = sb("qng", [128, 2], F32)
        self.kvng = sb("kvng", [128, 1], F32)
        self.cpe = sb("cpe", [128, 64], BF16)
        self.cb = sb("cb", [128, 2], F32)
        self.c2 = sb("c2", [128, 128], BF16)
        self.ukt = sb("ukt", [128, 512], BF16)
        self.uv = sb("uv", [128, 512], BF16)
        self.pb = sb("pb", [128, 2], F32)
        self.B_lp = Buf("layerparams")
        self.B_pb = Buf("pb")
        self.ckvT = sb("ckvT", [128, T], BF16)
        self.ckvtok = sb("ckvtok", [128, NT, 128], BF16)
        self.kpeT = sb("kpeT", [96, T], BF16)
        self.ksaug = sb("ksaug", [128, 2, T], BF16)
        self.kwT = sb("kwT", [128, T], BF16)
        self.Vs = sb("Vs", [128, NT, 192], BF16)
        self.Vw = sb("Vw", [128, NT, 192], BF16)
        self.kcT = sb("kcT", [128, NCC], BF16)
        self.vc = sb("vc", [128, max(1, NCC // 128), 2, 64], BF16)
        self.kch = sb("kch", [128, 528], BF16)
        self.vch = sb("vch", [128, 528], BF16)
        self.B_cache = [Buf("cache%d" % c) for c in range(self.NCH)]
        self.B_hist = Buf("hist")
        self.B_vc = Buf("vc")
        self.B_kc = Buf("kc")
        self.R1 = sb("R1", [128, 8, 512], F32)
        self.R2 = sb("R2", [128, 8, 512], F32)
        self.R3 = sb("R3", [128, 8, 512], BF16)
        self.R4 = sb("R4", [128, 22 * 512], BF16)
        self.B_R1, self.B_R2, self.B_R3, self.B_R4 = Buf("R1"), Buf("R2"), Buf("R3"), Buf("R4")
        self.B_xtok = Buf("xtok")
        self.B_oc, self.B_osb, self.B_owb, self.B_acc = Buf("oc"), Buf("osb"), Buf("owb"), Buf("acc")
        self.fence_t = sb("fence_t", [128, 2], F32)
        self.pt = [(sb("pt%d" % i, [128, 512], BF16), Buf("pt%d" % i)) for i in range(3)]
        self.pt_i = 0
        self.rope = sb("rope", [128, 4, 512], F32)
        self.B_rope = Buf("rope")
        self.tmpf = [(sb("tmpf%d" % i, [128, 512], F32), Buf("tmpf%d" % i)) for i in range(4)]
        self.tmpf_i = 0
        self.tmpb = [(sb("tmpb%d" % i, [128, 512], BF16), Buf("tmpb%d" % i)) for i in range(3)]
        self.tmpb_i = 0
        r2b = self.R2[:].rearrange("p a f -> p (a f)").bitcast(BF16)
        self.cqn = r2b[:, 0:1024].rearrange("p (a t) -> p a t", a=2)
        self.qnope = r2b[:, 1024:3072].rearrange("p (a t) -> p a t", a=4)
        self.B_cqn = self.B_R2
        self.B_qnope = self.B_R2
        self.qpe = sb("qpe", [96, 3, 512], BF16)
        self.B_qpe = Buf("qpe")
        self.gsig = sb("gsig", [32, 512], F32)
        self.B_gsig = Buf("gsig")
        self.omT = sb("omT", [128, 4, 512], BF16)
        self.onT = sb("onT", [128, 4, 512], BF16)
        self.B_omT, self.B_onT = Buf("omT"), Buf("onT")
        W4 = NCC + 4
        self.sbias = [(self.R2[:, 7, i * 256:(i + 1) * 256], Buf("sbias%d" % i)) for i in range(2)]
        self.pexp = [(self.R2[:, 4, 0:256], self.B_osb), (self.R2[:, 5, 0:256], self.B_owb)]
        self.pnb = [(sb("pnb%d" % i, [128, max(128, NCC)], BF16), Buf("pnb%d" % i)) for i in range(2)]
        self.pTt = [(sb("pTt%d" % i, [128, max(1, NCC // 128), 128], BF16), Buf("pTt%d" % i)) for i in range(2)]
        self.p4 = self.R2[:, 6, 0:W4]
        self.B_p4 = self.B_acc
        self.small = [(sb("small%d" % i, [128, 4], F32), Buf("small%d" % i)) for i in range(4)]
        self.small_i = 0
        self.cmp_i = 0
        self.imp = sb("imp", [128, 64], F32)
        self.imp2 = sb("imp2", [128, 64], F32)
        self.m8 = sb("m8", [128, 16], F32)
        self.nsb = sb("nsb", [128, 64], BF16)
        self.B_imp = Buf("imp")
        self.hidt = sb("hidt", [128, 4, 32], F32)
        self.hidb = sb("hidb", [128, 32], BF16)
        self.hidpad = sb("hidpad", [128, 128], BF16)
        self.B_hid = Buf("hid")
        self.B_out = Buf("out")
        self.B_scr = [Buf("scr%d" % c) for c in range(self.NCH)]

    def tf(self):
        r = self.tmpf[self.tmpf_i]
        self.tmpf_i = (self.tmpf_i + 1) % len(self.tmpf)
        return r

    def tb(self):
        r = self.tmpb[self.tmpb_i]
        self.tmpb_i = (self.tmpb_i + 1) % len(self.tmpb)
        return r

    def ptile(self):
        r = self.pt[self.pt_i]
        self.pt_i = (self.pt_i + 1) % len(self.pt)
        return r

    def sm(self):
        r = self.small[self.small_i]
        self.small_i = (self.small_i + 1) % len(self.small)
        return r

    def program(self):
        P, dr = self.P, self.dr
        self.tmpf_i = self.tmpb_i = self.pt_i = self.small_i = self.cmp_i = 0
        Bc = self.B_const
        P.dma("sp", lambda e: e.dma_start(out=self.identf[:], in_=dr["c_identf"]), Bc, writes=[Bc])
        P.dma("pool", lambda e: e.dma_start(out=self.bfc[:], in_=dr["c_bfc"]), Bc, writes=[Bc])
        import os as _os
        KSKIP = _os.environ.get("KSKIP", "")
        self.KSKIP = KSKIP
        if "c" not in KSKIP:
            P.dma("sp", lambda e: e.dma_start(out=self.stripc[:], in_=dr["c_stripc"]), Bc, writes=[Bc])
            P.dma("sp", lambda e: e.dma_start(out=self.stripj[:], in_=dr["c_stripj"]), Bc, writes=[Bc])
            P.dma("sp", lambda e: e.dma_start(out=self.selg[:], in_=dr["c_selg"]), Bc, writes=[Bc])
        Bc0 = self.B_cache[0]
        if "e" not in KSKIP:
            P.dma("pool", lambda e: e.dma_start(out=self.ksaug[64:128, 0, :], in_=dr["c_e"]), Bc, writes=[Bc])
            P.dma("pool", lambda e: e.dma_start(out=self.ksaug[0:64, 1, :], in_=dr["c_e"]), Bc, writes=[Bc])
        if "v" not in KSKIP:
            self.memset(self.Vs[:, :, 64:128], 1.0, [Bc], eng="pool")
            self.memset(self.Vw[:, :, 64:128], 1.0, [Bc], eng="pool")
        self.identb = self.bfc[:, 0:128]
        self.onesb = self.bfc[:, 128:256]
        self.on1024 = self.bfc[:, 256:384]
        self.on256 = self.bfc[:, 384:512]
        self.on128 = self.bfc[:, 512:640]
        self.LONG = 640
        if not P.dry:
            self.convert_weights()
        for l in range(self.NL):
            src = dr["x"] if l == 0 else dr["scr"]
            dst = dr["y"] if l == self.NL - 1 else dr["scr"]
            self.layer(l, src, dst, l == 0, l == self.NL - 1)

    def load_xtok(self, c, src, first_layer):
        P = self.P
        xtok = self.R4.bitcast(F32) if False else None
        xt = self.xtok_view()
        rd = [] if first_layer else [self.B_scr[c]]
        P.dma("sp", lambda e: e.dma_start(out=xt, in_=src[c * 512:(c + 1) * 512, :].rearrange("(a p) d -> p a d", p=128)),
              self.B_xtok, reads=rd, writes=[self.B_R4])

    def xtok_view(self):
        return self.R4[:, 0:8192].bitcast(F32).rearrange("p (a d) -> p a d", a=4)

    def layer(self, l, src, dst, first_layer, last_layer):
        P, dr = self.P, self.dr
        Blp = self.B_lp
        KSKIP = self.KSKIP
        if "l" not in KSKIP:
            P.dma("sp", lambda e: e.dma_start(out=self.lnp[:], in_=dr["lnp"][l]), Blp, writes=[Blp])
            P.dma("sp", lambda e: e.dma_start(out=self.qng[:], in_=dr["qng"][l]), Blp, writes=[Blp])
            P.dma("sp", lambda e: e.dma_start(out=self.kvng[:], in_=dr["kvng"][l]), Blp, writes=[Blp])
            P.dma("sp", lambda e: e.dma_start(out=self.cb[:], in_=dr["w_cb"][l]), Blp, writes=[Blp])
        if "p" not in KSKIP:
            P.dma("pool", lambda e: e.dma_start(out=self.cpe[:], in_=dr["w_cpe"][l]), Blp, writes=[Blp])
            P.dma("pool", lambda e: e.dma_start(out=self.c2[:], in_=dr["w_c2"][l]), Blp, writes=[Blp])
            P.dma("pool", lambda e: e.dma_start(out=self.ukt[:], in_=dr["w_ukt"][l]), Blp, writes=[Blp])
            P.dma("pool", lambda e: e.dma_start(out=self.uv[:], in_=dr["w_uv"][l]), Blp, writes=[Blp])
        if "m" not in KSKIP:
            self.memset(self.vc[:], 0.0, [self.B_vc], eng="pool")
            self.memset(self.kcT[:], 0.0, [self.B_kc], eng="pool")
            self.memset(self.kch[:], 0.0, [self.B_hist], eng="pool")
            self.memset(self.vch[:], 0.0, [self.B_hist], eng="pool")
            self.memset(self.hidpad[:], 0.0, [self.B_hid], eng="pool")
        self.load_xtok(0, src, first_layer)
        for c in range(self.NCH):
            self.chunk(l, c, src, dst, first_layer, last_layer)

    def chunk(self, l, c, src, dst, first_layer, last_layer):
        P = self.P
        R1, R2, R3 = self.R1, self.R2, self.R3
        B1, B2, B3, B4 = self.B_R1, self.B_R2, self.B_R3, self.B_R4
        xt = self.xtok_view()
        for k in range(8):
            pt_, pb_ = self.bank("G")
            for tt_ in range(4):
                self.tr(pt_[:, tt_ * 128:(tt_ + 1) * 128], xt[:, tt_, k * 128:(k + 1) * 128], self.identf[:],
                        [B4, self.B_const], pb_)
            self.act(R1[:, k, :], pt_[:], AF.Identity, [pb_], [B1], scale=ALPHA)
            self.cp(R3[:, k, :], pt_[:], [pb_], [B3])
        import os as _os
        _ks = _os.environ.get("KSTAGE", "")
        if _ks == "A":
            for k in range(8):
                self.cp(R2[:, k, :], R1[:, k, :], [B1], [B2])
        else:
            self.ffn(l, 0)
            if _ks != "F":
                self.layernorm(l, 0)
        if self.stop != 1:
            self.mixer(l, c)
            self.layernorm(l, 1)
        if self.stop == 0:
            self.ffn(l, 1)
        if c + 1 < self.NCH:
            self.load_xtok(c + 1, src, first_layer)
        if self.stop == 0:
            self.layernorm(l, 2, final=True)
        st = R1[:].rearrange("p a f -> p (a f)").rearrange("p (a d) -> p a d", a=4)
        for tt_ in range(4):
            for k2 in range(2):
                pt_, pb_ = self.bank("G")
                for kk in range(4):
                    k = k2 * 4 + kk
                    self.tr(pt_[:, kk * 128:(kk + 1) * 128], R2[:, k, tt_ * 128:(tt_ + 1) * 128], self.identf[:],
                            [B2, self.B_const], pb_)
                if k2 == 0:
                    self.cp(st[:, tt_, 0:512], pt_[:], [pb_], [B1], eng="act")
                else:
                    self.cp(st[:, tt_, 512:1024], pt_[:], [pb_], [B1])
        wb = self.B_out if last_layer else self.B_scr[c]
        P.dma("sp", lambda e: e.dma_start(out=dst[c * 512:(c + 1) * 512, :].rearrange("(a p) d -> p a d", p=128), in_=st),
              self.B_out, reads=[B1], writes=[wb])

    def ffn(self, l, which):
        dr = self.dr
        R1, R2, R3, R4 = self.R1, self.R2, self.R3, self.R4
        B1, B2, B3, B4 = self.B_R1, self.B_R2, self.B_R3, self.B_R4
        hT = R4[:].rearrange("p (j t) -> p j t", j=NFC)
        wg, wu, wd = dr["ffn_wg"][l, which], dr["ffn_wu"][l, which], dr["ffn_wd"][l, which]
        for fb in range(NFC // 2):
            f0 = fb * 256
            st, sbuf_ = self.wtile([
                (0, 8, 256, wg[:, f0:f0 + 256].rearrange("(k p) f -> p k f", p=128)),
                (2048, 8, 256, wu[:, f0:f0 + 256].rearrange("(k p) f -> p k f", p=128)),
            ])
            if st is None:
                continue
            wgv = self.wview(st, 0, 8, 256)
            wuv = self.wview(st, 2048, 8, 256)
            for fc in range(2):
                j = fb * 2 + fc
                pg, bg = self.bank("S")
                pu, bu = self.bank("G")
                for k in range(8):
                    self.mm(pg[:], wgv[:, k, fc * 128:(fc + 1) * 128], R3[:, k, :], k == 0, k == 7, [sbuf_, B3], bg)
                for k in range(8):
                    self.mm(pu[:], wuv[:, k, fc * 128:(fc + 1) * 128], R3[:, k, :], k == 0, k == 7, [sbuf_, B3], bu)
                t_, tb_ = self.tf()
                self.act(t_[:], pg[:], AF.Silu, [bg], [tb_])
                self.tt(hT[:, j, :], t_[:], pu[:], ALU.mult, [tb_, bu], [B4])
        for m in range(8):
            st, sbuf_ = self.wtile([(0, NFC, 128, wd[:, m * 128:(m + 1) * 128].rearrange("(j p) d -> p j d", p=128))])
            if st is None:
                continue
            wdv = self.wview(st, 0, NFC, 128)
            po, bo = self.bank("A")
            for j in range(NFC):
                self.mm(po[:], wdv[:, j, :], hT[:, j, :], j == 0, j == NFC - 1, [sbuf_, B4], bo)
            self.stt(R2[:, m, :], po[:], 0.5, R1[:, m, :], ALU.mult, ALU.add, [bo, B1], [B2])

    def layernorm(self, l, idx, final=False):
        R1, R2, R3, R4 = self.R1, self.R2, self.R3, self.R4
        B1, B2, B3, B4 = self.B_R1, self.B_R2, self.B_R3, self.B_R4
        pm, bm = self.bank("G")
        pq, bq = self.bank("G")
        if not final:
            zsq = R4[:, 0:4096].rearrange("p (k t) -> p k t", k=8)
            for k in range(8):
                self.cp(R3[:, k, :], R2[:, k, :], [B2], [B3])
                self.act(zsq[:, k, :], R2[:, k, :], AF.Square, [B2], [B4])
            for k in range(8):
                self.mm(pm[:], self.on1024, R3[:, k, :], k == 0, k == 7, [B3, self.B_const], bm)
            for k in range(8):
                self.mm(pq[:], self.on1024, zsq[:, k, :], k == 0, k == 7, [B4, self.B_const], bq)
        else:
            for k in range(8):
                self.cp(R3[:, k, :], R2[:, k, :], [B2], [B3])
            for k in range(8):
                self.mm(pm[:], self.on1024, R3[:, k, :], k == 0, k == 7, [B3, self.B_const], bm)
            for k in range(8):
                self.act(R3[:, k, :], R2[:, k, :], AF.Square, [B2], [B3])
            for k in range(8):
                self.mm(pq[:], self.on1024, R3[:, k, :], k == 0, k == 7, [B3, self.B_const], bq)
        mean, bmean = self.tf()
        m2, bm2 = self.tf()
        rstd, brstd = self.tf()
        nmr, bnmr = self.tf()
        self.cp(mean[:], pm[:], [bm], [bmean], eng="act")
        self.tt(m2[:], mean[:], mean[:], ALU.mult, [bmean], [bm2])
        self.tt(m2[:], pq[:], m2[:], ALU.subtract, [bq, bm2], [bm2])
        self.ts(m2[:], m2[:], LN_EPS, ALU.add, [bm2], [bm2])
        self.act(m2[:], m2[:], AF.Sqrt, [bm2], [bm2])
        self.recip(rstd[:], m2[:], [bm2], [brstd])
        self.stt(nmr[:], mean[:], -1.0, rstd[:], ALU.mult, ALU.mult, [bmean, brstd], [bnmr])
        gcol = self.lnp[:, idx * 16:idx * 16 + 8]
        bcol = self.lnp[:, idx * 16 + 8:idx * 16 + 16]
        for k in range(8):
            self.tt(R2[:, k, :], R2[:, k, :], rstd[:], ALU.mult, [B2, brstd], [B2])
            self.tt(R2[:, k, :], R2[:, k, :], nmr[:], ALU.add, [B2, bnmr], [B2])
            self.act(R2[:, k, :], R2[:, k, :], AF.Identity, [B2, self.B_lp], [B2], bias=bcol[:, k:k + 1], scale=gcol[:, k:k + 1])
            if not final:
                self.cp(R3[:, k, :], R2[:, k, :], [B2], [B3])
                self.act(R1[:, k, :], R2[:, k, :], AF.Identity, [B2], [B1], scale=ALPHA)

    def rmsnorm_fm(self, ps_list, ones_ap, gcols, outs, out_reads_writes):
        nk = len(ps_list)
        raws = []
        sqs = []
        for i, (pa, pb_) in enumerate(ps_list):
            r_, rb_ = self.tf()
            self.cp(r_[:], pa, [pb_], [rb_], eng="act")
            s_, sb_ = self.tb()
            self.act(s_[:], pa, AF.Square, [pb_], [sb_])
            raws.append((r_, rb_))
            sqs.append((s_, sb_))
        pss, bss = self.bank("G")
        for i, (s_, sb_) in enumerate(sqs):
            self.mm(pss[:], ones_ap, s_[:], i == 0, i == nk - 1, [sb_, self.B_const], bss)
        rq, brq = self.tf()
        self.act(rq[:], pss[:], AF.Sqrt, [bss], [brq], bias=RMS_EPS)
        self.recip(rq[:], rq[:], [brq], [brq])
        for i, (r_, rb_) in enumerate(raws):
            o_ap, o_w = outs[i]
            self.stt(o_ap, r_[:], gcols[i], rq[:], ALU.mult, ALU.mult, [rb_, brq, self.B_lp], o_w)

    def rope_apply(self, ps_main, b_main, ps_sw, b_sw, ctab, stab, rows, outs):
        t1, b1_ = self.tf()
        t2, b2_ = self.tf()
        self.tt(t1[0:rows, :], ps_main, ctab, ALU.mult, [b_main, self.B_rope], [b1_])
        self.tt(t2[0:rows, :], ps_sw, stab, ALU.mult, [b_sw, self.B_rope], [b2_])
        for (o_ap, r0, r1, wr) in outs:
            self.tt(o_ap, t1[r0:r1, :], t2[r0:r1, :], ALU.add, [b1_, b2_], wr)

    def mixer(self, l, c):
        P, dr = self.P, self.dr
        T = self.T
        R1, R2, R3, R4 = self.R1, self.R2, self.R3, self.R4
        B1, B2, B3, B4 = self.B_R1, self.B_R2, self.B_R3, self.B_R4
        Bc = self.B_cache[c]
        t0 = c * 512
        cs = slice(t0, t0 + 512)
        qabs = R4[:, 0:4096].rearrange("p (h t) -> p h t", h=8)
        qaug = R4[:, 4096:8192].rearrange("p (h t) -> p h t", h=8)
        qnw = None
        P.dma("sp", lambda e: e.dma_start(out=self.rope[:], in_=dr["c_rope"][:, :, t0:t0 + 512].rearrange("a p t -> p a t")),
              self.B_rope, writes=[self.B_rope])
        Cm, Sm, Cn, Sn = (self.rope[:, i, :] for i in range(4))

        st, sw_ = self.wtile([(0, 8, 384, dr["w_a"][l][:, 0:384].rearrange("(k p) f -> p k f", p=128))])
        if st is not None:
            wa = self.wview(st, 0, 8, 384)
            pl = []
            for i in range(2):
                p_, b_ = self.bank("G")
                for k in range(8):
                    self.mm(p_[:], wa[:, k, i * 128:(i + 1) * 128], R3[:, k, :], k == 0, k == 7, [sw_, B3], b_)
                pl.append((p_[:], b_))
            self.rmsnorm_fm(pl, self.on256, [self.qng[:, 0:1], self.qng[:, 1:2]],
                            [(self.cqn[:, 0, :], [self.B_cqn]), (self.cqn[:, 1, :], [self.B_cqn])], None)
            p_, b_ = self.bank("G")
            for k in range(8):
                self.mm(p_[:], wa[:, k, 256:384], R3[:, k, :], k == 0, k == 7, [sw_, B3], b_)
            self.rmsnorm_fm([(p_[:], b_)], self.on128, [self.kvng[:, 0:1]], [(self.ckvT[:, cs], [Bc])], None)
            pt_, pb_ = self.bank("G")
            ptb = pt_[:].bitcast(BF16)
            for tt_ in range(4):
                self.tr(ptb[:, tt_ * 128:(tt_ + 1) * 128], self.ckvT[:, t0 + tt_ * 128:t0 + (tt_ + 1) * 128], self.identb,
                        [Bc, self.B_const], pb_)
            self.cp(self.ckvtok[:, c * 4:(c + 1) * 4, :], ptb[:, 0:512].rearrange("p (a b) -> p a b", a=4), [pb_], [Bc])
        stp, swp_ = self.wtile([(0, 8, 192, dr["w_a"][l][:, 384:576].rearrange("(k p) f -> p k f", p=128))])
        if stp is not None:
            wap = self.wview(stp, 0, 8, 192)
            p1, b1_ = self.bank("G")
            p2, b2_ = self.bank("G")
            for k in range(8):
                self.mm(p1[0:96, :], wap[:, k, 0:96], R3[:, k, :], k == 0, k == 7, [swp_, B3], b1_)
            for k in range(8):
                self.mm(p2[0:96, :], wap[:, k, 96:192], R3[:, k, :], k == 0, k == 7, [swp_, B3], b2_)
            self.rope_apply(p1[0:96, :], b1_, p2[0:96, :], b2_, Cm[0:96, :], Sm[0:96, :], 96,
                            [(self.kpeT[:, cs], 0, 96, [Bc])])

        st, sw_ = self.wtile([(0, 2, 1088, dr["w_uq"][l].rearrange("(k p) f -> p k f", p=128))])
        if st is not None:
            wq = self.wview(st, 0, 2, 1088)
            for i in range(4):
                p_, b_ = self.bank("G")
                for k in range(2):
                    self.mm(p_[:], wq[:, k, i * 128:(i + 1) * 128], self.cqn[:, k, :], k == 0, k == 1, [sw_, self.B_cqn], b_)
                self.cp(self.qnope[:, i, :], p_[:], [b_], [self.B_qnope], eng=("act" if i % 2 else "dve"))
            uktv = self.ukt[:].rearrange("p (i m) -> p i m", i=4)
            for h in range(8):
                i, hh = h // 2, h % 2
                p_, b_ = self.bank("G")
                self.mm(p_[:], uktv[64 * hh:64 * hh + 64, i, :], self.qnope[64 * hh:64 * hh + 64, i, :], True, True,
                        [self.B_lp, self.B_qnope], b_)
                self.cp(qabs[:, h, :], p_[:], [b_], [B4], eng=("act" if h % 2 else "dve"))
            for j in range(3):
                rows = 96 if j < 2 else 64
                p1, b1_ = self.bank("G")
                p2, b2_ = self.bank("G")
                for k in range(2):
                    self.mm(p1[0:rows, :], wq[:, k, 512 + j * 96:512 + j * 96 + rows], self.cqn[:, k, :], k == 0, k == 1,
                            [sw_, self.B_cqn], b1_)
                for k in range(2):
                    self.mm(p2[0:rows, :], wq[:, k, 800 + j * 96:800 + j * 96 + rows], self.cqn[:, k, :], k == 0, k == 1,
                            [sw_, self.B_cqn], b2_)
                self.rope_apply(p1[0:rows, :], b1_, p2[0:rows, :], b2_, Cm[0:rows, :], Sm[0:rows, :], rows,
                                [(self.qpe[0:rows, j, :], 0, rows, [self.B_qpe])])

        for i in range(4):
            st, sw_ = self.wtile([(0, 8, 128, dr["w_q"][l][:, i * 128:(i + 1) * 128].rearrange("(k p) f -> p k f", p=128)),
                                  (1024, 8, 128, dr["w_q"][l][:, 512 + i * 128:512 + (i + 1) * 128].rearrange("(k p) f -> p k f", p=128))])
            if st is None:
                continue
            wq1 = self.wview(st, 0, 8, 128)
            wq2 = self.wview(st, 1024, 8, 128)
            p1, b1_ = self.bank("G")
            p2, b2_ = self.bank("G")
            for k in range(8):
                self.mm(p1[:], wq1[:, k, :], R3[:, k, :], k == 0, k == 7, [sw_, B3], b1_)
            for k in range(8):
                self.mm(p2[:], wq2[:, k, :], R3[:, k, :], k == 0, k == 7, [sw_, B3], b2_)
            self.rope_apply(p1[:], b1_, p2[:], b2_, Cn, Sn, 128,
                            [(qaug[0:64, i, :], 0, 64, [B4]), (qaug[64:128, 4 + i, :], 64, 128, [B4])])
        for bi in range(3):
            st, sw_ = self.wtile([(0, 8, 128, dr["w_k"][l][:, bi * 128:(bi + 1) * 128].rearrange("(k p) f -> p k f", p=128)),
                                  (1024, 8, 128, dr["w_k"][l][:, 512 + bi * 128:512 + (bi + 1) * 128].rearrange("(k p) f -> p k f", p=128))]
                                 + ([(2048, 8, 128, dr["w_k"][l][:, 384:512].rearrange("(k p) f -> p k f", p=128))] if bi == 0 else []))
            if st is None:
                continue
            wk1 = self.wview(st, 0, 8, 128)
            wk2 = self.wview(st, 1024, 8, 128)
            if bi == 0:
                self.cp(self.kch[:, 0:16], self.kch[:, 512:528], [self.B_hist], [self.B_hist])
                self.cp(self.vch[:, 0:16], self.vch[:, 512:528], [self.B_hist], [self.B_hist])
                wvc = self.wview(st, 2048, 8, 128)
                p1, b1_ = self.bank("G")
                for k in range(8):
                    self.mm(p1[:], wvc[:, k, :], R3[:, k, :], k == 0, k == 7, [sw_, B3], b1_)
                self.cp(self.vch[:, 16:528], p1[:], [b1_], [self.B_hist], eng="act")
            p1, b1_ = self.bank("G")
            p2, b2_ = self.bank("G")
            for k in range(8):
                self.mm(p1[:], wk1[:, k, :], R3[:, k, :], k == 0, k == 7, [sw_, B3], b1_)
            for k in range(8):
                self.mm(p2[:], wk2[:, k, :], R3[:, k, :], k == 0, k == 7, [sw_, B3], b2_)
            if bi == 0:
                outs = [(self.kch[:, 16:528], 0, 128, [self.B_hist])]
            elif bi == 1:
                outs = [(self.ksaug[0:64, 0, cs], 0, 64, [Bc]), (self.ksaug[64:128, 1, cs], 64, 128, [Bc])]
            else:
                outs = [(self.kwT[:, cs], 0, 128, [Bc])]
            self.rope_apply(p1[:], b1_, p2[:], b2_, Cn, Sn, 128, outs)
        st, sw_ = self.wtile([(0, 8, 288, dr["w_v"][l].rearrange("(k p) f -> p k f", p=128))])
        if st is not None:
            wv = self.wview(st, 0, 8, 288)
            for tt_ in range(4):
                p_, b_ = self.bank("G")
                for k in range(8):
                    self.mm(p_[:, 0:256], R3[:, k, tt_ * 128:(tt_ + 1) * 128], wv[:, k, 0:256], k == 0, k == 7, [sw_, B3], b_)
                kt = c * 4 + tt_
                vsrc = p_[:, 0:128].rearrange("p (g d) -> p g d", g=2)
                vdst = self.Vs[:, kt, :].rearrange("p (g d) -> p g d", g=3)[:, 0:3:2, :]
                self.cp(vdst, vsrc, [b_], [Bc])
                vsrc2 = p_[:, 128:256].rearrange("p (g d) -> p g d", g=2)
                vdst2 = self.Vw[:, kt, :].rearrange("p (g d) -> p g d", g=3)[:, 0:3:2, :]
                self.cp(vdst2, vsrc2, [b_], [Bc], eng="act")
            p_, b_ = self.bank("G")
            for k in range(8):
                self.mm(p_[0:32, :], wv[:, k, 256:288], R3[:, k, :], k == 0, k == 7, [sw_, B3], b_)
            self.act(self.gsig[:], p_[0:32, :], AF.Sigmoid, [b_], [self.B_gsig])

        cc0 = 32 * c
        for kv in range(2):
            st, sw_ = self.wtile([(0, 32, 128, dr["w_c1k" if kv == 0 else "w_c1v"][l].rearrange("p (i h) -> p i h", i=32))])
            if st is None:
                continue
            w1 = self.wview(st, 0, 32, 128)
            hist = self.kch if kv == 0 else self.vch
            if c == 0:
                p_, b_ = self.bank("G")
                for i in range(32):
                    self.mm(p_[:, 0:1], w1[0:64, i, :], self.cpe[0:64, kv * 32 + i:kv * 32 + i + 1], i == 0, i == 31,
                            [sw_, self.B_lp], b_)
                self.tt(self.pb[:, kv:kv + 1], p_[:, 0:1], self.cb[:, kv:kv + 1], ALU.add, [b_, self.B_lp], [self.B_pb])
            for g in range(2):
                p_, b_ = self.bank("G")
                for i in range(32):
                    self.mm(p_[:, 0:32], w1[64 * g:64 * g + 64, i, :], hist[64 * g:64 * g + 64, i:i + 497:16],
                            i == 0, i == 31, [sw_, self.B_hist], b_)
                ht = self.hidt
                Bh = self.B_hid
                self.act(ht[:, 0, :], p_[:, 0:32], AF.Identity, [b_, self.B_pb], [Bh], bias=self.pb[:, kv:kv + 1])
                self.tt(ht[:, 1, :], ht[:, 0, :], ht[:, 0, :], ALU.mult, [Bh], [Bh])
                self.ts(ht[:, 1, :], ht[:, 1, :], 0.044715, ALU.mult, [Bh], [Bh], s2=1.0, op1=ALU.add)
                self.tt(ht[:, 1, :], ht[:, 1, :], ht[:, 0, :], ALU.mult, [Bh], [Bh])
                self.act(ht[:, 2, :], ht[:, 1, :], AF.Sigmoid, [Bh], [Bh], scale=1.5957691216057308)
                if kv == 0:
                    self.tt(self.hidb[:], ht[:, 0, :], ht[:, 2, :], ALU.mult, [Bh], [Bh])
                    p2, b2_ = self.bank("G")
                    self.mm(p2[0:64, 0:32], self.c2[:, 0:64], self.hidb[:], True, True, [Bh, self.B_lp], b2_)
                    self.cp(self.kcT[64 * g:64 * g + 64, cc0:cc0 + 32], p2[0:64, 0:32], [b2_], [self.B_kc])
                else:
                    po = 32 * (c % 4)
                    self.tt(self.hidpad[:, po:po + 32], ht[:, 0, :], ht[:, 2, :], ALU.mult, [Bh], [Bh])
                    p2, b2_ = self.bank("G")
                    self.mm(p2[:, 0:64], self.hidpad[:], self.c2[:, 64:128], True, True, [Bh, self.B_lp], b2_)
                    self.tt(self.vc[:, c // 4, g, :], self.vc[:, c // 4, g, :], p2[:, 0:64], ALU.add, [b2_, self.B_vc], [self.B_vc])
                    self.memset(self.hidpad[:, po:po + 32], 0.0, [Bh])

        self.memset(self.fence_t[:, 0:1], 0.0, [B2, self.B_oc, self.B_osb, self.B_owb, self.B_acc, self.sbias[0][1], self.sbias[1][1]])
        self.compressed(l, c, qnw, qaug)

        self.attention(l, c, qabs, qaug, qnw)

        self.memset(self.fence_t[:, 1:2], 0.0, [B2, self.B_oc, self.B_osb, self.B_owb, self.B_acc, self.sbias[0][1], self.sbias[1][1]])
        yT = R4[:, 0:4096].rearrange("p (k t) -> p k t", k=8)
        for m in range(8):
            ms = slice(m * 128, (m + 1) * 128)
            st, sw_ = self.wtile([(0, 4, 128, dr["w_pm"][l][:, ms].rearrange("(k p) f -> p k f", p=128)),
                                  (512, 4, 128, dr["w_pn"][l][:, ms].rearrange("(k p) f -> p k f", p=128)),
                                  (1024, 8, 128, dr["w_gm"][l][:, ms].rearrange("(k p) f -> p k f", p=128)),
                                  (2048, 8, 128, dr["w_gn"][l][:, ms].rearrange("(k p) f -> p k f", p=128))])
            if st is None:
                continue
            wpm = self.wview(st, 0, 4, 128)
            wpn = self.wview(st, 512, 4, 128)
            wgm = self.wview(st, 1024, 8, 128)
            wgn = self.wview(st, 2048, 8, 128)
            ppm, bpm = self.bank("G")
            pgm, bgm = self.bank("S")
            ppn, bpn = self.bank("G")
            pgn, bgn = self.bank("S")
            for k in range(4):
                self.mm(ppm[:], wpm[:, k, :], self.omT[:, k, :], k == 0, k == 3, [sw_, self.B_omT], bpm)
            for k in range(8):
                self.mm(pgm[:], wgm[:, k, :], R3[:, k, :], k == 0, k == 7, [sw_, B3], bgm)
            for k in range(4):
                self.mm(ppn[:], wpn[:, k, :], self.onT[:, k, :], k == 0, k == 3, [sw_, self.B_onT], bpn)
            for k in range(8):
                self.mm(pgn[:], wgn[:, k, :], R3[:, k, :], k == 0, k == 7, [sw_, B3], bgn)
            s1, bs1 = self.tf()
            s2, bs2 = self.tf()
            self.act(s1[:], pgm[:], AF.Sigmoid, [bgm], [bs1])
            self.act(s2[:], pgn[:], AF.Sigmoid, [bgn], [bs2])
            self.tt(s1[:], ppm[:], s1[:], ALU.mult, [bs1, bpm], [bs1])
            self.tt(s2[:], ppn[:], s2[:], ALU.mult, [bs2, bpn], [bs2])
            self.tt(yT[:, m, :], s1[:], s2[:], ALU.add, [bs1, bs2], [B4])
        for db in range(2):
            d0 = db * 512
            st, sw_ = self.wtile([(0, 8, 512, dr["w_out"][l][:, d0:d0 + 512].rearrange("(k p) f -> p k f", p=128))])
            if st is None:
                continue
            wo = self.wview(st, 0, 8, 512)
            for mm_ in range(4):
                m = db * 4 + mm_
                p_, b_ = self.bank("A")
                for k in range(8):
                    self.mm(p_[:], wo[:, k, mm_ * 128:(mm_ + 1) * 128], yT[:, k, :], k == 0, k == 7, [sw_, B4], b_)
                self.tt(R2[:, m, :], p_[:], R1[:, m, :], ALU.add, [b_, B1], [B2])

    def compressed(self, l, c, qnw, qaug):
        P = self.P
        R2, B2, B4 = self.R2, self.B_R2, self.B_R4
        NCCc = 32 * (c + 1)
        ntile = (NCCc + 127) // 128
        NS = self.NS
        do_sel = (NS > 16) and (c >= 2)
        oc = R2[:, 0:4, :]
        if (not do_sel) or (self.NCC // 4 < 64):
            for i in range(4):
                self.memset(qaug[64:128, i, :], 0.0, [B4])
                self.memset(qaug[0:64, 4 + i, :], 0.0, [B4])
        for tt_ in range(4):
            qt = c * 4 + tt_
            ts_ = slice(tt_ * 128, (tt_ + 1) * 128)
            x0 = 248 - 8 * qt
            for g in range(2):
                rs_ = slice(64 * g, 64 * g + 64)
                for hh in range(4):
                    i = hh
                    ps_, bs_ = self.bank("S")
                    self.mm(ps_[:, 0:NCCc], qaug[rs_, i + 4 * g, ts_], self.kcT[rs_, 0:NCCc], True, True, [B4, self.B_kc], bs_)
                    k_ = self.cmp_i % 2
                    self.cmp_i += 1
                    sbt, sbb = self.sbias[k_]
                    pet, peb = self.pexp[k_]
                    pnt, pnb_ = self.pnb[k_]
                    pTt, pTb = self.pTt[k_]
                    self.stt(sbt[:, 0:NCCc], ps_[:, 0:NCCc], SC_NSA, self.stripc[:, x0:x0 + NCCc], ALU.mult, ALU.add,
                             [bs_, self.B_const], [sbb])
                    self.memset(sbt[:, 0:1], NEGB, [sbb])
                    sm_, smb = self.sm()
                    self.act(pet[:, 0:NCCc], sbt[:, 0:NCCc], AF.Exp, [sbb], [peb])
                    self.P.add("dve", (lambda o, i_: (lambda e: e.tensor_reduce(out=o, in_=i_, axis=AX.X, op=ALU.add)))(
                        sm_[:, 0:1], pet[:, 0:NCCc]), reads=[peb], writes=[smb])
                    self.ts(sm_[:, 1:2], sm_[:, 0:1], 1e-30, ALU.max, [smb], [smb])
                    self.recip(sm_[:, 2:3], sm_[:, 1:2], [smb], [smb])
                    self.ts(pnt[:, 0:NCCc], pet[:, 0:NCCc], sm_[:, 2:3], ALU.mult, [peb, smb], [pnb_])
                    if do_sel:
                        if hh == 0:
                            self.ts(self.p4[:, 0:NCCc], pet[:, 0:NCCc], sm_[:, 2:3], ALU.mult, [peb, smb], [self.B_p4])
                        else:
                            self.stt(self.p4[:, 0:NCCc], pet[:, 0:NCCc], sm_[:, 2:3], self.p4[:, 0:NCCc], ALU.mult, ALU.add,
                                     [peb, smb, self.B_p4], [self.B_p4])
                    ptp, ptb_ = self.bank("G")
                    ptbf = ptp[:].bitcast(BF16)
                    for ti in range(ntile):
                        w_ = min(128, NCCc - ti * 128)
                        self.tr(ptbf[0:w_, ti * 128:(ti + 1) * 128], pnt[:, ti * 128:ti * 128 + w_], self.identb,
                                [pnb_, self.B_const], ptb_)
                    for ti in range(ntile):
                        w_ = min(128, NCCc - ti * 128)
                        self.cp(pTt[0:w_, ti, :], ptbf[0:w_, ti * 128:(ti + 1) * 128], [ptb_], [pTb],
                                eng=("act" if ti % 2 else "dve"))
                    po_, bo_ = self.bank("A")
                    for ti in range(ntile):
                        w_ = min(128, NCCc - ti * 128)
                        self.mm(po_[0:64, 0:128], self.vc[0:w_, ti, g, :], pTt[0:w_, ti, :], ti == 0, ti == ntile - 1,
                                [self.B_vc, pTb], bo_)
                    self.cp(oc[rs_, i, ts_], po_[0:64, 0:128], [bo_], [self.B_oc])
                if do_sel:
                    Bi = self.B_imp
                    W4 = NCCc + 4
                    nsb_ = NS
                    self.memset(self.p4[:, NCCc:self.NCC + 4], 0.0, [self.B_p4])
                    nj = self.NCC // 4
                    self.P.add("dve", (lambda o, i_: (lambda e: e.tensor_reduce(out=o, in_=i_, axis=AX.X, op=ALU.add)))(
                        self.imp[:, 0:nj], self.p4[:, 0:4 * nj].rearrange("p (j r) -> p j r", r=4)),
                        reads=[self.B_p4], writes=[Bi])
                    self.tt(self.imp[:, 0:nj], self.imp[:, 0:nj], self.p4[:, 4:4 * nj + 4:4], ALU.add, [Bi, self.B_p4], [Bi])
                    xj = 62 - 2 * qt
                    self.tt(self.imp[:, 0:nj], self.imp[:, 0:nj], self.stripj[:, xj:xj + nj], ALU.add, [Bi, self.B_const], [Bi])
                    self.ts(self.imp[:, 0:1], self.imp[:, 0:1], 1e30, ALU.add, [Bi], [Bi])
                    self.P.add("dve", lambda e: e.max(out=self.m8[:, 0:8], in_=self.imp[:, 0:nj]), reads=[Bi], writes=[Bi])
                    self.P.add("dve", lambda e: e.match_replace(out=self.imp2[:, 0:nj], in_to_replace=self.m8[:, 0:8],
                                                                 in_values=self.imp[:, 0:nj], imm_value=-3e38),
                               reads=[Bi], writes=[Bi])
                    self.P.add("dve", lambda e: e.max(out=self.m8[:, 8:16], in_=self.imp2[:, 0:nj]), reads=[Bi], writes=[Bi])
                    self.ts(self.imp2[:, 0:nj], self.imp[:, 0:nj], self.m8[:, 15:16], ALU.is_ge, [Bi], [Bi])
                    self.ts(self.nsb[:, 0:nj], self.imp2[:, 0:nj], -NEGB, ALU.mult, [Bi], [Bi], s2=NEGB, op1=ALU.add)
                    ptp, ptb_ = self.bank("G")
                    ptbf = ptp[:].bitcast(BF16)
                    self.tr(ptbf[0:nj, 0:128], self.nsb[:, 0:nj], self.identb, [Bi, self.B_const], ptb_)
                    for i2 in range(4):
                        if g == 0:
                            self.cp(qaug[64:64 + nj, i2, ts_], ptbf[0:nj, 0:128], [ptb_], [B4])
                        else:
                            self.cp(qaug[0:nj, 4 + i2, ts_], ptbf[0:nj, 0:128], [ptb_], [B4])

    def run_stream(self, jobs, skew=2):
        n = len(jobs)
        pts = [None] * n
        deferred = []
        for j in range(n + skew):
            if j < n:
                ps_, bs_ = self.bank("S")
                jobs[j]["qk"](ps_, bs_)
                pt_, ptb_ = self.ptile()
                self.act(pt_[:], ps_[:], AF.Exp, [bs_], [ptb_], scale=jobs[j]["scale"])
                pts[j] = (pt_, ptb_)
            jj = j - skew
            if jj >= 0:
                jobs[jj]["pv"](*pts[jj])
                if jobs[jj].get("fin"):
                    jobs[jj]["fin"]()
                if jobs[jj].get("fin_pe"):
                    deferred.append((j + 3, jobs[jj]["fin_pe"]))
            while deferred and deferred[0][0] <= j:
                deferred.pop(0)[1]()
        for _, f in deferred:
            f()

    def attention(self, l, c, qabs, qaug, qnw):
        R2, B2, B4 = self.R2, self.B_R2, self.B_R4
        LONG = self.LONG
        nkt = 4 * c + 4
        caches = [self.B_cache[cc] for cc in range(c + 1)]
        uvv = self.uv[:].rearrange("p (h d) -> p h d", h=8)
        selv = self.selg[:].rearrange("p (a m) -> p a m", a=12)
        banks = [b for pool in ("S", "A", "G") for b in self.pools[pool]]

        def bias_ap(d):
            s0 = LONG + (3 - d) * 128
            return self.bfc[:, s0:s0 + 512]

        acc_banks = banks[3:7]
        ai = [0]

        def next_acc():
            b = acc_banks[ai[0] % len(acc_banks)]
            ai[0] += 1
            return b

        g7 = banks[7]
        jobs = []
        for h in range(8):
            i, hh = h // 2, h % 2
            j3, r3 = h // 3, h % 3
            pO, bO = next_acc()
            pS, bS = next_acc()
            for kt in range(nkt):
                def qk(ps_, bs_, kt=kt, h=h, j3=j3, r3=r3):
                    ks = slice(kt * 128, (kt + 1) * 128)
                    diag = kt >= 4 * c
                    self.mm(ps_[:], self.ckvT[:, ks], qabs[:, h, :], True, False, caches + [B4], bs_)
                    self.mm(ps_[:], self.kpeT[32 * r3:32 * r3 + 32, ks], self.qpe[32 * r3:32 * r3 + 32, j3, :], False, not diag,
                            caches + [self.B_qpe], bs_)
                    if diag:
                        self.mm(ps_[:], self.identb, bias_ap(kt - 4 * c), False, True, [self.B_const], bs_)

                def pv(pt_, ptb_, kt=kt, pO=pO, bO=bO, pS=pS, bS=bS):
                    self.mm(pO[:], self.ckvtok[:, kt, :], pt_[:], kt == 0, kt == nkt - 1, caches + [ptb_], bO)
                    self.mm(pS[:], self.onesb, pt_[:], kt == 0, kt == nkt - 1, [self.B_const, ptb_], bS)

                job = {"qk": qk, "pv": pv, "scale": SC_MLA}
                if kt == nkt - 1:
                    hold = {}

                    def fin(pO=pO, bO=bO, pS=pS, bS=bS, hold=hold):
                        rs_, rsb = self.tf()
                        self.cp(rs_[:], pS[:], [bS], [rsb], eng="act")
                        self.recip(rs_[:], rs_[:], [rsb], [rsb])
                        ol, olb = self.tb()
                        self.tt(ol[:], pO[:], rs_[:], ALU.mult, [bO, rsb], [olb])
                        hold["ol"] = (ol, olb)

                    def fin_pe(h=h, i=i, hh=hh, hold=hold):
                        ol, olb = hold["ol"]
                        p_, b_ = g7
                        self.mm(p_[0:64, :], uvv[:, h, :], ol[:], True, True, [self.B_lp, olb], b_)
                        self.cp(self.omT[64 * hh:64 * hh + 64, i, :], p_[0:64, :], [b_], [self.B_omT])

                    job["fin"] = fin
                    job["fin_pe"] = fin_pe
                jobs.append(job)
        self.run_stream(jobs)

        acc_banks = banks[3:6]
        ai[0] = 0
        gsel = [banks[6], banks[7]]
        gi = [0]
        kt0 = max(0, 4 * c - 4)
        osb = R2[:, 4, :]
        owb = R2[:, 5, :]
        acc = R2[:, 6, :]
        jobs = []
        for i in range(4):
            for g in range(2):
                h = i + 4 * g
                osl = slice(64 * g, 64 * g + 64)
                sml = slice(64 * (1 - g), 64 * (1 - g) + 64)
                vcol = slice(0, 128) if g == 0 else slice(64, 192)
                for br in range(2):
                    pO, bO = next_acc()
                    kts = list(range(nkt)) if br == 0 else list(range(kt0, nkt))
                    for kt in kts:
                        first, last = kt == kts[0], kt == kts[-1]
                        if br == 0:
                            def qk(ps_, bs_, kt=kt, g=g, h=h):
                                ks = slice(kt * 128, (kt + 1) * 128)
                                diag = kt >= 4 * c
                                self.mm(ps_[:], self.ksaug[:, g, ks], qaug[:, h, :], True, not diag, caches + [B4, self.B_const], bs_)
                                if diag:
                                    self.mm(ps_[:], self.identb, bias_ap(kt - 4 * c), False, True, [self.B_const], bs_)

                            def pv(pt_, ptb_, kt=kt, pO=pO, bO=bO, vcol=vcol, first=first, last=last):
                                self.mm(pO[:], self.Vs[:, kt, vcol], pt_[:], first, last, caches + [ptb_, self.B_const], bO)
                        else:
                            def qk(ps_, bs_, kt=kt, osl=osl, h=h):
                                ks = slice(kt * 128, (kt + 1) * 128)
                                self.mm(ps_[:], self.kwT[osl, ks], qaug[osl, h, :], True, False, caches + [B4], bs_)
                                p_ = kt - 4 * c + 4
                                s0 = LONG + (7 - p_) * 128
                                self.mm(ps_[:], self.identb, self.bfc[:, s0:s0 + 512], False, True, [self.B_const], bs_)

                            def pv(pt_, ptb_, kt=kt, pO=pO, bO=bO, vcol=vcol, first=first, last=last):
                                self.mm(pO[:], self.Vw[:, kt, vcol], pt_[:], first, last, caches + [ptb_, self.B_const], bO)
                        job = {"qk": qk, "pv": pv, "scale": SC_NSA}
                        if last:
                            dstb = osb if br == 0 else owb
                            dstB = self.B_osb if br == 0 else self.B_owb

                            def fin(pO=pO, bO=bO, osl=osl, sml=sml, dstb=dstb, dstB=dstB, last_of_pair=(g == 1 and br == 1), i=i):
                                rs_, rsb = self.tf()
                                self.cp(rs_[sml, :], pO[sml, :], [bO], [rsb], eng="act")
                                self.recip(rs_[sml, :], rs_[sml, :], [rsb], [rsb])
                                self.tt(dstb[osl, :], pO[osl, :], rs_[sml, :], ALU.mult, [bO, rsb], [dstB])
                                if last_of_pair:
                                    for br_ in range(3):
                                        pg_, bg_ = gsel[gi[0] % 2]
                                        gi[0] += 1
                                        self.mm(pg_[:], selv[:, br_ * 4 + i, :], self.gsig[:], True, True, [self.B_const, self.B_gsig], bg_)
                                        if br_ == 0:
                                            self.tt(acc, pg_[:], R2[:, i, :], ALU.mult, [self.B_oc, bg_], [self.B_acc])
                                        elif br_ == 1:
                                            self.tt(osb, pg_[:], osb, ALU.mult, [self.B_osb, bg_], [self.B_osb])
                                            self.tt(acc, acc, osb, ALU.add, [self.B_acc, self.B_osb], [self.B_acc])
                                        else:
                                            self.tt(owb, pg_[:], owb, ALU.mult, [self.B_owb, bg_], [self.B_owb])
                                            self.tt(self.onT[:, i, :], acc, owb, ALU.add, [self.B_acc, self.B_owb], [self.B_onT])

                            job["fin"] = fin
                        jobs.append(job)
        self.run_stream(jobs)


_CACHE = {}


def kernel(**inputs):
    x = np.asarray(inputs["x"], dtype=np.float32)
    Bn, T, _ = x.shape
    w = prep_weights(inputs)
    consts = make_consts(T)
    key = (T,)
    if key not in _CACHE:
        b = Builder(T)
        nc = b.build()
        _CACHE[key] = (b, nc)
    b, nc = _CACHE[key]
    shared = {}
    shared.update(w)
    shared.update(consts)
    in_maps = []
    for i in range(Bn):
        m = dict(shared)
        m["x"] = np.ascontiguousarray(x[i])
        in_maps.append(m)
    res = run_bass_kernel_spmd(nc, in_maps, core_ids=list(range(Bn)))
    return np.stack([np.asarray(r["y"], dtype=np.float32) for r in res.results], 0)
```

```python
from contextlib import ExitStack
import numpy as np
import concourse.bass as bass
import concourse.mybir as mybir
from concourse.bass_utils import run_bass_kernel_spmd

F32 = mybir.dt.float32
BF16 = mybir.dt.bfloat16
AF = mybir.ActivationFunctionType
ALU = mybir.AluOpType
AX = mybir.AxisListType

D = 1024
DFF = 2816
NFC = DFF // 128
DEPTH = 2
ALPHA = (2 * DEPTH) ** 0.25
LN_EPS = 1e-5
RMS_EPS = 1e-6
NEGB = -30000.0
SC_MLA = 96 ** -0.5
SC_NSA = 0.125
THETA = 500000.0
SLOT = 4096
NSLOT = 2
PF = 1

ENGS = ("pe", "act", "dve", "pool", "sp")


class Buf:
    __slots__ = ("name", "w", "r", "dsem", "dcnt", "x")

    def __init__(self, name, x=False):
        self.name = name
        self.x = x
        self.w = None
        self.r = {}
        self.dsem = None
        self.dcnt = 0


class Prog:
    def __init__(self, nc, dry=False):
        self.nc = nc
        self.dry = dry
        self.ops = {e: [] for e in ENGS}
        self.dma_cnt = {}
        self.dma_keys = []
        self.dma_sem = {}

    def _deps(self, eng, reads, writes):
        deps = set()
        for b in reads:
            if b.w is not None:
                deps.add(b.w)
            if b.x:
                for k_, h_ in b.r.items():
                    if k_ != eng:
                        deps.add(h_)
        for b in writes:
            if b.w is not None:
                deps.add(b.w)
            for h in b.r.values():
                deps.add(h)
        if eng == "pe":
            deps = {h for h in deps if not (h[0] == "e" and h[1] == "pe")}
        return deps

    def add(self, eng, fn, reads=(), writes=()):
        if self.dry:
            return None
        deps = self._deps(eng, reads, writes)
        idx = len(self.ops[eng])
        h = ("e", eng, idx)
        self.ops[eng].append([fn, deps, h, False])
        for b in writes:
            b.w = h
            b.r = {}
        for b in reads:
            b.r[eng] = h
        return h

    def dma(self, eng, fn, sem_buf, reads=(), writes=()):
        if self.dry:
            return None
        deps = self._deps(eng, reads, writes)
        kind = "sw" if eng == "pool" else "hw"
        key = (sem_buf, kind)
        if key not in self.dma_cnt:
            self.dma_cnt[key] = 0
            self.dma_keys.append(key)
        self.dma_cnt[key] += 1
        h = ("d", key, self.dma_cnt[key])
        self.ops[eng].append([fn, deps, h, False])
        for b in writes:
            b.w = h
            b.r = {}
        for b in reads:
            b.r[("d", id(sem_buf), kind)] = h
        return h

    def emit(self, final_waits=()):
        nc = self.nc
        for e in ENGS:
            for op in self.ops[e]:
                for h in op[1]:
                    if h[0] == "e":
                        self.ops[h[1]][h[2]][3] = True
        cum = {}
        for e in ENGS:
            c = 0
            arr = []
            for op in self.ops[e]:
                if op[3]:
                    c += 1
                arr.append(c)
            cum[e] = arr
        with ExitStack() as es:
            esem = {e: es.enter_context(nc.semaphore("s_" + e)) for e in ENGS}
            for i, key in enumerate(self.dma_keys):
                self.dma_sem[key] = es.enter_context(nc.semaphore("d%d" % i))
            block = es.enter_context(nc.Block())

            def run(e, engobj, extra=()):
                waited_e = {}
                waited_d = {}

                def do_wait(h):
                    if h[0] == "e":
                        v = cum[h[1]][h[2]]
                        if waited_e.get(h[1], 0) >= v:
                            return
                        waited_e[h[1]] = v
                        engobj.wait_ge(esem[h[1]], v)
                    else:
                        key = h[1]
                        v = 16 * h[2]
                        if waited_d.get(key, 0) >= v:
                            return
                        waited_d[key] = v
                        engobj.wait_ge(self.dma_sem[key], v)

                for fn, deps, h, need in self.ops[e]:
                    for d in deps:
                        do_wait(d)
                    ins = fn(engobj)
                    if h[0] == "d":
                        ins.then_inc(self.dma_sem[h[1]], 16)
                    elif need:
                        ins.then_inc(esem[e], 1)
                for hh in extra:
                    do_wait(hh)

            @block.tensor
            def _(eng):
                run("pe", eng)

            @block.scalar
            def _(eng):
                run("act", eng)

            @block.vector
            def _(eng):
                run("dve", eng)

            @block.gpsimd
            def _(eng):
                run("pool", eng)

            @block.sync
            def _(eng):
                run("sp", eng, extra=final_waits)


def make_consts(T):
    c = {}
    c["c_identf"] = np.eye(128, dtype=np.float32)
    bfc = np.zeros((128, 5 * 128 + 11 * 128), np.float32)
    bfc[:, 0:128] = np.eye(128)
    bfc[:, 128:256] = 1.0
    bfc[:, 256:384] = 1.0 / 1024
    bfc[:, 384:512] = 1.0 / 256
    bfc[:, 512:640] = 1.0 / 128
    p = np.arange(128)[:, None]
    f = np.arange(128)[None, :]
    for b in range(11):
        r = 3 - b
        if r > 0 or r < -4:
            blk = np.full((128, 128), NEGB, np.float32)
        elif r == 0:
            blk = np.where(p <= f, 0.0, NEGB).astype(np.float32)
        elif r == -4:
            blk = np.where(p > f, 0.0, NEGB).astype(np.float32)
        else:
            blk = np.zeros((128, 128), np.float32)
        bfc[:, 640 + b * 128: 640 + (b + 1) * 128] = blk
    c["c_bfc"] = bfc
    E = (np.arange(64)[:, None] == (np.arange(T)[None, :] // 64)).astype(np.float32)
    c["c_e"] = E
    t = np.arange(T, dtype=np.float32)
    inv_m = (1.0 / (THETA ** (np.arange(0, 32, 2, dtype=np.float32) / 32))).astype(np.float32)
    ang_m = t[None, :] * inv_m[:, None]
    cm = np.cos(ang_m).astype(np.float32)
    sm = np.sin(ang_m).astype(np.float32)
    Cm = np.concatenate([cm, cm], 0)
    Sm = np.concatenate([-sm, sm], 0)
    inv_n = (1.0 / (THETA ** (np.arange(0, 16, 2, dtype=np.float32) / 16))).astype(np.float32)
    ang_n = t[None, :] * inv_n[:, None]
    cn = np.cos(ang_n).astype(np.float32)
    sn = np.sin(ang_n).astype(np.float32)
    Cn = np.concatenate([cn, cn, np.ones((48, T), np.float32)], 0)
    Sn = np.concatenate([-sn, sn, np.zeros((48, T), np.float32)], 0)
    rope = np.zeros((4, 128, T), np.float32)
    rope[0, 0:96] = np.tile(Cm, (3, 1))
    rope[1, 0:96] = np.tile(Sm, (3, 1))
    rope[2] = np.tile(Cn, (2, 1))
    rope[3] = np.tile(Sn, (2, 1))
    c["c_rope"] = rope
    m = np.floor((np.arange(128) - 15) / 16.0)[:, None]
    x = np.arange(512)[None, :] - 248
    c["c_stripc"] = np.where(x <= m, 0.0, NEGB).astype(np.float32)
    hp = (np.arange(128) >= 64).astype(np.int64)[:, None]
    rel = np.arange(128)[None, :] - 62
    sj = np.zeros((128, 128), np.float32)
    sj = np.where(rel == hp, 2e30, sj)
    sj = np.where(rel == hp - 1, 4e30, sj)
    sj = np.where(rel > hp, -1e30, sj)
    c["c_stripj"] = sj.astype(np.float32)
    sel = np.zeros((32, 12, 128), np.float32)
    for br in range(3):
        for i in range(4):
            sel[br * 8 + i, br * 4 + i, 0:64] = 1.0
            sel[br * 8 + 4 + i, br * 4 + i, 64:128] = 1.0
    c["c_selg"] = sel.reshape(32, 12 * 128)
    return c


def prep_weights(inp):
    L = DEPTH
    w = {}
    g = lambda k: np.asarray(inp[k], dtype=np.float32)
    w["ffn_wg"] = np.ascontiguousarray(np.stack([g("ffn1_wg"), g("ffn2_wg")], 1))
    w["ffn_wu"] = np.ascontiguousarray(np.stack([g("ffn1_wu"), g("ffn2_wu")], 1))
    w["ffn_wd"] = np.ascontiguousarray(np.stack([g("ffn1_wd"), g("ffn2_wd")], 1))
    win = g("w_in")
    o_cq, o_ckv, o_kpe, o_qn, o_kvn, o_gn, o_gm, o_gnn = 0, 256, 384, 416, 928, 1696, 1720, 2744
    sw32 = np.concatenate([np.arange(16, 32), np.arange(0, 16)])
    kpe = win[:, :, o_kpe:o_kpe + 32]
    kpe_sw = kpe[:, :, sw32]
    w["w_a"] = np.ascontiguousarray(np.concatenate(
        [win[:, :, o_cq:o_cq + 384], np.tile(kpe, (1, 1, 3)), np.tile(kpe_sw, (1, 1, 3))], 2))
    sw64 = np.concatenate([np.arange(8, 16), np.arange(0, 8), np.arange(16, 64)])
    qn = win[:, :, o_qn:o_qn + 512].reshape(L, D, 8, 64)
    qn_sw = qn[:, :, :, sw64]
    pair = [0, 4, 1, 5, 2, 6, 3, 7]
    w["w_q"] = np.ascontiguousarray(np.concatenate(
        [qn[:, :, pair].reshape(L, D, 512), qn_sw[:, :, pair].reshape(L, D, 512)], 2))
    kvn = win[:, :, o_kvn:o_kvn + 768].reshape(L, D, 6, 2, 64)
    kvn_sw = kvn[:, :, :, :, sw64]
    f2 = lambda a: a.reshape(L, D, 128)
    w["w_k"] = np.ascontiguousarray(np.concatenate(
        [f2(kvn[:, :, 0]), f2(kvn[:, :, 2]), f2(kvn[:, :, 4]), f2(kvn[:, :, 1]),
         f2(kvn_sw[:, :, 0]), f2(kvn_sw[:, :, 2]), f2(kvn_sw[:, :, 4])], 2))
    gn = win[:, :, o_gn:o_gn + 24].reshape(L, D, 8, 3).transpose(0, 1, 3, 2).reshape(L, D, 24)
    w["w_v"] = np.ascontiguousarray(np.concatenate(
        [f2(kvn[:, :, 3]), f2(kvn[:, :, 5]), gn, np.zeros((L, D, 8), np.float32)], 2))
    w["w_gm"] = np.ascontiguousarray(win[:, :, o_gm:o_gm + 1024])
    w["w_gn"] = np.ascontiguousarray(win[:, :, o_gnn:o_gnn + 1024])
    wuq = g("w_uq").reshape(L, 256, 8, 96)
    nope = wuq[:, :, :, 0:64].reshape(L, 256, 512)
    rp = wuq[:, :, :, 64:96]
    rp_sw = rp[:, :, :, sw32]
    hsel = [0, 1, 2, 3, 4, 5, 6, 7, 7]
    w["w_uq"] = np.ascontiguousarray(np.concatenate(
        [nope, rp[:, :, hsel].reshape(L, 256, 288), rp_sw[:, :, hsel].reshape(L, 256, 288)], 2))
    wukv = g("w_ukv").reshape(L, 128, 8, 128)
    wuk = wukv[:, :, :, 0:64]
    w["w_ukt"] = np.ascontiguousarray(
        wuk.reshape(L, 128, 4, 2, 64).transpose(0, 3, 4, 2, 1).reshape(L, 128, 4 * 128))
    w["w_uv"] = np.ascontiguousarray(wukv[:, :, :, 64:128].reshape(L, 128, 512))
    for nm, k1 in (("w_c1k", "cmp_k_w1"), ("w_c1v", "cmp_v_w1")):
        a = g(k1).reshape(L, 32, 64, 128).transpose(0, 2, 1, 3)
        w[nm] = np.ascontiguousarray(np.concatenate([a, a], 1).reshape(L, 128, 32 * 128))
    pek = g("cmp_pe_k").transpose(0, 2, 1)
    pev = g("cmp_pe_v").transpose(0, 2, 1)
    w["w_cpe"] = np.ascontiguousarray(np.concatenate(
        [np.concatenate([pek, pek], 1), np.concatenate([pev, pev], 1)], 2))
    w["w_cb"] = np.ascontiguousarray(np.stack([g("cmp_k_b1"), g("cmp_v_b1")], 2))
    w["w_c2"] = np.ascontiguousarray(np.concatenate([g("cmp_k_w2"), g("cmp_v_w2")], 2))
    w["w_pm"] = g("w_proj_mla")
    wpn = g("w_proj_nsa").reshape(L, 8, 64, 1024)
    w["w_pn"] = np.ascontiguousarray(wpn[:, pair].reshape(L, 512, 1024))
    w["w_out"] = g("w_out")
    lnp = np.stack([g("ln_f1_g"), g("ln_f1_b"), g("ln_mix_g"), g("ln_mix_b"), g("ln_f2_g"), g("ln_f2_b")], 1)
    w["lnp"] = np.ascontiguousarray(lnp.reshape(L, 6, 8, 128).transpose(0, 3, 1, 2).reshape(L, 128, 48))
    w["qng"] = np.ascontiguousarray(g("q_norm_g").reshape(L, 2, 128).transpose(0, 2, 1))
    w["kvng"] = np.ascontiguousarray(g("kv_norm_g").reshape(L, 1, 128).transpose(0, 2, 1))
    return w


class Builder:
    def __init__(self, T, nl=DEPTH, stop=0):
        self.stop = stop
        self.T = T
        self.NL = nl
        self.NCH = T // 512
        self.NT = T // 128
        self.NS = T // 64
        self.NCC = T // 16

    def mm(self, out, lhsT, rhs, start, stop, reads, wb):
        self.P.add("pe", lambda e: e.matmul(out, lhsT, rhs, start=start, stop=stop), reads=reads, writes=[wb])

    def tr(self, out, in_, ident, reads, wb):
        self.P.add("pe", lambda e: e.transpose(out, in_, ident), reads=reads, writes=[wb])

    def act(self, out, in_, func, reads, writes, bias=0.0, scale=1.0, accum_out=None):
        if accum_out is None:
            self.P.add("act", lambda e: e.activation(out=out, in_=in_, func=func, bias=bias, scale=scale),
                       reads=reads, writes=writes)
        else:
            self.P.add("act", lambda e: e.activation(out=out, in_=in_, func=func, bias=bias, scale=scale,
                                                     accum_out=accum_out), reads=reads, writes=writes)

    def tt(self, out, in0, in1, op, reads, writes, eng="dve"):
        self.P.add(eng, lambda e: e.tensor_tensor(out=out, in0=in0, in1=in1, op=op), reads=reads, writes=writes)

    def ts(self, out, in0, s1, op0, reads, writes, s2=None, op1=None, eng="dve"):
        if op1 is None:
            self.P.add(eng, lambda e: e.tensor_scalar(out=out, in0=in0, scalar1=s1, scalar2=None, op0=op0),
                       reads=reads, writes=writes)
        else:
            self.P.add(eng, lambda e: e.tensor_scalar(out=out, in0=in0, scalar1=s1, scalar2=s2, op0=op0, op1=op1),
                       reads=reads, writes=writes)

    def stt(self, out, in0, scalar, in1, op0, op1, reads, writes, eng="dve"):
        self.P.add(eng, lambda e: e.scalar_tensor_tensor(out=out, in0=in0, scalar=scalar, in1=in1, op0=op0, op1=op1),
                   reads=reads, writes=writes)

    def cp(self, out, in_, reads, writes, eng="dve"):
        if eng == "act":
            self.P.add("act", lambda e: e.activation(out=out, in_=in_, func=AF.Copy), reads=reads, writes=writes)
        else:
            self.P.add(eng, lambda e: e.tensor_copy(out=out, in_=in_), reads=reads, writes=writes)

    def memset(self, ap, val, writes, eng="dve"):
        self.P.add(eng, lambda e: e.memset(ap, val), writes=writes)

    def recip(self, out, in_, reads, writes):
        self.P.add("dve", lambda e: e.reciprocal(out=out, in_=in_), reads=reads, writes=writes)

    def bank(self, pool):
        lst = self.pools[pool]
        i = self.pool_i[pool]
        self.pool_i[pool] = (i + 1) % len(lst)
        return lst[i]

    def wtile(self, parts):
        if self.P.dry:
            self.wlist.append(parts)
            return None, None
        i = self.wi
        self.wi += 1
        while self.wissued < min(i + 1 + PF, len(self.wlist)):
            j = self.wissued
            st, sb = self.slots[j % NSLOT]
            lj = j // (self.NCH * self.TPL)
            tj = j % self.TPL
            used = max(off + a * b for (off, a, b, src) in self.wlist[j])
            srcap = self.w16[lj * self.TPL + tj, :, 0:used]
            self.P.dma("sp", (lambda d, s_: (lambda e: e.dma_start(out=d, in_=s_)))(st[:, 0:used], srcap), sb,
                       reads=[self.B_w16[lj][tj]], writes=[sb])
            self.wissued += 1
        return self.slots[i % NSLOT]

    def convert_weights(self):
        for l in range(self.NL):
            for t in range(self.TPL):
                parts = self.wlist[l * self.NCH * self.TPL + t]
                for pi, (off, a, b, src) in enumerate(parts):
                    dst = self.w16[l * self.TPL + t, :, off:off + a * b].rearrange("p (a b) -> p a b", a=a)
                    wr = [self.B_w16[l][t]] if pi == len(parts) - 1 else []
                    rd = [self.B_w16[l - 1][t]] if (l > 0 and pi == 0) else []
                    self.P.dma("pool", (lambda d, s_: (lambda e: e.dma_start(out=d, in_=s_)))(dst, src),
                               self.B_convsem[t], reads=rd, writes=wr)

    def wview(self, st, off, a, b, rows=128):
        return st[0:rows, off:off + a * b].rearrange("p (a b) -> p a b", a=a)

    def build(self):
        nc = bass.Bass("TRN2", target_bir_lowering=False)
        self.nc = nc
        T, NL = self.T, self.NL
        dt = nc.dram_tensor
        dr = {}
        dr["x"] = dt("x", [T, D], F32, kind="ExternalInput").ap()
        dr["y"] = dt("y", [T, D], F32, kind="ExternalOutput").ap()
        dr["scr"] = dt("scr", [T, D], F32, kind="Internal").ap()
        shapes = {
            "ffn_wg": [2, 2, D, DFF], "ffn_wu": [2, 2, D, DFF], "ffn_wd": [2, 2, DFF, D],
            "w_a": [2, D, 576], "w_q": [2, D, 1024], "w_k": [2, D, 896], "w_v": [2, D, 288],
            "w_gm": [2, D, 1024], "w_gn": [2, D, 1024], "w_uq": [2, 256, 1088], "w_ukt": [2, 128, 512],
            "w_uv": [2, 128, 512], "w_c1k": [2, 128, 4096], "w_c1v": [2, 128, 4096], "w_cpe": [2, 128, 64],
            "w_cb": [2, 128, 2], "w_c2": [2, 128, 128], "w_pm": [2, 512, 1024], "w_pn": [2, 512, 1024],
            "w_out": [2, D, 1024], "lnp": [2, 128, 48], "qng": [2, 128, 2], "kvng": [2, 128, 1],
            "c_identf": [128, 128], "c_bfc": [128, 2048], "c_e": [64, T], "c_rope": [4, 128, T],
            "c_stripc": [128, 512], "c_stripj": [128, 128], "c_selg": [32, 1536],
        }
        for k, s in shapes.items():
            dr[k] = dt(k, s, F32, kind="ExternalInput").ap()
        self.dr = dr
        self.in_names = ["x"] + list(shapes.keys())

        with ExitStack() as es:
            def sb(name, shape, dtype):
                return es.enter_context(nc.sbuf_tensor("s_" + name, shape, dtype))

            self.sb = sb
            banks = []
            for i in range(8):
                t_ = es.enter_context(nc.psum_tensor("ps%d" % i, [128, 512], F32))
                banks.append((t_, Buf("ps%d" % i, x=True)))
            self.pools = {"S": banks[0:3], "A": banks[3:5], "G": banks[5:8]}
            self.pool_i = {"S": 0, "A": 0, "G": 0}
            self.slots = [(sb("wslot%d" % i, [128, SLOT], BF16), Buf("wslot%d" % i)) for i in range(NSLOT)]
            self.alloc_persistent()
            self.wlist = []
            self.P = Prog(nc, dry=True)
            self.program()
            self.TPL = len(self.wlist) // (self.NL * self.NCH)
            assert self.TPL * self.NL * self.NCH == len(self.wlist)
            self.w16 = nc.dram_tensor("w16", [self.NL * self.TPL, 128, SLOT], BF16, kind="Internal").ap()
            self.B_w16 = [[Buf("w16_%d_%d" % (l, t)) for t in range(self.TPL)] for l in range(self.NL)]
            self.B_convsem = [Buf("cv%d" % t) for t in range(self.TPL)]
            self.P = Prog(nc, dry=False)
            self.wi = 0
            self.wissued = 0
            self.pool_i = {"S": 0, "A": 0, "G": 0}
            self.program()
            finals = [("d", (self.B_out, "hw"), self.P.dma_cnt[(self.B_out, "hw")])]
            self.P.emit(final_waits=finals)
        return nc

    def alloc_persistent(self):
        sb, T = self.sb, self.T
        NT, NCC = self.NT, self.NCC
        self.identf = sb("identf", [128, 128], F32)
        self.bfc = sb("bfc", [128, 2048], BF16)
        self.stripc = sb("stripc", [128, 512], F32)
        self.stripj = sb("stripj", [128, 128], F32)
        self.selg = sb("selg", [32, 1536], F32)
        self.B_const = Buf("const")
        self.lnp = sb("lnp", [128, 48], F32)
        self.qng = sb("qng", [128, 2], F32)
        self.kvng = sb("kvng", [128, 1], F32)
        self.cpe = sb("cpe", [128, 64], BF16)
        self.cb = sb("cb", [128, 2], F32)
        self.c2 = sb("c2", [128, 128], BF16)
        self.ukt = sb("ukt", [128, 512], BF16)
        self.uv = sb("uv", [128, 512], BF16)
        self.pb = sb("pb", [128, 2], F32)
        self.B_lp = Buf("layerparams")
        self.B_pb = Buf("pb")
        self.ckvT = sb("ckvT", [128, T], BF16)
        self.ckvtok = sb("ckvtok", [128, NT, 128], BF16)
        self.kpeT = sb("kpeT", [96, T], BF16)
        self.ksaug = sb("ksaug", [128, 2, T], BF16)
        self.kwT = sb("kwT", [128, T], BF16)
        self.Vs = sb("Vs", [128, NT, 192], BF16)
        self.Vw = sb("Vw", [128, NT, 192], BF16)
        self.kcT = sb("kcT", [128, NCC], BF16)
        self.vc = sb("vc", [128, max(1, NCC // 128), 2, 64], BF16)
        self.kch = sb("kch", [128, 528], BF16)
        self.vch = sb("vch", [128, 528], BF16)
        self.B_cache = [Buf("cache%d" % c) for c in range(self.NCH)]
        self.B_hist = Buf("hist")
        self.B_vc = Buf("vc")
        self.B_kc = Buf("kc")
        self.R1 = sb("R1", [128, 8, 512], F32)
        self.R2 = sb("R2", [128, 8, 512], F32)
        self.R3 = sb("R3", [128, 8, 512], BF16)
        self.R4 = sb("R4", [128, 22 * 512], BF16)
        self.B_R1, self.B_R2, self.B_R3, self.B_R4 = Buf("R1"), Buf("R2"), Buf("R3"), Buf("R4")
        self.B_xtok = Buf("xtok")
        self.B_oc, self.B_osb, self.B_owb, self.B_acc = Buf("oc"), Buf("osb"), Buf("owb"), Buf("acc")
        self.fence_t = sb("fence_t", [128, 2], F32)
        self.pt = [(sb("pt%d" % i, [128, 512], BF16), Buf("pt%d" % i)) for i in range(3)]
        self.pt_i = 0
        self.rope = sb("rope", [128, 4, 512], F32)
        self.B_rope = Buf("rope")
        self.tmpf = [(sb("tmpf%d" % i, [128, 512], F32), Buf("tmpf%d" % i)) for i in range(4)]
        self.tmpf_i = 0
        self.tmpb = [(sb("tmpb%d" % i, [128, 512], BF16), Buf("tmpb%d" % i)) for i in range(3)]
        self.tmpb_i = 0
        r2b = self.R2[:].rearrange("p a f -> p (a f)").bitcast(BF16)
        self.cqn = r2b[:, 0:1024].rearrange("p (a t) -> p a t", a=2)
        self.qnope = r2b[:, 1024:3072].rearrange("p (a t) -> p a t", a=4)
        self.B_cqn = self.B_R2
        self.B_qnope = self.B_R2
        self.qpe = sb("qpe", [96, 3, 512], BF16)
        self.B_qpe = Buf("qpe")
        self.gsig = sb("gsig", [32, 512], F32)
        self.B_gsig = Buf("gsig")
        self.omT = sb("omT", [128, 4, 512], BF16)
        self.onT = sb("onT", [128, 4, 512], BF16)
        self.B_omT, self.B_onT = Buf("omT"), Buf("onT")
        W4 = NCC + 4
        self.sbias = [(self.R2[:, 7, i * 256:(i + 1) * 256], Buf("sbias%d" % i)) for i in range(2)]
        self.pexp = [(self.R2[:, 4, 0:256], self.B_osb), (self.R2[:, 5, 0:256], self.B_owb)]
        self.pnb = [(sb("pnb%d" % i, [128, max(128, NCC)], BF16), Buf("pnb%d" % i)) for i in range(2)]
        self.pTt = [(sb("pTt%d" % i, [128, max(1, NCC // 128), 128], BF16), Buf("pTt%d" % i)) for i in range(2)]
        self.p4 = self.R2[:, 6, 0:W4]
        self.B_p4 = self.B_acc
        self.small = [(sb("small%d" % i, [128, 4], F32), Buf("small%d" % i)) for i in range(4)]
        self.small_i = 0
        self.cmp_i = 0
        self.imp = sb("imp", [128, 64], F32)
        self.imp2 = sb("imp2", [128, 64], F32)
        self.m8 = sb("m8", [128, 16], F32)
        self.nsbs = [(sb("nsb%d" % i, [128, 64], BF16), Buf("nsb%d" % i)) for i in range(2)]
        self.B_imp = Buf("imp")
        self.hidt = sb("hidt", [128, 4, 32], F32)
        self.hidb = sb("hidb", [128, 32], BF16)
        self.hidpad = sb("hidpad", [128, 128], BF16)
        self.B_hid = Buf("hid")
        self.B_out = Buf("out")
        self.B_scr = [Buf("scr%d" % c) for c in range(self.NCH)]

    def tf(self):
        r = self.tmpf[self.tmpf_i]
        self.tmpf_i = (self.tmpf_i + 1) % len(self.tmpf)
        return r

    def tb(self):
        r = self.tmpb[self.tmpb_i]
        self.tmpb_i = (self.tmpb_i + 1) % len(self.tmpb)
        return r

    def ptile(self):
        r = self.pt[self.pt_i]
        self.pt_i = (self.pt_i + 1) % len(self.pt)
        return r

    def sm(self):
        r = self.small[self.small_i]
        self.small_i = (self.small_i + 1) % len(self.small)
        return r

    def program(self):
        P, dr = self.P, self.dr
        self.tmpf_i = self.tmpb_i = self.pt_i = self.small_i = self.cmp_i = 0
        Bc = self.B_const
        P.dma("sp", lambda e: e.dma_start(out=self.identf[:], in_=dr["c_identf"]), Bc, writes=[Bc])
        P.dma("pool", lambda e: e.dma_start(out=self.bfc[:], in_=dr["c_bfc"]), Bc, writes=[Bc])
        import os as _os
        KSKIP = _os.environ.get("KSKIP", "")
        self.KSKIP = KSKIP
        if "c" not in KSKIP:
            P.dma("sp", lambda e: e.dma_start(out=self.stripc[:], in_=dr["c_stripc"]), Bc, writes=[Bc])
            P.dma("sp", lambda e: e.dma_start(out=self.stripj[:], in_=dr["c_stripj"]), Bc, writes=[Bc])
            P.dma("sp", lambda e: e.dma_start(out=self.selg[:], in_=dr["c_selg"]), Bc, writes=[Bc])
        Bc0 = self.B_cache[0]
        if "e" not in KSKIP:
            P.dma("pool", lambda e: e.dma_start(out=self.ksaug[64:128, 0, :], in_=dr["c_e"]), Bc, writes=[Bc])
            P.dma("pool", lambda e: e.dma_start(out=self.ksaug[0:64, 1, :], in_=dr["c_e"]), Bc, writes=[Bc])
        if "v" not in KSKIP:
            self.memset(self.Vs[:, :, 64:128], 1.0, [Bc], eng="pool")
            self.memset(self.Vw[:, :, 64:128], 1.0, [Bc], eng="pool")
        self.identb = self.bfc[:, 0:128]
        self.onesb = self.bfc[:, 128:256]
        self.on1024 = self.bfc[:, 256:384]
        self.on256 = self.bfc[:, 384:512]
        self.on128 = self.bfc[:, 512:640]
        self.LONG = 640
        if not P.dry:
            self.convert_weights()
        for l in range(self.NL):
            src = dr["x"] if l == 0 else dr["scr"]
            dst = dr["y"] if l == self.NL - 1 else dr["scr"]
            self.layer(l, src, dst, l == 0, l == self.NL - 1)

    def load_xtok(self, c, src, first_layer):
        P = self.P
        xtok = self.R4.bitcast(F32) if False else None
        xt = self.xtok_view()
        rd = [] if first_layer else [self.B_scr[c]]
        P.dma("sp", lambda e: e.dma_start(out=xt, in_=src[c * 512:(c + 1) * 512, :].rearrange("(a p) d -> p a d", p=128)),
              self.B_xtok, reads=rd, writes=[self.B_R4])

    def xtok_view(self):
        return self.R4[:, 0:8192].bitcast(F32).rearrange("p (a d) -> p a d", a=4)

    def layer(self, l, src, dst, first_layer, last_layer):
        P, dr = self.P, self.dr
        Blp = self.B_lp
        KSKIP = self.KSKIP
        if "l" not in KSKIP:
            P.dma("sp", lambda e: e.dma_start(out=self.lnp[:], in_=dr["lnp"][l]), Blp, writes=[Blp])
            P.dma("sp", lambda e: e.dma_start(out=self.qng[:], in_=dr["qng"][l]), Blp, writes=[Blp])
            P.dma("sp", lambda e: e.dma_start(out=self.kvng[:], in_=dr["kvng"][l]), Blp, writes=[Blp])
            P.dma("sp", lambda e: e.dma_start(out=self.cb[:], in_=dr["w_cb"][l]), Blp, writes=[Blp])
        if "p" not in KSKIP:
            P.dma("pool", lambda e: e.dma_start(out=self.cpe[:], in_=dr["w_cpe"][l]), Blp, writes=[Blp])
            P.dma("pool", lambda e: e.dma_start(out=self.c2[:], in_=dr["w_c2"][l]), Blp, writes=[Blp])
            P.dma("pool", lambda e: e.dma_start(out=self.ukt[:], in_=dr["w_ukt"][l]), Blp, writes=[Blp])
            P.dma("pool", lambda e: e.dma_start(out=self.uv[:], in_=dr["w_uv"][l]), Blp, writes=[Blp])
        if "m" not in KSKIP:
            self.memset(self.vc[:], 0.0, [self.B_vc], eng="pool")
            self.memset(self.kcT[:], 0.0, [self.B_kc], eng="pool")
            self.memset(self.kch[:], 0.0, [self.B_hist], eng="pool")
            self.memset(self.vch[:], 0.0, [self.B_hist], eng="pool")
            self.memset(self.hidpad[:], 0.0, [self.B_hid], eng="pool")
        self.load_xtok(0, src, first_layer)
        for c in range(self.NCH):
            self.chunk(l, c, src, dst, first_layer, last_layer)

    def chunk(self, l, c, src, dst, first_layer, last_layer):
        P = self.P
        R1, R2, R3 = self.R1, self.R2, self.R3
        B1, B2, B3, B4 = self.B_R1, self.B_R2, self.B_R3, self.B_R4
        xt = self.xtok_view()
        for k in range(8):
            pt_, pb_ = self.bank("G")
            for tt_ in range(4):
                self.tr(pt_[:, tt_ * 128:(tt_ + 1) * 128], xt[:, tt_, k * 128:(k + 1) * 128], self.identf[:],
                        [B4, self.B_const], pb_)
            self.act(R1[:, k, :], pt_[:], AF.Identity, [pb_], [B1], scale=ALPHA)
            self.cp(R3[:, k, :], pt_[:], [pb_], [B3])
        import os as _os
        _ks = _os.environ.get("KSTAGE", "")
        if _ks == "A":
            for k in range(8):
                self.cp(R2[:, k, :], R1[:, k, :], [B1], [B2])
        else:
            self.ffn(l, 0)
            if _ks != "F":
                self.layernorm(l, 0)
        if self.stop != 1:
            self.mixer(l, c)
            self.layernorm(l, 1)
        if self.stop == 0:
            self.ffn(l, 1)
        if c + 1 < self.NCH:
            self.load_xtok(c + 1, src, first_layer)
        if self.stop == 0:
            self.layernorm(l, 2, final=True)
        st = R1[:].rearrange("p a f -> p (a f)").rearrange("p (a d) -> p a d", a=4)
        for tt_ in range(4):
            for k2 in range(2):
                pt_, pb_ = self.bank("G")
                for kk in range(4):
                    k = k2 * 4 + kk
                    self.tr(pt_[:, kk * 128:(kk + 1) * 128], R2[:, k, tt_ * 128:(tt_ + 1) * 128], self.identf[:],
                            [B2, self.B_const], pb_)
                if k2 == 0:
                    self.cp(st[:, tt_, 0:512], pt_[:], [pb_], [B1], eng="act")
                else:
                    self.cp(st[:, tt_, 512:1024], pt_[:], [pb_], [B1])
        wb = self.B_out if last_layer else self.B_scr[c]
        P.dma("sp", lambda e: e.dma_start(out=dst[c * 512:(c + 1) * 512, :].rearrange("(a p) d -> p a d", p=128), in_=st),
              self.B_out, reads=[B1], writes=[wb])

    def ffn(self, l, which):
        dr = self.dr
        R1, R2, R3, R4 = self.R1, self.R2, self.R3, self.R4
        B1, B2, B3, B4 = self.B_R1, self.B_R2, self.B_R3, self.B_R4
        hT = R4[:].rearrange("p (j t) -> p j t", j=NFC)
        wg, wu, wd = dr["ffn_wg"][l, which], dr["ffn_wu"][l, which], dr["ffn_wd"][l, which]
        for fb in range(NFC // 2):
            f0 = fb * 256
            st, sbuf_ = self.wtile([
                (0, 8, 256, wg[:, f0:f0 + 256].rearrange("(k p) f -> p k f", p=128)),
                (2048, 8, 256, wu[:, f0:f0 + 256].rearrange("(k p) f -> p k f", p=128)),
            ])
            if st is None:
                continue
            wgv = self.wview(st, 0, 8, 256)
            wuv = self.wview(st, 2048, 8, 256)
            for fc in range(2):
                j = fb * 2 + fc
                pg, bg = self.bank("S")
                pu, bu = self.bank("G")
                for k in range(8):
                    self.mm(pg[:], wgv[:, k, fc * 128:(fc + 1) * 128], R3[:, k, :], k == 0, k == 7, [sbuf_, B3], bg)
                for k in range(8):
                    self.mm(pu[:], wuv[:, k, fc * 128:(fc + 1) * 128], R3[:, k, :], k == 0, k == 7, [sbuf_, B3], bu)
                t_, tb_ = self.tf()
                self.act(t_[:], pg[:], AF.Silu, [bg], [tb_])
                self.tt(hT[:, j, :], t_[:], pu[:], ALU.mult, [tb_, bu], [B4])
        for m in range(8):
            st, sbuf_ = self.wtile([(0, NFC, 128, wd[:, m * 128:(m + 1) * 128].rearrange("(j p) d -> p j d", p=128))])
            if st is None:
                continue
            wdv = self.wview(st, 0, NFC, 128)
            po, bo = self.bank("A")
            for j in range(NFC):
                self.mm(po[:], wdv[:, j, :], hT[:, j, :], j == 0, j == NFC - 1, [sbuf_, B4], bo)
            self.stt(R2[:, m, :], po[:], 0.5, R1[:, m, :], ALU.mult, ALU.add, [bo, B1], [B2])

    def layernorm(self, l, idx, final=False):
        R1, R2, R3, R4 = self.R1, self.R2, self.R3, self.R4
        B1, B2, B3, B4 = self.B_R1, self.B_R2, self.B_R3, self.B_R4
        pm, bm = self.bank("G")
        pq, bq = self.bank("G")
        if not final:
            zsq = R4[:, 0:4096].rearrange("p (k t) -> p k t", k=8)
            for k in range(8):
                self.cp(R3[:, k, :], R2[:, k, :], [B2], [B3])
                self.act(zsq[:, k, :], R2[:, k, :], AF.Square, [B2], [B4])
            for k in range(8):
                self.mm(pm[:], self.on1024, R3[:, k, :], k == 0, k == 7, [B3, self.B_const], bm)
            for k in range(8):
                self.mm(pq[:], self.on1024, zsq[:, k, :], k == 0, k == 7, [B4, self.B_const], bq)
        else:
            for k in range(8):
                self.cp(R3[:, k, :], R2[:, k, :], [B2], [B3])
            for k in range(8):
                self.mm(pm[:], self.on1024, R3[:, k, :], k == 0, k == 7, [B3, self.B_const], bm)
            for k in range(8):
                self.act(R3[:, k, :], R2[:, k, :], AF.Square, [B2], [B3])
            for k in range(8):
                self.mm(pq[:], self.on1024, R3[:, k, :], k == 0, k == 7, [B3, self.B_const], bq)
        mean, bmean = self.tf()
        m2, bm2 = self.tf()
        rstd, brstd = self.tf()
        nmr, bnmr = self.tf()
        self.cp(mean[:], pm[:], [bm], [bmean], eng="act")
        self.tt(m2[:], mean[:], mean[:], ALU.mult, [bmean], [bm2])
        self.tt(m2[:], pq[:], m2[:], ALU.subtract, [bq, bm2], [bm2])
        self.ts(m2[:], m2[:], LN_EPS, ALU.add, [bm2], [bm2])
        self.act(m2[:], m2[:], AF.Sqrt, [bm2], [bm2])
        self.recip(rstd[:], m2[:], [bm2], [brstd])
        self.stt(nmr[:], mean[:], -1.0, rstd[:], ALU.mult, ALU.mult, [bmean, brstd], [bnmr])
        gcol = self.lnp[:, idx * 16:idx * 16 + 8]
        bcol = self.lnp[:, idx * 16 + 8:idx * 16 + 16]
        for k in range(8):
            self.tt(R2[:, k, :], R2[:, k, :], rstd[:], ALU.mult, [B2, brstd], [B2])
            self.tt(R2[:, k, :], R2[:, k, :], nmr[:], ALU.add, [B2, bnmr], [B2])
            self.act(R2[:, k, :], R2[:, k, :], AF.Identity, [B2, self.B_lp], [B2], bias=bcol[:, k:k + 1], scale=gcol[:, k:k + 1])
            if not final:
                self.cp(R3[:, k, :], R2[:, k, :], [B2], [B3])
                self.act(R1[:, k, :], R2[:, k, :], AF.Identity, [B2], [B1], scale=ALPHA)

    def rmsnorm_fm(self, ps_list, ones_ap, gcols, outs, out_reads_writes):
        nk = len(ps_list)
        raws = []
        sqs = []
        for i, (pa, pb_) in enumerate(ps_list):
            r_, rb_ = self.tf()
            self.cp(r_[:], pa, [pb_], [rb_], eng="act")
            s_, sb_ = self.tb()
            self.act(s_[:], pa, AF.Square, [pb_], [sb_])
            raws.append((r_, rb_))
            sqs.append((s_, sb_))
        pss, bss = self.bank("G")
        for i, (s_, sb_) in enumerate(sqs):
            self.mm(pss[:], ones_ap, s_[:], i == 0, i == nk - 1, [sb_, self.B_const], bss)
        rq, brq = self.tf()
        self.act(rq[:], pss[:], AF.Sqrt, [bss], [brq], bias=RMS_EPS)
        self.recip(rq[:], rq[:], [brq], [brq])
        for i, (r_, rb_) in enumerate(raws):
            o_ap, o_w = outs[i]
            self.stt(o_ap, r_[:], gcols[i], rq[:], ALU.mult, ALU.mult, [rb_, brq, self.B_lp], o_w)

    def rope_apply(self, ps_main, b_main, ps_sw, b_sw, ctab, stab, rows, outs):
        t1, b1_ = self.tf()
        t2, b2_ = self.tf()
        self.tt(t1[0:rows, :], ps_main, ctab, ALU.mult, [b_main, self.B_rope], [b1_])
        self.tt(t2[0:rows, :], ps_sw, stab, ALU.mult, [b_sw, self.B_rope], [b2_])
        for (o_ap, r0, r1, wr) in outs:
            self.tt(o_ap, t1[r0:r1, :], t2[r0:r1, :], ALU.add, [b1_, b2_], wr)

    def mixer(self, l, c):
        P, dr = self.P, self.dr
        T = self.T
        R1, R2, R3, R4 = self.R1, self.R2, self.R3, self.R4
        B1, B2, B3, B4 = self.B_R1, self.B_R2, self.B_R3, self.B_R4
        Bc = self.B_cache[c]
        t0 = c * 512
        cs = slice(t0, t0 + 512)
        qabs = R4[:, 0:4096].rearrange("p (h t) -> p h t", h=8)
        qaug = R4[:, 4096:8192].rearrange("p (h t) -> p h t", h=8)
        qnw = None
        P.dma("sp", lambda e: e.dma_start(out=self.rope[:], in_=dr["c_rope"][:, :, t0:t0 + 512].rearrange("a p t -> p a t")),
              self.B_rope, writes=[self.B_rope])
        Cm, Sm, Cn, Sn = (self.rope[:, i, :] for i in range(4))

        st, sw_ = self.wtile([(0, 8, 384, dr["w_a"][l][:, 0:384].rearrange("(k p) f -> p k f", p=128))])
        if st is not None:
            wa = self.wview(st, 0, 8, 384)
            pl = []
            for i in range(2):
                p_, b_ = self.bank("G")
                for k in range(8):
                    self.mm(p_[:], wa[:, k, i * 128:(i + 1) * 128], R3[:, k, :], k == 0, k == 7, [sw_, B3], b_)
                pl.append((p_[:], b_))
            self.rmsnorm_fm(pl, self.on256, [self.qng[:, 0:1], self.qng[:, 1:2]],
                            [(self.cqn[:, 0, :], [self.B_cqn]), (self.cqn[:, 1, :], [self.B_cqn])], None)
            p_, b_ = self.bank("G")
            for k in range(8):
                self.mm(p_[:], wa[:, k, 256:384], R3[:, k, :], k == 0, k == 7, [sw_, B3], b_)
            self.rmsnorm_fm([(p_[:], b_)], self.on128, [self.kvng[:, 0:1]], [(self.ckvT[:, cs], [Bc])], None)
            pt_, pb_ = self.bank("G")
            ptb = pt_[:].bitcast(BF16)
            for tt_ in range(4):
                self.tr(ptb[:, tt_ * 128:(tt_ + 1) * 128], self.ckvT[:, t0 + tt_ * 128:t0 + (tt_ + 1) * 128], self.identb,
                        [Bc, self.B_const], pb_)
            self.cp(self.ckvtok[:, c * 4:(c + 1) * 4, :], ptb[:, 0:512].rearrange("p (a b) -> p a b", a=4), [pb_], [Bc])
        stp, swp_ = self.wtile([(0, 8, 192, dr["w_a"][l][:, 384:576].rearrange("(k p) f -> p k f", p=128))])
        if stp is not None:
            wap = self.wview(stp, 0, 8, 192)
            p1, b1_ = self.bank("G")
            p2, b2_ = self.bank("G")
            for k in range(8):
                self.mm(p1[0:96, :], wap[:, k, 0:96], R3[:, k, :], k == 0, k == 7, [swp_, B3], b1_)
            for k in range(8):
                self.mm(p2[0:96, :], wap[:, k, 96:192], R3[:, k, :], k == 0, k == 7, [swp_, B3], b2_)
            self.rope_apply(p1[0:96, :], b1_, p2[0:96, :], b2_, Cm[0:96, :], Sm[0:96, :], 96,
                            [(self.kpeT[:, cs], 0, 96, [Bc])])

        st, sw_ = self.wtile([(0, 2, 1088, dr["w_uq"][l].rearrange("(k p) f -> p k f", p=128))])
        if st is not None:
            wq = self.wview(st, 0, 2, 1088)
            for i in range(4):
                p_, b_ = self.bank("G")
                for k in range(2):
                    self.mm(p_[:], wq[:, k, i * 128:(i + 1) * 128], self.cqn[:, k, :], k == 0, k == 1, [sw_, self.B_cqn], b_)
                self.cp(self.qnope[:, i, :], p_[:], [b_], [self.B_qnope], eng=("act" if i % 2 else "dve"))
            uktv = self.ukt[:].rearrange("p (i m) -> p i m", i=4)
            for h in range(8):
                i, hh = h // 2, h % 2
                p_, b_ = self.bank("G")
                self.mm(p_[:], uktv[64 * hh:64 * hh + 64, i, :], self.qnope[64 * hh:64 * hh + 64, i, :], True, True,
                        [self.B_lp, self.B_qnope], b_)
                self.cp(qabs[:, h, :], p_[:], [b_], [B4], eng=("act" if h % 2 else "dve"))
            for j in range(3):
                rows = 96 if j < 2 else 64
                p1, b1_ = self.bank("G")
                p2, b2_ = self.bank("G")
                for k in range(2):
                    self.mm(p1[0:rows, :], wq[:, k, 512 + j * 96:512 + j * 96 + rows], self.cqn[:, k, :], k == 0, k == 1,
                            [sw_, self.B_cqn], b1_)
                for k in range(2):
                    self.mm(p2[0:rows, :], wq[:, k, 800 + j * 96:800 + j * 96 + rows], self.cqn[:, k, :], k == 0, k == 1,
                            [sw_, self.B_cqn], b2_)
                self.rope_apply(p1[0:rows, :], b1_, p2[0:rows, :], b2_, Cm[0:rows, :], Sm[0:rows, :], rows,
                                [(self.qpe[0:rows, j, :], 0, rows, [self.B_qpe])])

        for i in range(4):
            st, sw_ = self.wtile([(0, 8, 128, dr["w_q"][l][:, i * 128:(i + 1) * 128].rearrange("(k p) f -> p k f", p=128)),
                                  (1024, 8, 128, dr["w_q"][l][:, 512 + i * 128:512 + (i + 1) * 128].rearrange("(k p) f -> p k f", p=128))])
            if st is None:
                continue
            wq1 = self.wview(st, 0, 8, 128)
            wq2 = self.wview(st, 1024, 8, 128)
            p1, b1_ = self.bank("G")
            p2, b2_ = self.bank("G")
            for k in range(8):
                self.mm(p1[:], wq1[:, k, :], R3[:, k, :], k == 0, k == 7, [sw_, B3], b1_)
            for k in range(8):
                self.mm(p2[:], wq2[:, k, :], R3[:, k, :], k == 0, k == 7, [sw_, B3], b2_)
            self.rope_apply(p1[:], b1_, p2[:], b2_, Cn, Sn, 128,
                            [(qaug[0:64, i, :], 0, 64, [B4]), (qaug[64:128, 4 + i, :], 64, 128, [B4])])
        for bi in range(3):
            st, sw_ = self.wtile([(0, 8, 128, dr["w_k"][l][:, bi * 128:(bi + 1) * 128].rearrange("(k p) f -> p k f", p=128)),
                                  (1024, 8, 128, dr["w_k"][l][:, 512 + bi * 128:512 + (bi + 1) * 128].rearrange("(k p) f -> p k f", p=128))]
                                 + ([(2048, 8, 128, dr["w_k"][l][:, 384:512].rearrange("(k p) f -> p k f", p=128))] if bi == 0 else []))
            if st is None:
                continue
            wk1 = self.wview(st, 0, 8, 128)
            wk2 = self.wview(st, 1024, 8, 128)
            if bi == 0:
                self.cp(self.kch[:, 0:16], self.kch[:, 512:528], [self.B_hist], [self.B_hist])
                self.cp(self.vch[:, 0:16], self.vch[:, 512:528], [self.B_hist], [self.B_hist])
                wvc = self.wview(st, 2048, 8, 128)
                p1, b1_ = self.bank("G")
                for k in range(8):
                    self.mm(p1[:], wvc[:, k, :], R3[:, k, :], k == 0, k == 7, [sw_, B3], b1_)
                self.cp(self.vch[:, 16:528], p1[:], [b1_], [self.B_hist], eng="act")
            p1, b1_ = self.bank("G")
            p2, b2_ = self.bank("G")
            for k in range(8):
                self.mm(p1[:], wk1[:, k, :], R3[:, k, :], k == 0, k == 7, [sw_, B3], b1_)
            for k in range(8):
                self.mm(p2[:], wk2[:, k, :], R3[:, k, :], k == 0, k == 7, [sw_, B3], b2_)
            if bi == 0:
                outs = [(self.kch[:, 16:528], 0, 128, [self.B_hist])]
            elif bi == 1:
                outs = [(self.ksaug[0:64, 0, cs], 0, 64, [Bc]), (self.ksaug[64:128, 1, cs], 64, 128, [Bc])]
            else:
                outs = [(self.kwT[:, cs], 0, 128, [Bc])]
            self.rope_apply(p1[:], b1_, p2[:], b2_, Cn, Sn, 128, outs)
        st, sw_ = self.wtile([(0, 8, 288, dr["w_v"][l].rearrange("(k p) f -> p k f", p=128))])
        if st is not None:
            wv = self.wview(st, 0, 8, 288)
            for tt_ in range(4):
                p_, b_ = self.bank("G")
                for k in range(8):
                    self.mm(p_[:, 0:256], R3[:, k, tt_ * 128:(tt_ + 1) * 128], wv[:, k, 0:256], k == 0, k == 7, [sw_, B3], b_)
                kt = c * 4 + tt_
                vsrc = p_[:, 0:128].rearrange("p (g d) -> p g d", g=2)
                vdst = self.Vs[:, kt, :].rearrange("p (g d) -> p g d", g=3)[:, 0:3:2, :]
                self.cp(vdst, vsrc, [b_], [Bc])
                vsrc2 = p_[:, 128:256].rearrange("p (g d) -> p g d", g=2)
                vdst2 = self.Vw[:, kt, :].rearrange("p (g d) -> p g d", g=3)[:, 0:3:2, :]
                self.cp(vdst2, vsrc2, [b_], [Bc], eng="act")
            p_, b_ = self.bank("G")
            for k in range(8):
                self.mm(p_[0:32, :], wv[:, k, 256:288], R3[:, k, :], k == 0, k == 7, [sw_, B3], b_)
            self.act(self.gsig[:], p_[0:32, :], AF.Sigmoid, [b_], [self.B_gsig])

        cc0 = 32 * c
        for kv in range(2):
            st, sw_ = self.wtile([(0, 32, 128, dr["w_c1k" if kv == 0 else "w_c1v"][l].rearrange("p (i h) -> p i h", i=32))])
            if st is None:
                continue
            w1 = self.wview(st, 0, 32, 128)
            hist = self.kch if kv == 0 else self.vch
            if c == 0:
                p_, b_ = self.bank("G")
                for i in range(32):
                    self.mm(p_[:, 0:1], w1[0:64, i, :], self.cpe[0:64, kv * 32 + i:kv * 32 + i + 1], i == 0, i == 31,
                            [sw_, self.B_lp], b_)
                self.tt(self.pb[:, kv:kv + 1], p_[:, 0:1], self.cb[:, kv:kv + 1], ALU.add, [b_, self.B_lp], [self.B_pb])
            for g in range(2):
                p_, b_ = self.bank("G")
                for i in range(32):
                    self.mm(p_[:, 0:32], w1[64 * g:64 * g + 64, i, :], hist[64 * g:64 * g + 64, i:i + 497:16],
                            i == 0, i == 31, [sw_, self.B_hist], b_)
                ht = self.hidt
                Bh = self.B_hid
                self.act(ht[:, 0, :], p_[:, 0:32], AF.Identity, [b_, self.B_pb], [Bh], bias=self.pb[:, kv:kv + 1])
                self.tt(ht[:, 1, :], ht[:, 0, :], ht[:, 0, :], ALU.mult, [Bh], [Bh])
                self.ts(ht[:, 1, :], ht[:, 1, :], 0.044715, ALU.mult, [Bh], [Bh], s2=1.0, op1=ALU.add)
                self.tt(ht[:, 1, :], ht[:, 1, :], ht[:, 0, :], ALU.mult, [Bh], [Bh])
                self.act(ht[:, 2, :], ht[:, 1, :], AF.Sigmoid, [Bh], [Bh], scale=1.5957691216057308)
                if kv == 0:
                    self.tt(self.hidb[:], ht[:, 0, :], ht[:, 2, :], ALU.mult, [Bh], [Bh])
                    p2, b2_ = self.bank("G")
                    self.mm(p2[0:64, 0:32], self.c2[:, 0:64], self.hidb[:], True, True, [Bh, self.B_lp], b2_)
                    self.cp(self.kcT[64 * g:64 * g + 64, cc0:cc0 + 32], p2[0:64, 0:32], [b2_], [self.B_kc])
                else:
                    po = 32 * (c % 4)
                    self.tt(self.hidpad[:, po:po + 32], ht[:, 0, :], ht[:, 2, :], ALU.mult, [Bh], [Bh])
                    p2, b2_ = self.bank("G")
                    self.mm(p2[:, 0:64], self.hidpad[:], self.c2[:, 64:128], True, True, [Bh, self.B_lp], b2_)
                    self.tt(self.vc[:, c // 4, g, :], self.vc[:, c // 4, g, :], p2[:, 0:64], ALU.add, [b2_, self.B_vc], [self.B_vc])
                    self.memset(self.hidpad[:, po:po + 32], 0.0, [Bh])

        self.memset(self.fence_t[:, 0:1], 0.0, [B2, self.B_oc, self.B_osb, self.B_owb, self.B_acc, self.sbias[0][1], self.sbias[1][1]])
        self.compressed(l, c, qnw, qaug)

        self.attention(l, c, qabs, qaug, qnw)

        self.memset(self.fence_t[:, 1:2], 0.0, [B2, self.B_oc, self.B_osb, self.B_owb, self.B_acc, self.sbias[0][1], self.sbias[1][1]])
        yT = R4[:, 0:4096].rearrange("p (k t) -> p k t", k=8)
        for m in range(8):
            ms = slice(m * 128, (m + 1) * 128)
            st, sw_ = self.wtile([(0, 4, 128, dr["w_pm"][l][:, ms].rearrange("(k p) f -> p k f", p=128)),
                                  (512, 4, 128, dr["w_pn"][l][:, ms].rearrange("(k p) f -> p k f", p=128)),
                                  (1024, 8, 128, dr["w_gm"][l][:, ms].rearrange("(k p) f -> p k f", p=128)),
                                  (2048, 8, 128, dr["w_gn"][l][:, ms].rearrange("(k p) f -> p k f", p=128))])
            if st is None:
                continue
            wpm = self.wview(st, 0, 4, 128)
            wpn = self.wview(st, 512, 4, 128)
            wgm = self.wview(st, 1024, 8, 128)
            wgn = self.wview(st, 2048, 8, 128)
            ppm, bpm = self.bank("G")
            pgm, bgm = self.bank("S")
            ppn, bpn = self.bank("G")
            pgn, bgn = self.bank("S")
            for k in range(4):
                self.mm(ppm[:], wpm[:, k, :], self.omT[:, k, :], k == 0, k == 3, [sw_, self.B_omT], bpm)
            for k in range(8):
                self.mm(pgm[:], wgm[:, k, :], R3[:, k, :], k == 0, k == 7, [sw_, B3], bgm)
            for k in range(4):
                self.mm(ppn[:], wpn[:, k, :], self.onT[:, k, :], k == 0, k == 3, [sw_, self.B_onT], bpn)
            for k in range(8):
                self.mm(pgn[:], wgn[:, k, :], R3[:, k, :], k == 0, k == 7, [sw_, B3], bgn)
            s1, bs1 = self.tf()
            s2, bs2 = self.tf()
            self.act(s1[:], pgm[:], AF.Sigmoid, [bgm], [bs1])
            self.act(s2[:], pgn[:], AF.Sigmoid, [bgn], [bs2])
            self.tt(s1[:], ppm[:], s1[:], ALU.mult, [bs1, bpm], [bs1])
            self.tt(s2[:], ppn[:], s2[:], ALU.mult, [bs2, bpn], [bs2])
            self.tt(yT[:, m, :], s1[:], s2[:], ALU.add, [bs1, bs2], [B4])
        for db in range(2):
            d0 = db * 512
            st, sw_ = self.wtile([(0, 8, 512, dr["w_out"][l][:, d0:d0 + 512].rearrange("(k p) f -> p k f", p=128))])
            if st is None:
                continue
            wo = self.wview(st, 0, 8, 512)
            for mm_ in range(4):
                m = db * 4 + mm_
                p_, b_ = self.bank("A")
                for k in range(8):
                    self.mm(p_[:], wo[:, k, mm_ * 128:(mm_ + 1) * 128], yT[:, k, :], k == 0, k == 7, [sw_, B4], b_)
                self.tt(R2[:, m, :], p_[:], R1[:, m, :], ALU.add, [b_, B1], [B2])

    def compressed(self, l, c, qnw, qaug):
        P = self.P
        R2, B2, B4 = self.R2, self.B_R2, self.B_R4
        NCCc = 32 * (c + 1)
        ntile = (NCCc + 127) // 128
        NS = self.NS
        do_sel = (NS > 16) and (c >= 2)
        oc = R2[:, 0:4, :]
        if (not do_sel) or (self.NCC // 4 < 64):
            for i in range(4):
                self.memset(qaug[64:128, i, :], 0.0, [B4])
                self.memset(qaug[0:64, 4 + i, :], 0.0, [B4])
        jobs = []
        for tt_ in range(4):
            for g in range(2):
                for hh in range(4):
                    jobs.append((tt_, g, hh))
        st1 = {}
        deferred = []

        def stage1(j):
            tt_, g, hh = jobs[j]
            qt = c * 4 + tt_
            ts_ = slice(tt_ * 128, (tt_ + 1) * 128)
            x0 = 248 - 8 * qt
            rs_ = slice(64 * g, 64 * g + 64)
            i = hh
            ps_, bs_ = self.bank("S")
            self.mm(ps_[:, 0:NCCc], qaug[rs_, i + 4 * g, ts_], self.kcT[rs_, 0:NCCc], True, True, [B4, self.B_kc], bs_)
            k_ = j % 2
            sbt, sbb = self.sbias[k_]
            pet, peb = self.pexp[k_]
            pnt, pnb_ = self.pnb[k_]
            self.stt(sbt[:, 0:NCCc], ps_[:, 0:NCCc], SC_NSA, self.stripc[:, x0:x0 + NCCc], ALU.mult, ALU.add,
                     [bs_, self.B_const], [sbb])
            self.memset(sbt[:, 0:1], NEGB, [sbb])
            sm_, smb = self.sm()
            self.act(pet[:, 0:NCCc], sbt[:, 0:NCCc], AF.Exp, [sbb], [peb])
            self.P.add("dve", (lambda o, i_: (lambda e: e.tensor_reduce(out=o, in_=i_, axis=AX.X, op=ALU.add)))(
                sm_[:, 0:1], pet[:, 0:NCCc]), reads=[peb], writes=[smb])
            self.ts(sm_[:, 1:2], sm_[:, 0:1], 1e-30, ALU.max, [smb], [smb])
            self.recip(sm_[:, 2:3], sm_[:, 1:2], [smb], [smb])
            self.ts(pnt[:, 0:NCCc], pet[:, 0:NCCc], sm_[:, 2:3], ALU.mult, [peb, smb], [pnb_])
            if do_sel:
                if hh == 0:
                    self.ts(self.p4[:, 0:NCCc], pet[:, 0:NCCc], sm_[:, 2:3], ALU.mult, [peb, smb], [self.B_p4])
                else:
                    self.stt(self.p4[:, 0:NCCc], pet[:, 0:NCCc], sm_[:, 2:3], self.p4[:, 0:NCCc], ALU.mult, ALU.add,
                             [peb, smb, self.B_p4], [self.B_p4])
                if hh == 3:
                    Bi = self.B_imp
                    self.memset(self.p4[:, NCCc:self.NCC + 4], 0.0, [self.B_p4])
                    nj = self.NCC // 4
                    self.P.add("dve", (lambda o, i_: (lambda e: e.tensor_reduce(out=o, in_=i_, axis=AX.X, op=ALU.add)))(
                        self.imp[:, 0:nj], self.p4[:, 0:4 * nj].rearrange("p (j r) -> p j r", r=4)),
                        reads=[self.B_p4], writes=[Bi])
                    self.tt(self.imp[:, 0:nj], self.imp[:, 0:nj], self.p4[:, 4:4 * nj + 4:4], ALU.add, [Bi, self.B_p4], [Bi])
                    xj = 62 - 2 * qt
                    self.tt(self.imp[:, 0:nj], self.imp[:, 0:nj], self.stripj[:, xj:xj + nj], ALU.add, [Bi, self.B_const], [Bi])
                    self.ts(self.imp[:, 0:1], self.imp[:, 0:1], 1e30, ALU.add, [Bi], [Bi])
                    self.P.add("dve", lambda e: e.max(out=self.m8[:, 0:8], in_=self.imp[:, 0:nj]), reads=[Bi], writes=[Bi])
                    self.P.add("dve", lambda e: e.match_replace(out=self.imp2[:, 0:nj], in_to_replace=self.m8[:, 0:8],
                                                                 in_values=self.imp[:, 0:nj], imm_value=-3e38),
                               reads=[Bi], writes=[Bi])
                    self.P.add("dve", lambda e: e.max(out=self.m8[:, 8:16], in_=self.imp2[:, 0:nj]), reads=[Bi], writes=[Bi])
                    self.ts(self.imp2[:, 0:nj], self.imp[:, 0:nj], self.m8[:, 15:16], ALU.is_ge, [Bi], [Bi])
                    nsb_t, nsb_b = self.nsbs[(j // 4) % 2]
                    self.ts(nsb_t[:, 0:nj], self.imp2[:, 0:nj], -NEGB, ALU.mult, [Bi], [nsb_b], s2=NEGB, op1=ALU.add)

                    def post_pe(nsb_t=nsb_t, nsb_b=nsb_b, nj=nj, g=g, ts_=ts_):
                        ptp, ptb_ = self.bank("G")
                        ptbf = ptp[:].bitcast(BF16)
                        self.tr(ptbf[0:nj, 0:128], nsb_t[:, 0:nj], self.identb, [nsb_b, self.B_const], ptb_)
                        for i2 in range(4):
                            if g == 0:
                                self.cp(qaug[64:64 + nj, i2, ts_], ptbf[0:nj, 0:128], [ptb_], [B4])
                            else:
                                self.cp(qaug[0:nj, 4 + i2, ts_], ptbf[0:nj, 0:128], [ptb_], [B4])

                    deferred.append((j + 3, post_pe))

        def stage2(j):
            k_ = j % 2
            pnt, pnb_ = self.pnb[k_]
            pTt, pTb = self.pTt[k_]
            ptp, ptb_ = self.bank("G")
            ptbf = ptp[:].bitcast(BF16)
            for ti in range(ntile):
                w_ = min(128, NCCc - ti * 128)
                self.tr(ptbf[0:w_, ti * 128:(ti + 1) * 128], pnt[:, ti * 128:ti * 128 + w_], self.identb,
                        [pnb_, self.B_const], ptb_)
            for ti in range(ntile):
                w_ = min(128, NCCc - ti * 128)
                self.cp(pTt[0:w_, ti, :], ptbf[0:w_, ti * 128:(ti + 1) * 128], [ptb_], [pTb])

        def stage3(j):
            tt_, g, hh = jobs[j]
            ts_ = slice(tt_ * 128, (tt_ + 1) * 128)
            rs_ = slice(64 * g, 64 * g + 64)
            k_ = j % 2
            pTt, pTb = self.pTt[k_]
            po_, bo_ = self.bank("A")
            for ti in range(ntile):
                w_ = min(128, NCCc - ti * 128)
                self.mm(po_[0:64, 0:128], self.vc[0:w_, ti, g, :], pTt[0:w_, ti, :], ti == 0, ti == ntile - 1,
                        [self.B_vc, pTb], bo_)
            self.cp(oc[rs_, hh, ts_], po_[0:64, 0:128], [bo_], [self.B_oc])

        n = len(jobs)
        for j in range(n + 2):
            if j < n:
                stage1(j)
            if 0 <= j - 1 < n:
                stage2(j - 1)
            if 0 <= j - 2 < n:
                stage3(j - 2)
            while deferred and deferred[0][0] <= j:
                deferred.pop(0)[1]()
        for _, f in deferred:
            f()

    def run_stream(self, jobs, skew=2):
        n = len(jobs)
        pts = [None] * n
        deferred = []
        for j in range(n + skew):
            if j < n:
                ps_, bs_ = self.bank("S")
                jobs[j]["qk"](ps_, bs_)
                pt_, ptb_ = self.ptile()
                self.act(pt_[:], ps_[:], AF.Exp, [bs_], [ptb_], scale=jobs[j]["scale"])
                pts[j] = (pt_, ptb_)
            jj = j - skew
            if jj >= 0:
                jobs[jj]["pv"](*pts[jj])
                if jobs[jj].get("fin"):
                    jobs[jj]["fin"]()
                if jobs[jj].get("fin_pe"):
                    deferred.append((j + 3, jobs[jj]["fin_pe"]))
            while deferred and deferred[0][0] <= j:
                deferred.pop(0)[1]()
        for _, f in deferred:
            f()

    def attention(self, l, c, qabs, qaug, qnw):
        R2, B2, B4 = self.R2, self.B_R2, self.B_R4
        LONG = self.LONG
        nkt = 4 * c + 4
        caches = [self.B_cache[cc] for cc in range(c + 1)]
        uvv = self.uv[:].rearrange("p (h d) -> p h d", h=8)
        selv = self.selg[:].rearrange("p (a m) -> p a m", a=12)
        banks = [b for pool in ("S", "A", "G") for b in self.pools[pool]]

        def bias_ap(d):
            s0 = LONG + (3 - d) * 128
            return self.bfc[:, s0:s0 + 512]

        acc_banks = banks[3:7]
        ai = [0]

        def next_acc():
            b = acc_banks[ai[0] % len(acc_banks)]
            ai[0] += 1
            return b

        g7 = banks[7]
        jobs = []
        for h in range(8):
            i, hh = h // 2, h % 2
            j3, r3 = h // 3, h % 3
            pO, bO = next_acc()
            pS, bS = next_acc()
            for kt in range(nkt):
                def qk(ps_, bs_, kt=kt, h=h, j3=j3, r3=r3):
                    ks = slice(kt * 128, (kt + 1) * 128)
                    diag = kt >= 4 * c
                    self.mm(ps_[:], self.ckvT[:, ks], qabs[:, h, :], True, False, caches + [B4], bs_)
                    self.mm(ps_[:], self.kpeT[32 * r3:32 * r3 + 32, ks], self.qpe[32 * r3:32 * r3 + 32, j3, :], False, not diag,
                            caches + [self.B_qpe], bs_)
                    if diag:
                        self.mm(ps_[:], self.identb, bias_ap(kt - 4 * c), False, True, [self.B_const], bs_)

                def pv(pt_, ptb_, kt=kt, pO=pO, bO=bO, pS=pS, bS=bS):
                    self.mm(pO[:], self.ckvtok[:, kt, :], pt_[:], kt == 0, kt == nkt - 1, caches + [ptb_], bO)
                    self.mm(pS[:], self.onesb, pt_[:], kt == 0, kt == nkt - 1, [self.B_const, ptb_], bS)

                job = {"qk": qk, "pv": pv, "scale": SC_MLA}
                if kt == nkt - 1:
                    hold = {}

                    def fin(pO=pO, bO=bO, pS=pS, bS=bS, hold=hold):
                        rs_, rsb = self.tf()
                        self.cp(rs_[:], pS[:], [bS], [rsb], eng="act")
                        self.recip(rs_[:], rs_[:], [rsb], [rsb])
                        ol, olb = self.tb()
                        self.tt(ol[:], pO[:], rs_[:], ALU.mult, [bO, rsb], [olb])
                        hold["ol"] = (ol, olb)

                    def fin_pe(h=h, i=i, hh=hh, hold=hold):
                        ol, olb = hold["ol"]
                        p_, b_ = g7
                        self.mm(p_[0:64, :], uvv[:, h, :], ol[:], True, True, [self.B_lp, olb], b_)
                        self.cp(self.omT[64 * hh:64 * hh + 64, i, :], p_[0:64, :], [b_], [self.B_omT])

                    job["fin"] = fin
                    job["fin_pe"] = fin_pe
                jobs.append(job)
        self.run_stream(jobs)

        acc_banks = banks[3:6]
        ai[0] = 0
        gsel = [banks[6], banks[7]]
        gi = [0]
        kt0 = max(0, 4 * c - 4)
        osb = R2[:, 4, :]
        owb = R2[:, 5, :]
        acc = R2[:, 6, :]
        jobs = []
        for i in range(4):
            for g in range(2):
                h = i + 4 * g
                osl = slice(64 * g, 64 * g + 64)
                sml = slice(64 * (1 - g), 64 * (1 - g) + 64)
                vcol = slice(0, 128) if g == 0 else slice(64, 192)
                for br in range(2):
                    pO, bO = next_acc()
                    kts = list(range(nkt)) if br == 0 else list(range(kt0, nkt))
                    for kt in kts:
                        first, last = kt == kts[0], kt == kts[-1]
                        if br == 0:
                            def qk(ps_, bs_, kt=kt, g=g, h=h):
                                ks = slice(kt * 128, (kt + 1) * 128)
                                diag = kt >= 4 * c
                                self.mm(ps_[:], self.ksaug[:, g, ks], qaug[:, h, :], True, not diag, caches + [B4, self.B_const], bs_)
                                if diag:
                                    self.mm(ps_[:], self.identb, bias_ap(kt - 4 * c), False, True, [self.B_const], bs_)

                            def pv(pt_, ptb_, kt=kt, pO=pO, bO=bO, vcol=vcol, first=first, last=last):
                                self.mm(pO[:], self.Vs[:, kt, vcol], pt_[:], first, last, caches + [ptb_, self.B_const], bO)
                        else:
                            def qk(ps_, bs_, kt=kt, osl=osl, h=h):
                                ks = slice(kt * 128, (kt + 1) * 128)
                                self.mm(ps_[:], self.kwT[osl, ks], qaug[osl, h, :], True, False, caches + [B4], bs_)
                                p_ = kt - 4 * c + 4
                                s0 = LONG + (7 - p_) * 128
                                self.mm(ps_[:], self.identb, self.bfc[:, s0:s0 + 512], False, True, [self.B_const], bs_)

                            def pv(pt_, ptb_, kt=kt, pO=pO, bO=bO, vcol=vcol, first=first, last=last):
                                self.mm(pO[:], self.Vw[:, kt, vcol], pt_[:], first, last, caches + [ptb_, self.B_const], bO)
                        job = {"qk": qk, "pv": pv, "scale": SC_NSA}
                        if last:
                            dstb = osb if br == 0 else owb
                            dstB = self.B_osb if br == 0 else self.B_owb

                            def fin(pO=pO, bO=bO, osl=osl, sml=sml, dstb=dstb, dstB=dstB, last_of_pair=(g == 1 and br == 1), i=i):
                                rs_, rsb = self.tf()
                                self.cp(rs_[sml, :], pO[sml, :], [bO], [rsb], eng="act")
                                self.recip(rs_[sml, :], rs_[sml, :], [rsb], [rsb])
                                self.tt(dstb[osl, :], pO[osl, :], rs_[sml, :], ALU.mult, [bO, rsb], [dstB])
                                if last_of_pair:
                                    for br_ in range(3):
                                        pg_, bg_ = gsel[gi[0] % 2]
                                        gi[0] += 1
                                        self.mm(pg_[:], selv[:, br_ * 4 + i, :], self.gsig[:], True, True, [self.B_const, self.B_gsig], bg_)
                                        if br_ == 0:
                                            self.tt(acc, pg_[:], R2[:, i, :], ALU.mult, [self.B_oc, bg_], [self.B_acc])
                                        elif br_ == 1:
                                            self.tt(osb, pg_[:], osb, ALU.mult, [self.B_osb, bg_], [self.B_osb])
                                            self.tt(acc, acc, osb, ALU.add, [self.B_acc, self.B_osb], [self.B_acc])
                                        else:
                                            self.tt(owb, pg_[:], owb, ALU.mult, [self.B_owb, bg_], [self.B_owb])
                                            self.tt(self.onT[:, i, :], acc, owb, ALU.add, [self.B_acc, self.B_owb], [self.B_onT])

                            job["fin"] = fin
                        jobs.append(job)
        self.run_stream(jobs)


_CACHE = {}


def kernel(**inputs):
    x = np.asarray(inputs["x"], dtype=np.float32)
    Bn, T, _ = x.shape
    w = prep_weights(inputs)
    consts = make_consts(T)
    key = (T,)
    if key not in _CACHE:
        b = Builder(T)
        nc = b.build()
        _CACHE[key] = (b, nc)
    b, nc = _CACHE[key]
    shared = {}
    shared.update(w)
    shared.update(consts)
    in_maps = []
    for i in range(Bn):
        m = dict(shared)
        m["x"] = np.ascontiguousarray(x[i])
        in_maps.append(m)
    res = run_bass_kernel_spmd(nc, in_maps, core_ids=list(range(Bn)))
    return np.stack([np.asarray(r["y"], dtype=np.float32) for r in res.results], 0)
```
